# Optimizing a Trainium2 kernel written in Bass

```python
import math
import jax, jax.numpy as jnp
from jax import lax
import numpy as np

D_MODEL = 1024
BATCH = 32
SEQ = 256
DEPTH = 2
DEC_BATCH = 8
DEC_SEQ = 4096
PAST_LEN = 512

GRID_W = 64
BRANCH_W = 512
N_BRANCH = 3
H_RET = 4
RET_DK = 64
RET_DV = 128
RET_CHUNK = 128
H_DIFF = 4
DIFF_HD = 64
H_MLA = 8
MLA_NOPE = 64
MLA_ROPE = 32
MLA_V = 64
Q_LORA = 384
KV_LORA = 256
MLA_SCALE = (MLA_NOPE + MLA_ROPE) ** -0.5
D_FF = -(-8 * D_MODEL // (3 * 256)) * 256
Q_BLOCK = 128
ROPE_BASE = 10000.0
EPS = 1e-6
IN_SIZES = (H_RET * RET_DK, H_RET * RET_DK, H_RET * RET_DV, H_RET * RET_DV,
            H_DIFF * 2 * DIFF_HD, H_DIFF * 2 * DIFF_HD, H_DIFF * 2 * DIFF_HD,
            Q_LORA, KV_LORA, MLA_ROPE, N_BRANCH * D_MODEL)
N_IN = sum(IN_SIZES)

kernel_name = 'hybrid_ret_diff_mla_dit_step'


def _rmsnorm(x, g):
    xf = x.astype(jnp.float32)
    y = xf * lax.rsqrt(jnp.mean(xf * xf, axis=-1, keepdims=True) + EPS)
    return (y * g.astype(jnp.float32)).astype(x.dtype)


def _head_rmsnorm(o):
    of = o.astype(jnp.float32)
    return of * lax.rsqrt(jnp.mean(of * of, axis=-1, keepdims=True) + EPS)


def _group_layernorm(o):
    mu = jnp.mean(o, axis=-1, keepdims=True)
    var = jnp.mean(jnp.square(o - mu), axis=-1, keepdims=True)
    return (o - mu) * lax.rsqrt(var + EPS)


def _adaln(cond, w, b):
    return jax.nn.silu(cond) @ w + b


def _axial_rope_tables(n_tokens, rot_dim):
    t = jnp.arange(n_tokens)
    row = (t // GRID_W).astype(jnp.float32)
    col = (t % GRID_W).astype(jnp.float32)
    nf = rot_dim // 4
    inv = ROPE_BASE ** (-jnp.arange(nf, dtype=jnp.float32) / nf)
    ang = jnp.concatenate([row[:, None] * inv, col[:, None] * inv], axis=-1)
    return jnp.cos(ang), jnp.sin(ang)


def _rope(x, cos, sin):
    shape = (x.shape[1],) + (1,) * (x.ndim - 3) + (cos.shape[-1],)
    c = cos.reshape(shape).astype(x.dtype)
    s = sin.reshape(shape).astype(x.dtype)
    x1, x2 = jnp.split(x, 2, axis=-1)
    return jnp.concatenate([x1 * c - x2 * s, x2 * c + x1 * s], axis=-1)


def _retention(q, k, v, decay_logit, s0, strict):
    B, S, H, _ = q.shape
    dv = v.shape[-1]
    C = RET_CHUNK
    n = S // C
    lg = jax.nn.log_sigmoid(decay_logit.astype(jnp.float32))
    i = jnp.arange(C, dtype=jnp.float32)
    dist = i[:, None] - i[None, :]
    mask = (dist > 0) if strict else (dist >= 0)
    dmat = jnp.where(mask[None], jnp.exp(jnp.maximum(dist, 0.0)[None] * lg[:, None, None]), 0.0)
    xi = jnp.exp((i + 1.0)[:, None] * lg[None, :])[None, :, :, None]
    zeta = jnp.exp((C - 1.0 - i)[:, None] * lg[None, :])[None, :, :, None]
    g_chunk = jnp.exp(C * lg)[None, :, None, None]

    def chunks(a):
        return jnp.moveaxis(a.astype(jnp.float32).reshape((B, n, C) + a.shape[2:]), 1, 0)

    def step(s, qkv):
        qc, kc, vc = qkv
        sc = jnp.einsum('bihd,bjhd->bhij', qc, kc) * dmat
        inner = jnp.einsum('bhij,bjhe->bihe', sc, vc)
        cross = jnp.einsum('bihd,bhde->bihe', qc, s) * xi
        s_new = g_chunk * s + jnp.einsum('bjhd,bjhe->bhde', kc * zeta, vc)
        return s_new, inner + cross

    s_fin, out = lax.scan(step, s0.astype(jnp.float32), (chunks(q), chunks(k), chunks(v)))
    return jnp.moveaxis(out, 0, 1).reshape(B, S, H, dv), s_fin


def _query_blocks(fn, *qs):
    B, S = qs[0].shape[:2]
    nb = S // Q_BLOCK
    blocks = tuple(jnp.moveaxis(q.reshape((B, nb, Q_BLOCK) + q.shape[2:]), 1, 0) for q in qs)
    out = lax.map(lambda a: fn(*a), blocks)
    return jnp.moveaxis(out, 0, 1).reshape((B, S) + out.shape[3:])


def _diff_attention(q, k, v, lam, lam_init):
    s = jnp.einsum('bqhcd,bkhcd->bhcqk', q, k).astype(jnp.float32) * DIFF_HD ** -0.5
    p = jax.nn.softmax(s, axis=-1)
    a = p[:, :, 0] - lam * p[:, :, 1]
    o = jnp.einsum('bhqk,bkhe->bqhe', a.astype(v.dtype), v)
    return (_head_rmsnorm(o) * (1.0 - lam_init)).astype(v.dtype)


def _mla_attention(q_nope, q_pe, k_nope, k_pe, v):
    s = jnp.einsum('bqhd,bkhd->bhqk', q_nope, k_nope) + jnp.einsum('bqhr,bkr->bhqk', q_pe, k_pe)
    p = jax.nn.softmax(s.astype(jnp.float32) * MLA_SCALE, axis=-1)
    return jnp.einsum('bhqk,bkhe->bqhe', p.astype(v.dtype), v)


def _swiglu(h, w_in, w_out):
    a, b = jnp.split(h @ w_in, 2, axis=-1)
    return (jax.nn.silu(a) * b) @ w_out


def _mixers(h, lw, lam_init, rope, ctx):
    B, S, _ = h.shape
    idx = np.cumsum(IN_SIZES)[:-1].tolist()
    rq, rk, rv, rg, dq, dk, dv, cq, ckv, kpe, gate_logits = jnp.split(h @ lw['w_in'], idx, axis=-1)

    rq = rq.reshape(B, S, H_RET, RET_DK)
    rk = rk.reshape(B, S, H_RET, RET_DK) * RET_DK ** -0.5
    rv = rv.reshape(B, S, H_RET, RET_DV)
    dq = dq.reshape(B, S, H_DIFF, 2, DIFF_HD)
    dk = dk.reshape(B, S, H_DIFF, 2, DIFF_HD)
    dv = dv.reshape(B, S, H_DIFF, 2 * DIFF_HD)
    q_mla = (_rmsnorm(cq, lw['mla_q_norm']) @ lw['w_uq']).reshape(B, S, H_MLA, MLA_NOPE + MLA_ROPE)
    q_nope, q_pe = jnp.split(q_mla, [MLA_NOPE], axis=-1)
    ckv = _rmsnorm(ckv, lw['mla_kv_norm'])

    if rope is not None:
        (c64, s64), (c32, s32) = rope
        rq, rk = _rope(rq, c64, s64), _rope(rk, c64, s64)
        dq, dk = _rope(dq, c64, s64), _rope(dk, c64, s64)
        q_pe, kpe = _rope(q_pe, c32, s32), _rope(kpe, c32, s32)

    if ctx is None:
        s_f0 = jnp.zeros((B, H_RET, RET_DK, RET_DV), jnp.float32)
        s_b0 = s_f0
        dk_all, dv_all, ckv_all, kpe_all = dk, dv, ckv, kpe
    else:
        s_f0, s_b0, ck, cv, cckv, ckpe = ctx
        L = ck.shape[1]
        dk_all = jnp.concatenate([ck.reshape(B, L, H_DIFF, 2, DIFF_HD), dk], axis=1)
        dv_all = jnp.concatenate([cv, dv], axis=1)
        ckv_all = jnp.concatenate([cckv, ckv], axis=1)
        kpe_all = jnp.concatenate([ckpe, kpe], axis=1)

    o_f, s_f = _retention(rq, rk, rv, lw['ret_decay_fwd'], s_f0, strict=False)
    o_b, s_b = _retention(jnp.flip(rq, 1), jnp.flip(rk, 1), jnp.flip(rv, 1), lw['ret_decay_bwd'], s_b0, strict=True)
    o_ret = _group_layernorm(o_f + jnp.flip(o_b, 1)).reshape(B, S, BRANCH_W).astype(h.dtype)
    y_ret = jax.nn.silu(rg) * o_ret

    lq1, lk1, lq2, lk2 = lw['diff_lambda'][0], lw['diff_lambda'][1], lw['diff_lambda'][2], lw['diff_lambda'][3]
    lam = jnp.exp(jnp.sum(lq1 * lk1)) - jnp.exp(jnp.sum(lq2 * lk2)) + lam_init
    y_diff = _query_blocks(lambda q: _diff_attention(q, dk_all, dv_all, lam, lam_init), dq).reshape(B, S, BRANCH_W)

    kv = (ckv_all @ lw['w_ukv']).reshape(B, ckv_all.shape[1], H_MLA, MLA_NOPE + MLA_V)
    k_nope, v_mla = jnp.split(kv, [MLA_NOPE], axis=-1)
    y_mla = _query_blocks(lambda qn, qp: _mla_attention(qn, qp, k_nope, kpe_all, v_mla), q_nope, q_pe).reshape(B, S, BRANCH_W)

    branches = jnp.einsum('bsgc,gcd->bsgd', jnp.stack([y_ret, y_diff, y_mla], axis=2), lw['w_branch'])
    gates = jax.nn.sigmoid(gate_logits.reshape(B, S, N_BRANCH, D_MODEL))
    out = jnp.sum(gates * branches, axis=2) @ lw['w_out']
    ctx_out = None if ctx is not None else (s_f, s_b, dk.reshape(B, S, H_DIFF, 2 * DIFF_HD), dv, ckv, kpe)
    return out, ctx_out


def _layer(x, mod, lw, lam_init, rope, ctx):
    sh1, sc1, g1, sh2, sc2, g2 = jnp.split(mod, 6, axis=-1)
    h = _rmsnorm(x, lw['norm1_g']) * (1.0 + sc1) + sh1
    mix, ctx_out = _mixers(h, lw, lam_init, rope, ctx)
    x = x + g1 * mix
    h = _rmsnorm(x, lw['norm2_g']) * (1.0 + sc2) + sh2
    x = x + g2 * _swiglu(h, lw['w_ffn_in'], lw['w_ffn_out'])
    return x, ctx_out


def setup_inputs(seed: int = 0) -> dict:
    key = jax.random.key(seed)
    ks = jax.random.split(key, 32)
    f32 = jnp.float32

    def nrm(k, shape, scale):
        return jax.random.normal(k, shape, f32) * scale

    ret_init = jnp.log(jnp.exp2(5.0 + jnp.arange(H_RET, dtype=f32)) - 1.0)
    return {
        'x_prompt': nrm(ks[0], (BATCH, SEQ, D_MODEL), 1.0),
        'x_sample': nrm(ks[1], (DEC_BATCH, DEC_SEQ, D_MODEL), 1.0),
        'state_ret_fwd': nrm(ks[2], (DEC_BATCH, DEPTH, H_RET, RET_DK, RET_DV), 0.5),
        'state_ret_bwd': nrm(ks[3], (DEC_BATCH, DEPTH, H_RET, RET_DK, RET_DV), 0.5),
        'cache_diff_k': nrm(ks[4], (DEC_BATCH, DEPTH, PAST_LEN, H_DIFF, 2 * DIFF_HD), 1.0),
        'cache_diff_v': nrm(ks[5], (DEC_BATCH, DEPTH, PAST_LEN, H_DIFF, 2 * DIFF_HD), 1.0),
        'cache_mla_ckv': nrm(ks[6], (DEC_BATCH, DEPTH, PAST_LEN, KV_LORA), 1.0),
        'cache_mla_kpe': nrm(ks[7], (DEC_BATCH, DEPTH, PAST_LEN, MLA_ROPE), 1.0),
        'c': nrm(ks[8], (DEC_BATCH, D_MODEL), 1.0),
        'c_ctx': nrm(ks[9], (D_MODEL,), 1.0),
        'norm1_g': 1.0 + nrm(ks[10], (DEPTH, D_MODEL), 0.1),
        'norm2_g': 1.0 + nrm(ks[11], (DEPTH, D_MODEL), 0.1),
        'w_ada': nrm(ks[12], (DEPTH, D_MODEL, 6 * D_MODEL), 0.5 * D_MODEL ** -0.5),
        'b_ada': nrm(ks[13], (DEPTH, 6 * D_MODEL), 0.02),
        'w_in': nrm(ks[14], (DEPTH, D_MODEL, N_IN), D_MODEL ** -0.5),
        'ret_decay_fwd': ret_init[None, :] + nrm(ks[15], (DEPTH, H_RET), 0.1),
        'ret_decay_bwd': ret_init[None, :] + nrm(ks[16], (DEPTH, H_RET), 0.1),
        'diff_lambda': nrm(ks[17], (DEPTH, 4, DIFF_HD), 0.1),
        'mla_q_norm': 1.0 + nrm(ks[18], (DEPTH, Q_LORA), 0.1),
        'mla_kv_norm': 1.0 + nrm(ks[19], (DEPTH, KV_LORA), 0.1),
        'w_uq': nrm(ks[20], (DEPTH, Q_LORA, H_MLA * (MLA_NOPE + MLA_ROPE)), Q_LORA ** -0.5),
        'w_ukv': nrm(ks[21], (DEPTH, KV_LORA, H_MLA * (MLA_NOPE + MLA_V)), KV_LORA ** -0.5),
        'w_branch': nrm(ks[22], (DEPTH, N_BRANCH, BRANCH_W, D_MODEL), BRANCH_W ** -0.5),
        'w_out': nrm(ks[23], (DEPTH, D_MODEL, D_MODEL), D_MODEL ** -0.5),
        'w_ffn_in': nrm(ks[24], (DEPTH, D_MODEL, 2 * D_FF), D_MODEL ** -0.5),
        'w_ffn_out': nrm(ks[25], (DEPTH, D_FF, D_MODEL), D_FF ** -0.5),
        'final_g': 1.0 + nrm(ks[26], (D_MODEL,), 0.1),
    }


def reference(x_prompt, x_sample, state_ret_fwd, state_ret_bwd, cache_diff_k, cache_diff_v,
              cache_mla_ckv, cache_mla_kpe, c, c_ctx, norm1_g, norm2_g, w_ada, b_ada, w_in,
              ret_decay_fwd, ret_decay_bwd, diff_lambda, mla_q_norm, mla_kv_norm, w_uq, w_ukv,
              w_branch, w_out, w_ffn_in, w_ffn_out, final_g):
    rows = x_sample.shape[1] // GRID_W
    n_lat = rows * GRID_W
    rope = (_axial_rope_tables(n_lat, RET_DK), _axial_rope_tables(n_lat, MLA_ROPE))

    xp, xs = x_prompt, x_sample
    ret_f, ret_b, dks, dvs, ckvs, kpes = [], [], [], [], [], []
    for l in range(DEPTH):
        lw = {'norm1_g': norm1_g[l], 'norm2_g': norm2_g[l], 'w_in': w_in[l],
              'ret_decay_fwd': ret_decay_fwd[l], 'ret_decay_bwd': ret_decay_bwd[l],
              'diff_lambda': diff_lambda[l], 'mla_q_norm': mla_q_norm[l], 'mla_kv_norm': mla_kv_norm[l],
              'w_uq': w_uq[l], 'w_ukv': w_ukv[l], 'w_branch': w_branch[l], 'w_out': w_out[l],
              'w_ffn_in': w_ffn_in[l], 'w_ffn_out': w_ffn_out[l]}
        lam_init = 0.8 - 0.6 * math.exp(-0.3 * l)

        mod_ctx = _adaln(c_ctx[None, :], w_ada[l], b_ada[l])[:, None, :]
        xp, ctx_t = _layer(xp, mod_ctx, lw, lam_init, None, None)
        ret_f.append(ctx_t[0]); ret_b.append(ctx_t[1]); dks.append(ctx_t[2])
        dvs.append(ctx_t[3]); ckvs.append(ctx_t[4]); kpes.append(ctx_t[5])

        mod_lat = _adaln(c, w_ada[l], b_ada[l])[:, None, :]
        cache_l = (state_ret_fwd[:, l], state_ret_bwd[:, l], cache_diff_k[:, l], cache_diff_v[:, l],
                   cache_mla_ckv[:, l], cache_mla_kpe[:, l])
        xs, _ = _layer(xs, mod_lat, lw, lam_init, rope, cache_l)

    y_prompt = _rmsnorm(xp, final_g)
    y_sample = _rmsnorm(xs, final_g)
    new_ret_fwd = jnp.stack(ret_f, axis=1)
    new_ret_bwd = jnp.stack(ret_b, axis=1)
    new_diff_k = jnp.stack(dks, axis=1)
    new_diff_v = jnp.stack(dvs, axis=1)
    new_mla_ckv = jnp.stack(ckvs, axis=1)
    new_mla_kpe = jnp.stack(kpes, axis=1)
    return (y_prompt, y_sample, new_ret_fwd, new_ret_bwd, new_diff_k, new_diff_v, new_mla_ckv, new_mla_kpe)
```

```python
import contextlib
import math
import numpy as np
import concourse.bass as bass
import concourse.mybir as mybir
from concourse.bass_utils import run_bass_kernel_spmd

F32 = mybir.dt.float32
BF16 = mybir.dt.bfloat16
AF = mybir.ActivationFunctionType
ALU = mybir.AluOpType

D = 1024
DEPTH = 2
NS = 4096
NPSEQ = 4
PS_ = 256
NT = NS + NPSEQ * PS_
LCTX = 512
NKEY = LCTX + NT
DFF = 2816
EPS = 1e-6
MLA_SCALE = 96 ** -0.5
GRID_W = 64

O_RQ, O_RK, O_RV, O_RG, O_DQ, O_DK, O_DV, O_CQ, O_CKV, O_KPE, O_GATE = 0, 256, 512, 1024, 1536, 2048, 2560, 3072, 3456, 3712, 3744
A_DQ, A_DQS, A_DK, A_DKS, A_RQ2, A_RQ2S, A_RK, A_RKS, A_CQ, A_CKV, A_KPE, A_KPES, A_DV, A_RV = (
    0, 512, 1024, 1536, 2048, 2560, 3072, 3328, 3584, 3968, 4224, 4256, 4288, 4800)
NA = 5312
NC1 = 512 + 3072


class Buf:
    __slots__ = ("name", "w", "r", "toks")

    def __init__(self, name=""):
        self.name = name
        self.w = None
        self.r = {}


class Eng:
    def __init__(self, fw, name, eng, sem, self_sync=True):
        self.fw = fw
        self.name = name
        self.e = eng
        self.sem = sem
        self.count = 0
        self.known = {}
        self.self_sync = self_sync

    def wait_tok(self, tok):
        if tok is None:
            return
        sem, val = tok
        if sem is self.sem and not self.self_sync:
            return
        k = id(sem)
        if self.known.get(k, 0) >= val:
            return
        self.e.wait_ge(sem, val)
        self.known[k] = val

    def deps(self, reads, writes):
        for b in reads:
            self.wait_tok(b.w)
        for b in writes:
            self.wait_tok(b.w)
            for tok in list(b.r.values()):
                self.wait_tok(tok)

    def mark(self, tok, reads, writes):
        sem, val = tok
        for b in reads:
            old = b.r.get(id(sem))
            if old is None or old[1] < val:
                b.r[id(sem)] = (sem, val)
        for b in writes:
            b.w = tok
            b.r = {}

    def op(self, fns, reads=(), writes=()):
        self.deps(reads, writes)
        if callable(fns):
            fns = [fns]
        ins = None
        for f in fns:
            ins = f()
        self.count += 1
        ins.then_inc(self.sem, 1)
        tok = (self.sem, self.count)
        self.mark(tok, reads, writes)
        self.fw.ninst += len(fns)
        return tok


class DmaQ:
    def __init__(self, fw, name, eng, sems):
        self.fw = fw
        self.eng = eng
        self.sems = sems
        self.tot = [0] * len(sems)
        self.i = 0

    def dma(self, out, in_, reads=(), writes=()):
        q = self.eng
        q.deps(reads, writes)
        i = self.i
        self.i = (self.i + 1) % len(self.sems)
        sem = self.sems[i]
        if self.tot[i] > 0:
            q.wait_tok((sem, self.tot[i]))
        self.tot[i] += 16
        q.e.dma_start(out=out, in_=in_).then_inc(sem, 16)
        tok = (sem, self.tot[i])
        q.mark(tok, reads, writes)
        self.fw.ndma += 1
        return tok


class FW:
    def __init__(self, nc, es):
        self.nc = nc
        self.ninst = 0
        self.ndma = 0
        mk = lambda n: es.enter_context(nc.semaphore(n))
        self.pe = Eng(self, "pe", nc.tensor, mk("s_pe"), self_sync=False)
        self.act = Eng(self, "act", nc.scalar, mk("s_act"))
        self.dve = Eng(self, "dve", nc.vector, mk("s_dve"))
        self.pool = Eng(self, "pool", nc.gpsimd, mk("s_pool"))
        self.sp = Eng(self, "sp", nc.sync, mk("s_sp"))
        self.engs = [self.pe, self.act, self.dve, self.pool, self.sp]
        self.q_ld = DmaQ(self, "q_ld", self.sp, [mk(f"s_ld{i}") for i in range(12)])
        self.q_st = DmaQ(self, "q_st", self.pool, [mk(f"s_st{i}") for i in range(12)])
        self.qs = [self.q_ld, self.q_st]

    def barrier(self):
        toks = [(e.sem, e.count) for e in self.engs if e.count > 0]
        for q in self.qs:
            for s, t in zip(q.sems, q.tot):
                if t > 0:
                    toks.append((s, t))
        for e in self.engs:
            for tok in toks:
                if tok[0] is e.sem:
                    continue
                e.wait_tok(tok)


class _StopBuild(Exception):
    pass


def _ckpt(name):
    import os
    if os.environ.get("ASTOP", "") == name:
        raise _StopBuild(name)


def build_program(stop_after=None):
    nc = bass.Bass("TRN2", target_bir_lowering=False)

    def din(name, shape, dt=F32):
        return nc.dram_tensor(name, list(shape), dt, kind="ExternalInput").ap()

    def dout(name, shape, dt=F32):
        return nc.dram_tensor(name, list(shape), dt, kind="ExternalOutput").ap()

    def dscr(name, shape, dt=BF16):
        return nc.dram_tensor(name, list(shape), dt, kind="Internal").ap()

    x_in = din("x_in", [NT, D])
    st_f = din("st_f", [DEPTH, 4, 64, 128])
    st_b = din("st_b", [DEPTH, 4, 64, 128])
    c_dk = din("c_dk", [DEPTH, LCTX, 512])
    c_dv = din("c_dv", [DEPTH, LCTX, 512])
    c_ckv = din("c_ckv", [DEPTH, LCTX, 256])
    c_kpe = din("c_kpe", [DEPTH, LCTX, 32])
    cond_fm = din("cond_fm", [128, 8, 2])
    n1g = din("n1g", [DEPTH, 128, 8])
    n2g = din("n2g", [DEPTH, 128, 8])
    w_ada = din("w_ada", [DEPTH, D, 6 * D])
    b_ada_fm = din("b_ada_fm", [DEPTH, 128, 48])
    b_ada = din("b_ada", [DEPTH, 6 * D])
    w_a = din("w_a", [DEPTH, D, NA])
    w_c1 = din("w_c1", [DEPTH, D, NC1])
    dec_f = din("dec_f", [DEPTH, 4])
    dec_b = din("dec_b", [DEPTH, 4])
    dlam = din("dlam", [DEPTH, 256])
    qng = din("qng", [DEPTH, 128, 3])
    kvng = din("kvng", [DEPTH, 128, 2])
    kvn_row = din("kvn_row", [DEPTH, 256])
    w_uq = din("w_uq", [DEPTH, 384, 1536])
    w_ukv = din("w_ukv", [DEPTH, 256, 1024])
    w_br = din("w_br", [DEPTH, 1536, D])
    w_o = din("w_o", [DEPTH, D, D])
    w_f1 = din("w_f1", [DEPTH, D, 2 * DFF])
    w_f2 = din("w_f2", [DEPTH, DFF, D])
    fin_g = din("fin_g", [D])
    ident_in = din("ident_in", [128, 128])
    rope_t = din("rope_t", [8, 128, NS])
    ret_c = din("ret_c", [5, 128, 128])
    ret_z = din("ret_z", [128, 2])
    sel65 = din("sel65", [65, 64])

    y_out = dout("y_out", [NT, D])
    o_rf = dout("o_rf", [NPSEQ, DEPTH, 4, 64, 128])
    o_rb = dout("o_rb", [NPSEQ, DEPTH, 4, 64, 128])
    o_dk = dout("o_dk", [NPSEQ, DEPTH, PS_, 512])
    o_dv = dout("o_dv", [NPSEQ, DEPTH, PS_, 512])
    o_ckv = dout("o_ckv", [NPSEQ, DEPTH, PS_, 256])
    o_kpe = dout("o_kpe", [NPSEQ, DEPTH, PS_, 32])

    s_qd = dscr("s_qd", [4, 128, NT])
    s_kd = dscr("s_kd", [4, 128, NKEY])
    s_vd = dscr("s_vd", [NKEY, 512])
    s_rq = dscr("s_rq", [4, 128, NT])
    s_rk = dscr("s_rk", [4, 64, NT])
    s_rv = dscr("s_rv", [NT, 512])
    s_qm = dscr("s_qm", [8, 128, NT])
    s_km = dscr("s_km", [8, 128, NKEY])
    s_kpe = dscr("s_kpe", [32, NKEY])
    s_vm = dscr("s_vm", [NKEY, 8, 128])
    s_oret = dscr("s_oret", [4, 128, NT])
    s_yd = dscr("s_yd", [4, 128, NT])
    s_ym = dscr("s_ym", [512, NT])
    s_xmid = dscr("s_xmid", [NT, D], F32)
    s_x1 = dscr("s_x1", [NT, D], F32)
    s_gb = dscr("s_gb", [2, 2, 128, D], F32)

    SEQS = [(0, NS, 0, LCTX + NS, True)] + [
        (NS + p * PS_, PS_, LCTX + NS + p * PS_, PS_, False) for p in range(NPSEQ)]

    BLOCKS = [(0, NS, 0, LCTX + NS, 1), (NS, PS_, LCTX + NS, PS_, NPSEQ)]

    with contextlib.ExitStack() as ges:
        fw = FW(nc, ges)
        pe, act, dve, pool, q_ld, q_st = fw.pe, fw.act, fw.dve, fw.pool, fw.q_ld, fw.q_st
        V, S, T, G_ = nc.vector, nc.scalar, nc.tensor, nc.gpsimd

        _uid = [0]

        def sbt(es, name, shape, dt):
            _uid[0] += 1
            return es.enter_context(nc.sbuf_tensor(f"{name}_{_uid[0]}", list(shape), dt))

        PSB = [ges.enter_context(nc.psum_tensor(f"psb{i}", [128, 512], F32)) for i in range(8)]
        PSBUF = [Buf(f"psb{i}") for i in range(8)]

        class Ring:
            def __init__(self, items):
                self.items = items
                self.i = 0

            def next(self):
                it = self.items[self.i]
                self.i = (self.i + 1) % len(self.items)
                return it

        B = {}
        for nm in ["wa", "wc1", "wuq", "wukv", "wbr", "wo", "wf1", "wf2"]:
            for l in range(DEPTH):
                B[(nm, l)] = Buf(nm)
        for nm in ["qd", "kd", "vd", "rq", "rk", "rv", "qm", "km", "kpe", "vm", "oret", "yd", "ym", "xmid", "x1", "gb",
                   "out"]:
            B[nm] = Buf(nm)

        ident_f = sbt(ges, "ident_f", [128, 128], F32)
        ident_b = sbt(ges, "ident_b", [128, 128], BF16)
        ones_b = sbt(ges, "ones_b", [128, 128], BF16)
        ones_f = sbt(ges, "ones_f", [128, 128], F32)
        mods = sbt(ges, "mods", [128, 2, 4, 8], F32)
        b_mods = Buf("mods")
        b_const = Buf("const")
        q_ld.dma(ident_f[:], ident_in, writes=[b_const])
        dve.op(lambda: V.tensor_copy(out=ident_b[:], in_=ident_f[:]), reads=[b_const], writes=[b_const])
        dve.op(lambda: V.memset(ones_b[:], 1.0), writes=[b_const])
        dve.op(lambda: V.memset(ones_f[:], 1.0), writes=[b_const])
        cb_t = sbt(ges, "cb_t", [128, 4], F32)
        for i_, v_ in enumerate([EPS, 1.0, 128.0 * EPS, 0.0]):
            dve.op(lambda i_=i_, v_=v_: V.memset(cb_t[:, i_:i_ + 1], v_), writes=[b_const])
        CB_EPS, CB_ONE, CB_128EPS, CB_ZERO = 0, 1, 2, 3

        def act_pow(out_ap, in_ap, p, mul, cbi, reads, writes, p0=0, p1=128):
            act.op(lambda: S.activation(out=out_ap, in_=in_ap, func=AF.Ln, scale=mul, bias=cb_t[p0:p1, cbi:cbi + 1]),
                   reads=list(reads) + [b_const], writes=writes)
            act.op(lambda: S.activation(out=out_ap, in_=out_ap, func=AF.Exp, scale=p), reads=writes, writes=writes)

        with contextlib.ExitStack() as ies:
            zt = sbt(ies, "zt", [32, NKEY], BF16)
            vt = sbt(ies, "vt", [128, 8, 64], BF16)
            bzt = Buf("zt")
            dve.op(lambda: V.memset(zt[:], 0.0), writes=[bzt])
            dve.op(lambda: V.memset(vt[:], 0.0), writes=[bzt])
            dve.op(lambda: V.memset(vt[:, :, 0:1], 1.0), writes=[bzt])
            for h_ in range(8):
                q_ld.dma(s_km[h_, 96:128, :], zt[:, 0:NKEY], reads=[bzt], writes=[B["km"]])
                q_ld.dma(s_qm[h_, 96:128, :], zt[:, 0:NT], reads=[bzt], writes=[B["qm"]])
            fw.barrier()

        def ada_phase(l):
            with contextlib.ExitStack() as es:
                cond = sbt(es, "cond", [128, 8, 2], F32)
                scond = sbt(es, "scond", [128, 8, 2], F32)
                scb = sbt(es, "scb", [128, 2, 8, 128], F32)
                wblk = [sbt(es, f"wblk{i}", [128, 8, 512], F32) for i in range(3)]
                b_wblk = [Buf("wblk0"), Buf("wblk1"), Buf("wblk2")]
                bfm = sbt(es, "bfm", [128, 48], F32)
                brow = sbt(es, "brow", [1, 6 * D], F32)
                g1t = sbt(es, "g1t", [128, 8], F32)
                g2t = sbt(es, "g2t", [128, 8], F32)
                modT = sbt(es, "modT", [128, 48, 2], F32)
                gbt = [sbt(es, f"gbt{i}", [128, 512], F32) for i in range(2)]
                b_gbt = [Buf("gbt0"), Buf("gbt1")]
                b_c = Buf("cond"); b_sc = Buf("scond"); b_scb = Buf("scb"); b_misc = Buf("misc"); b_modT = Buf("modT")
                q_ld.dma(cond[:], cond_fm, writes=[b_c])
                q_ld.dma(bfm[:], b_ada_fm[l], writes=[b_misc])
                q_ld.dma(brow[:], b_ada[l:l + 1, :], writes=[b_misc])
                q_ld.dma(g1t[:], n1g[l], writes=[b_misc])
                q_ld.dma(g2t[:], n2g[l], writes=[b_misc])
                act.op(lambda: S.activation(out=scond[:], in_=cond[:], func=AF.Silu), reads=[b_c], writes=[b_sc])
                for c in range(2):
                    for kc in range(8):
                        dve.op(lambda c=c, kc=kc: V.tensor_scalar(out=scb[:, c, kc, :], in0=ones_f[:],
                                                                  scalar1=scond[:, kc, c:c + 1], scalar2=None,
                                                                  op0=ALU.mult),
                               reads=[b_sc, b_const], writes=[b_scb])
                pm = PSB[7]
                bpm = PSBUF[7]
                pmv = pm[:, 0:96].rearrange("p (j c) -> p j c", c=2)
                gring = Ring([0, 1])
                for cb in range(12):
                    wi = cb % 3
                    wt = wblk[wi]
                    for kc in range(8):
                        q_ld.dma(wt[:, kc, :], w_ada[l, kc * 128:(kc + 1) * 128, cb * 512:(cb + 1) * 512],
                                 writes=[b_wblk[wi]])
                    if cb in (4, 5, 10, 11):
                        gi = 0 if cb < 6 else 1
                        half = cb % 2
                        for c in range(2):
                            pb = gring.next()
                            ps_, bps = PSB[pb], PSBUF[pb]
                            fns = [(lambda kc=kc, c=c, ps_=ps_, wt=wt: T.matmul(ps_[:], lhsT=scb[:, c, kc, :],
                                                                            rhs=wt[:, kc, :], start=(kc == 0),
                                                                            stop=False)) for kc in range(8)]
                            fns.append(lambda ps_=ps_, cb=cb: T.matmul(ps_[:], lhsT=ones_f[0:1, :],
                                                                      rhs=brow[0:1, cb * 512:(cb + 1) * 512],
                                                                      start=False, stop=True))
                            pe.op(fns, reads=[b_scb, b_wblk[wi], b_misc, b_const], writes=[bps])
                            gt = gbt[pb]
                            act.op(lambda gt=gt, ps_=ps_: S.copy(out=gt[:], in_=ps_[:]), reads=[bps], writes=[b_gbt[pb]])
                            q_st.dma(s_gb[gi, c, :, half * 512:(half + 1) * 512], gt[:], reads=[b_gbt[pb]],
                                     writes=[B["gb"]])
                    for jj in range(4):
                        j = cb * 4 + jj
                        fns = [(lambda kc=kc, j=j, jj=jj, wt=wt: T.matmul(pmv[:, j, :],
                                                                       lhsT=wt[:, kc, jj * 128:(jj + 1) * 128],
                                                                       rhs=scond[:, kc, :], start=(kc == 0),
                                                                       stop=(kc == 7))) for kc in range(8)]
                        pe.op(fns, reads=[b_sc, b_wblk[wi]], writes=[bpm])
                for c in range(2):
                    dve.op(lambda c=c: V.tensor_tensor(out=modT[:, :, c], in0=pmv[:, :, c], in1=bfm[:], op=ALU.add),
                           reads=[bpm, b_misc], writes=[b_modT])
                for c in range(2):
                    dve.op(lambda c=c: V.scalar_tensor_tensor(out=mods[:, c, 0, :], in0=modT[:, 8:16, c], scalar=1.0,
                                                              in1=g1t[:], op0=ALU.add, op1=ALU.mult),
                           reads=[b_modT, b_misc], writes=[b_mods])
                    dve.op(lambda c=c: V.tensor_copy(out=mods[:, c, 1, :], in_=modT[:, 0:8, c]),
                           reads=[b_modT], writes=[b_mods])
                    dve.op(lambda c=c: V.scalar_tensor_tensor(out=mods[:, c, 2, :], in0=modT[:, 32:40, c], scalar=1.0,
                                                              in1=g2t[:], op0=ALU.add, op1=ALU.mult),
                           reads=[b_modT, b_misc], writes=[b_mods])
                    dve.op(lambda c=c: V.tensor_copy(out=mods[:, c, 3, :], in_=modT[:, 24:32, c]),
                           reads=[b_modT], writes=[b_mods])
                fw.barrier()

        def make_norm_ctx(es, nx=4, nh=1):
            ctx = {}
            ctx["xt"] = [sbt(es, f"xt{i}", [128, D], F32) for i in range(nx)]
            ctx["bx"] = [Buf(f"xt{i}") for i in range(nx)]
            ctx["xn"] = [sbt(es, f"xn{i}", [128, D], BF16) for i in range(4)]
            ctx["bxn"] = [Buf(f"xn{i}") for i in range(4)]
            ctx["junk"] = sbt(es, "junk", [128, D], BF16)
            ctx["bjunk"] = Buf("junk")
            ctx["ss"] = sbt(es, "ss", [128, 8], F32)
            ctx["bss"] = [Buf(f"ss{i}") for i in range(4)]
            ctx["hTs"] = [sbt(es, f"hT{i}", [128, 8, 512], BF16) for i in range(nh)]
            ctx["bhTs"] = [Buf(f"hT{i}") for i in range(nh)]
            ctx["hT"] = ctx["hTs"][0]
            ctx["bhT"] = ctx["bhTs"][0]
            return ctx

        def norm_to_hT(ctx, xsrc, bsrc, g0, cidx, which, psbanks, slot=0, xo=0):
            xn, bxn, ss, bss = (ctx[k] for k in ["xn", "bxn", "ss", "bss"])
            xt, bx = ctx["xt"][xo:xo + 4], ctx["bx"][xo:xo + 4]
            hT, bhT = ctx["hTs"][slot], ctx["bhTs"][slot]
            junk, bjunk = ctx["junk"], ctx["bjunk"]
            for t in range(4):
                q_ld.dma(xt[t][:], xsrc[g0 + t * 128:g0 + (t + 1) * 128, :], reads=[bsrc], writes=[bx[t]])
                act.op(lambda t=t: S.activation(out=junk[:], in_=xt[t][:], func=AF.Square, accum_out=ss[:, t:t + 1]),
                       reads=[bx[t]], writes=[bjunk, bss[t]])
                act_pow(ss[:, 4 + t:5 + t], ss[:, t:t + 1], -0.5, 1.0 / D, CB_EPS, [bss[t]], [bss[t]])
                dve.op(lambda t=t: V.tensor_scalar(out=xn[t][:], in0=xt[t][:], scalar1=ss[:, 4 + t:5 + t], scalar2=None,
                                                   op0=ALU.mult),
                       reads=[bx[t], bss[t]], writes=[bxn[t]])
            ai, bi = (0, 1) if which == 0 else (2, 3)
            for j in range(8):
                pb = psbanks.next()
                pst = PSB[pb][:].bitcast(BF16)
                fns = [(lambda t=t, j=j, pst=pst: T.transpose(pst[:, t * 128:(t + 1) * 128],
                                                              xn[t][:, j * 128:(j + 1) * 128], ident_b[:]))
                       for t in range(4)]
                pe.op(fns, reads=bxn + [b_const], writes=[PSBUF[pb]])
                c = cidx
                if j % 2 == 0:
                    act.op(lambda j=j, pst=pst, c=c: S.activation(out=hT[:, j, :], in_=pst[:, 0:512], func=AF.Identity,
                                                                  scale=mods[:, c, ai, j:j + 1],
                                                                  bias=mods[:, c, bi, j:j + 1]),
                           reads=[PSBUF[pb], b_mods], writes=[bhT])
                else:
                    dve.op(lambda j=j, pst=pst, c=c: V.tensor_scalar(out=hT[:, j, :], in0=pst[:, 0:512],
                                                                     scalar1=mods[:, c, ai, j:j + 1],
                                                                     scalar2=mods[:, c, bi, j:j + 1], op0=ALU.mult,
                                                                     op1=ALU.add),
                           reads=[PSBUF[pb], b_mods], writes=[bhT])

        cast_rr = [0]

        def load_w(es, name, src, kchunks, ncols, key, l, krows=128):
            wt = sbt(es, name, [krows, kchunks, ncols], BF16)
            bw = Buf(name)
            CW = 2048
            with contextlib.ExitStack() as ses:
                stg = [sbt(ses, f"stg{i}", [128, CW], F32) for i in range(3)]
                bstg = [Buf(f"stg{i}") for i in range(3)]
                k = 0
                for kc in range(kchunks):
                    for c0 in range(0, ncols, CW):
                        cw = min(CW, ncols - c0)
                        i = k % 3
                        k += 1
                        q_ld.dma(stg[i][0:krows, 0:cw], src[kc * krows:(kc + 1) * krows, c0:c0 + cw], writes=[bstg[i]])
                        e = (0, 1, 0, 1, 2)[cast_rr[0] % 5]
                        cast_rr[0] += 1
                        if e == 0:
                            act.op(lambda i=i, kc=kc, c0=c0, cw=cw: S.copy(out=wt[:, kc, c0:c0 + cw], in_=stg[i][0:krows, 0:cw]),
                                   reads=[bstg[i]], writes=[bw])
                        elif e == 1:
                            dve.op(lambda i=i, kc=kc, c0=c0, cw=cw: V.tensor_copy(out=wt[:, kc, c0:c0 + cw], in_=stg[i][0:krows, 0:cw]),
                                   reads=[bstg[i]], writes=[bw])
                        else:
                            pool.op(lambda i=i, kc=kc, c0=c0, cw=cw: G_.tensor_copy(out=wt[:, kc, c0:c0 + cw], in_=stg[i][0:krows, 0:cw]),
                                    reads=[bstg[i]], writes=[bw])
                fw.barrier()
            return wt, bw

        def phase_a(l, xsrc, bsrc):
            with contextlib.ExitStack() as es:
                WA, bWA = load_w(es, "WA", w_a[l], 8, NA, "wa", l)
                WUQ, bWUQ = load_w(es, "WUQ", w_uq[l], 3, 1536, "wuq", l)
                WUKV, bWUKV = load_w(es, "WUKV", w_ukv[l], 2, 1024, "wukv", l)
                ctx = make_norm_ctx(es, nh=2)
                hT, bhT = ctx["hT"], ctx["bhT"]
                normed = [-1]
                rt = sbt(es, "rt", [128, 8, 512], F32)
                brt = Buf("rt")
                qg = sbt(es, "qg", [128, 3], F32)
                kvg = sbt(es, "kvg", [128, 2], F32)
                kvrow = sbt(es, "kvrow", [128, 256], F32)
                bsm = Buf("small")
                q_ld.dma(qg[:], qng[l], writes=[bsm])
                q_ld.dma(kvg[:], kvng[l], writes=[bsm])
                q_ld.dma(kvrow[:], kvn_row[l].partition_broadcast(128), writes=[bsm])
                NTMP = 4
                tmpf = [sbt(es, f"tmpf{i}", [128, 512], F32) for i in range(NTMP)]
                btmpf = [Buf(f"tmpf{i}") for i in range(NTMP)]
                tring = Ring(list(range(NTMP)))
                NOB = 6
                ob = [sbt(es, f"ob{i}", [128, 512], BF16) for i in range(NOB)]
                bob = [Buf(f"ob{i}") for i in range(NOB)]
                oring = Ring(list(range(NOB)))
                obv = [sbt(es, f"obv{i}", [128, 8, 128], BF16) for i in range(2)]
                bobv = [Buf(f"obv{i}") for i in range(2)]
                obvring = Ring([0, 1])
                for i_ in range(2):
                    dve.op(lambda i_=i_: V.memset(obv[i_][:], 0.0), writes=[bobv[i_]])
                    dve.op(lambda i_=i_: V.memset(obv[i_][:, :, 64:65], 1.0), writes=[bobv[i_]])
                of = [sbt(es, f"of{i}", [128, 512], F32) for i in range(2)]
                bof = [Buf(f"of{i}") for i in range(2)]
                ofring = Ring([0, 1])
                cqg = sbt(es, "cqg", [128, 3, 512], BF16)
                bcqg = Buf("cqg")
                sq = sbt(es, "sq", [128, 3, 512], BF16)
                bsq = Buf("sq")
                rstd_q = sbt(es, "rstd_q", [128, 512], F32)
                b_rq_ = Buf("rstd_q")
                rstd_k = sbt(es, "rstd_k", [128, 512], F32)
                b_rk_ = Buf("rstd_k")
                ckvn = sbt(es, "ckvn", [128, 2, 512], BF16)
                bckvn = Buf("ckvn")
                cqf = sbt(es, "cqf", [128, 3, 512], F32)
                bcqf = Buf("cqf")
                sst = sbt(es, "sst", [128, 4], F32)
                bsst = Buf("sst")
                psr = Ring([0, 1, 2, 3, 4, 5])
                pstr = Ring([6, 7])

                def proj_fm(pb, col0, M, G=512):
                    ps_ = PSB[pb]
                    fns = [(lambda kc=kc: T.matmul(ps_[0:M, 0:G], lhsT=WA[:, kc, col0:col0 + M], rhs=hT[:, kc, 0:G],
                                                   start=(kc == 0), stop=(kc == 7))) for kc in range(8)]
                    pe.op(fns, reads=[bWA, bhT], writes=[PSBUF[pb]])

                def store(dst, src_ap, bsrc_, bdst):
                    q_st.dma(dst, src_ap, reads=[bsrc_], writes=[bdst])

                rope_rr = [0]

                def evac_rope(pbx, pbs, M, ci, si, dst, bdst, rstd=None, dsts=None):
                    i1, i2 = tring.next(), tring.next()
                    t1, t2 = tmpf[i1], tmpf[i2]
                    dve.op(lambda: V.tensor_tensor(out=t1[0:M, :], in0=PSB[pbx][0:M, :], in1=rt[0:M, ci, :],
                                                   op=ALU.mult), reads=[PSBUF[pbx], brt], writes=[btmpf[i1]])
                    dve.op(lambda: V.tensor_tensor(out=t2[0:M, :], in0=PSB[pbs][0:M, :], in1=rt[0:M, si, :],
                                                   op=ALU.mult), reads=[PSBUF[pbs], brt], writes=[btmpf[i2]])
                    oi = oring.next()
                    o = ob[oi]
                    if rstd is None:
                        rope_rr[0] += 1
                        if rope_rr[0] % 3 != 1:
                            dve.op(lambda: V.tensor_tensor(out=o[0:M, :], in0=t1[0:M, :], in1=t2[0:M, :], op=ALU.add),
                                   reads=[btmpf[i1], btmpf[i2]], writes=[bob[oi]])
                        else:
                            pool.op(lambda: G_.tensor_tensor(out=o[0:M, :], in0=t1[0:M, :], in1=t2[0:M, :], op=ALU.add),
                                    reads=[btmpf[i1], btmpf[i2]], writes=[bob[oi]])
                    else:
                        dve.op(lambda: V.tensor_tensor(out=t1[0:M, :], in0=t1[0:M, :], in1=t2[0:M, :], op=ALU.add),
                               reads=[btmpf[i1], btmpf[i2]], writes=[btmpf[i1]])
                        dve.op(lambda: V.tensor_tensor(out=o[0:M, :], in0=t1[0:M, :], in1=rstd[0:M, :], op=ALU.mult),
                               reads=[btmpf[i1], b_rq_], writes=[bob[oi]])
                    if dsts is not None:
                        for d_ in dsts:
                            store(d_, o[0:M, :], bob[oi], bdst)
                    else:
                        store(dst, o[0:M, :], bob[oi], bdst)

                def evac_plain(pbx, M, dst, bdst, scale=1.0, use_act=True, G=512, v3=False, dsts=None):
                    oi = oring.next()
                    o = ob[oi]
                    if use_act:
                        act.op(lambda: S.activation(out=o[0:M, 0:G], in_=PSB[pbx][0:M, 0:G], func=AF.Copy, scale=scale),
                               reads=[PSBUF[pbx]], writes=[bob[oi]])
                    else:
                        dve.op(lambda: V.tensor_scalar(out=o[0:M, 0:G], in0=PSB[pbx][0:M, 0:G], scalar1=scale,
                                                       scalar2=None, op0=ALU.mult),
                               reads=[PSBUF[pbx]], writes=[bob[oi]])
                    if v3:
                        store(dst, o[0:M, 0:G].rearrange("p (h e) -> p h e", e=64), bob[oi], bdst)
                    elif dsts is not None:
                        for d_ in dsts:
                            store(d_, o[0:M, 0:G], bob[oi], bdst)
                    else:
                        store(dst, o[0:M, 0:G], bob[oi], bdst)

                def fm_chunk(col, cols, M, dst, bdst, rope, ci=0, si=1, scale=1.0):
                    pbx = psr.next()
                    proj_fm(pbx, col, M)
                    if rope:
                        pbs = psr.next()
                        proj_fm(pbs, cols, M)
                        evac_rope(pbx, pbs, M, ci, si, dst, bdst)
                    else:
                        evac_plain(pbx, M, dst, bdst, scale=scale)

                def rms_fm(src_t, bsrc_t, nchunks, nfeat, rstd_t, brstd):
                    for i in range(nchunks):
                        act.op(lambda i=i: S.activation(out=sq[:, i, :], in_=src_t[:, i, :], func=AF.Square),
                               reads=[bsrc_t], writes=[bsq])
                    _ckpt("sq")
                    pb = psr.next()
                    fns = [(lambda i=i: T.matmul(PSB[pb][:], lhsT=ones_b[:], rhs=sq[:, i, :], start=(i == 0),
                                                 stop=(i == nchunks - 1))) for i in range(nchunks)]
                    pe.op(fns, reads=[bsq, b_const], writes=[PSBUF[pb]])
                    _ckpt("onesmm")
                    act_pow(rstd_t[:], PSB[pb][:], -0.5, 1.0 / nfeat, CB_EPS, [PSBUF[pb]], [brstd])

                def mla_from_ckvn(k0):
                    for h in range(8):
                        pb = psr.next()
                        fns = [(lambda kc=kc, h=h, pb=pb: T.matmul(PSB[pb][0:64, :], lhsT=WUKV[:, kc, h * 64:(h + 1) * 64],
                                                                   rhs=ckvn[:, kc, :], start=(kc == 0), stop=(kc == 1)))
                               for kc in range(2)]
                        pe.op(fns, reads=[bWUKV, bckvn], writes=[PSBUF[pb]])
                        evac_plain(pb, 64, s_km[h, 0:64, k0:k0 + 512], B["km"], use_act=(h % 2 == 0))
                    for t in range(4):
                        pb = psr.next()
                        fns = [(lambda kc=kc, t=t, pb=pb: T.matmul(PSB[pb][:], lhsT=ckvn[:, kc, t * 128:(t + 1) * 128],
                                                                   rhs=WUKV[:, kc, 512:1024], start=(kc == 0),
                                                                   stop=(kc == 1))) for kc in range(2)]
                        pe.op(fns, reads=[bWUKV, bckvn], writes=[PSBUF[pb]])
                        vi = obvring.next()
                        if t % 2 == 1:
                            act.op(lambda vi=vi, pb=pb: S.copy(out=obv[vi][:, :, 0:64],
                                                               in_=PSB[pb][:].rearrange("p (h e) -> p h e", e=64)),
                                   reads=[PSBUF[pb]], writes=[bobv[vi]])
                        else:
                            dve.op(lambda vi=vi, pb=pb: V.tensor_copy(out=obv[vi][:, :, 0:64],
                                                                      in_=PSB[pb][:].rearrange("p (h e) -> p h e", e=64)),
                                   reads=[PSBUF[pb]], writes=[bobv[vi]])
                        store(s_vm[k0 + t * 128:k0 + (t + 1) * 128, :, :], obv[vi][:], bobv[vi], B["vm"])

                def cache_group():
                    ck = [sbt(es, f"ck{i}", [128, 512], F32) for i in range(2)]
                    bck = [Buf("ck0"), Buf("ck1")]
                    ckring = Ring([0, 1])
                    for t in range(4):
                        i = ckring.next()
                        q_ld.dma(ck[i][:], c_dv[l, t * 128:(t + 1) * 128, :], writes=[bck[i]])
                        oi = oring.next()
                        act.op(lambda oi=oi, i=i: S.copy(out=ob[oi][:], in_=ck[i][:]), reads=[bck[i]], writes=[bob[oi]])
                        store(s_vd[t * 128:(t + 1) * 128, :], ob[oi][:], bob[oi], B["vd"])
                    for t in range(4):
                        i = ckring.next()
                        q_ld.dma(ck[i][:], c_dk[l, t * 128:(t + 1) * 128, :], writes=[bck[i]])
                        pb = psr.next()
                        fns = [(lambda h=h, i=i, pb=pb: T.transpose(PSB[pb][:, h * 128:(h + 1) * 128],
                                                                    ck[i][:, h * 128:(h + 1) * 128], ident_f[:]))
                               for h in range(4)]
                        pe.op(fns, reads=[bck[i], b_const], writes=[PSBUF[pb]])
                        oi = oring.next()
                        act.op(lambda oi=oi, pb=pb: S.copy(out=ob[oi][:], in_=PSB[pb][:]), reads=[PSBUF[pb]],
                               writes=[bob[oi]])
                        for h in range(4):
                            store(s_kd[h, :, t * 128:(t + 1) * 128], ob[oi][:, h * 128:(h + 1) * 128], bob[oi], B["kd"])
                    pbk = psr.next()
                    for t in range(4):
                        i = ckring.next()
                        q_ld.dma(ck[i][:, 0:256], c_ckv[l, t * 128:(t + 1) * 128, :], writes=[bck[i]])
                        q_ld.dma(ck[i][:, 256:288], c_kpe[l, t * 128:(t + 1) * 128, :], writes=[bck[i]])
                        pb = psr.next()
                        fns = [(lambda kc=kc, i=i, pb=pb: T.transpose(PSB[pb][:, kc * 128:(kc + 1) * 128],
                                                                      ck[i][:, kc * 128:(kc + 1) * 128], ident_f[:]))
                               for kc in range(2)]
                        pe.op(fns, reads=[bck[i], b_const], writes=[PSBUF[pb]])
                        for kc in range(2):
                            dve.op(lambda kc=kc, t=t, pb=pb: V.tensor_copy(out=ckvn[:, kc, t * 128:(t + 1) * 128],
                                                                           in_=PSB[pb][:, kc * 128:(kc + 1) * 128]),
                                   reads=[PSBUF[pb]], writes=[bckvn])
                        pe.op(lambda t=t, i=i: T.transpose(PSB[pbk][0:32, t * 128:(t + 1) * 128], ck[i][:, 256:288],
                                                           ident_f[:]), reads=[bck[i], b_const], writes=[PSBUF[pbk]])
                    evac_plain(pbk, 32, s_kpe[:, 0:LCTX], B["kpe"])
                    mla_from_ckvn(0)

                def token_group(g0):
                    is_sample = g0 < NS
                    cidx = 0 if is_sample else 1
                    k0 = LCTX + g0
                    rope = is_sample
                    nonlocal hT, bhT
                    slot = (g0 // 512) % 2
                    if normed[0] != g0:
                        norm_to_hT(ctx, xsrc, bsrc, g0, cidx, 0, pstr, slot=slot)
                    hT, bhT = ctx["hTs"][slot], ctx["bhTs"][slot]
                    _ckpt("norm")
                    if rope:
                        for i in range(8):
                            q_ld.dma(rt[:, i, :], rope_t[i, :, g0:g0 + 512], writes=[brt])
                    for h in range(4):
                        fm_chunk(A_DQ + h * 128, A_DQS + h * 128, 128, s_qd[h, :, g0:g0 + 512], B["qd"], rope, 0, 1)
                    for h in range(4):
                        fm_chunk(A_DK + h * 128, A_DKS + h * 128, 128, s_kd[h, :, k0:k0 + 512], B["kd"], rope, 0, 1)
                    _ckpt("dqk")
                    if g0 + 512 < NT:
                        norm_to_hT(ctx, xsrc, bsrc, g0 + 512, 0 if g0 + 512 < NS else 1, 0, pstr, slot=1 - slot)
                        normed[0] = g0 + 512
                    for h in range(4):
                        fm_chunk(A_RQ2 + h * 128, A_RQ2S + h * 128, 128, s_rq[h, :, g0:g0 + 512], B["rq"], rope, 0, 1)
                    for h in range(4):
                        fm_chunk(A_RK + h * 64, A_RKS + h * 64, 64, s_rk[h, :, g0:g0 + 512], B["rk"], rope, 2, 3,
                                 scale=0.125)
                    _ckpt("rqk")
                    if rope:
                        pbx, pbs = psr.next(), psr.next()
                        proj_fm(pbx, A_KPE, 32)
                        proj_fm(pbs, A_KPES, 32)
                        evac_rope(pbx, pbs, 32, 6, 7, s_kpe[:, k0:k0 + 512], B["kpe"])
                    else:
                        pbx = psr.next()
                        proj_fm(pbx, A_KPE, 32)
                        evac_plain(pbx, 32, s_kpe[:, k0:k0 + 512], B["kpe"])
                    _ckpt("kpe")
                    for i in range(3):
                        pb = psr.next()
                        proj_fm(pb, A_CQ + i * 128, 128)
                        act.op(lambda i=i, pb=pb: S.copy(out=cqf[:, i, :], in_=PSB[pb][:]), reads=[PSBUF[pb]], writes=[bcqf])
                        dve.op(lambda i=i: V.tensor_scalar(out=cqg[:, i, :], in0=cqf[:, i, :], scalar1=qg[:, i:i + 1],
                                                           scalar2=None, op0=ALU.mult),
                               reads=[bcqf, bsm], writes=[bcqg])
                    _ckpt("cq")
                    rms_fm(cqf, bcqf, 3, 384, rstd_q, b_rq_)
                    _ckpt("rms")
                    for h in range(8):
                        if h == 1:
                            _ckpt("uq1e")
                        pbx = psr.next()
                        fns = [(lambda kc=kc, h=h, pbx=pbx: T.matmul(PSB[pbx][0:96, :], lhsT=WUQ[:, kc, h * 96:(h + 1) * 96],
                                                                     rhs=cqg[:, kc, :], start=(kc == 0), stop=(kc == 2)))
                               for kc in range(3)]
                        pe.op(fns, reads=[bWUQ, bcqg], writes=[PSBUF[pbx]])
                        _ckpt("uq1")
                        if rope:
                            pbs = psr.next()
                            fns = [(lambda kc=kc, h=h, pbs=pbs: T.matmul(PSB[pbs][0:96, :],
                                                                         lhsT=WUQ[:, kc, 768 + h * 96:768 + (h + 1) * 96],
                                                                         rhs=cqg[:, kc, :], start=(kc == 0),
                                                                         stop=(kc == 2))) for kc in range(3)]
                            pe.op(fns, reads=[bWUQ, bcqg], writes=[PSBUF[pbs]])
                            evac_rope(pbx, pbs, 96, 4, 5, s_qm[h, 0:96, g0:g0 + 512], B["qm"], rstd=rstd_q)
                        else:
                            oi = oring.next()
                            dve.op(lambda oi=oi, pbx=pbx: V.tensor_tensor(out=ob[oi][0:96, :], in0=PSB[pbx][0:96, :],
                                                                          in1=rstd_q[0:96, :], op=ALU.mult),
                                   reads=[PSBUF[pbx], b_rq_], writes=[bob[oi]])
                            store(s_qm[h, 0:96, g0:g0 + 512], ob[oi][0:96, :], bob[oi], B["qm"])
                    _ckpt("qmla")
                    for i in range(2):
                        pb = psr.next()
                        proj_fm(pb, A_CKV + i * 128, 128)
                        act.op(lambda i=i, pb=pb: S.copy(out=cqf[:, i, :], in_=PSB[pb][:]), reads=[PSBUF[pb]], writes=[bcqf])
                    rms_fm(cqf, bcqf, 2, 256, rstd_k, b_rk_)
                    for i in range(2):
                        dve.op(lambda i=i: V.scalar_tensor_tensor(out=ckvn[:, i, :], in0=cqf[:, i, :], scalar=kvg[:, i:i + 1],
                                                                  in1=rstd_k[:], op0=ALU.mult, op1=ALU.mult),
                               reads=[bcqf, b_rk_, bsm], writes=[bckvn])
                    mla_from_ckvn(k0)
                    _ckpt("ckv")
                    for t in range(4):
                        tk = g0 + t * 128
                        for (col, dst, bdst, is_dv) in [(A_DV, s_vd[LCTX + tk:LCTX + tk + 128, :], B["vd"], True),
                                                        (A_RV, s_rv[tk:tk + 128, :], B["rv"], False)]:
                            pb = psr.next()
                            fns = [(lambda kc=kc, pb=pb, col=col, t=t: T.matmul(PSB[pb][:], lhsT=hT[:, kc, t * 128:(t + 1) * 128],
                                                                                rhs=WA[:, kc, col:col + 512],
                                                                                start=(kc == 0), stop=(kc == 7)))
                                   for kc in range(8)]
                            pe.op(fns, reads=[bWA, bhT], writes=[PSBUF[pb]])
                            evac_plain(pb, 128, dst, bdst, use_act=is_dv)
                            if is_dv and not is_sample:
                                p = (tk - NS) // PS_
                                s0 = (tk - NS) % PS_
                                fi = ofring.next()
                                dve.op(lambda fi=fi, pb=pb: V.tensor_copy(out=of[fi][:], in_=PSB[pb][:]),
                                       reads=[PSBUF[pb]], writes=[bof[fi]])
                                store(o_dv[p, l, s0:s0 + 128, :], of[fi][:], bof[fi], B["out"])
                        if not is_sample:
                            p = (tk - NS) // PS_
                            s0 = (tk - NS) % PS_
                            pb = psr.next()
                            fns = [(lambda kc=kc, pb=pb, t=t: T.matmul(PSB[pb][:], lhsT=hT[:, kc, t * 128:(t + 1) * 128],
                                                                       rhs=WA[:, kc, A_DK:A_DK + 512], start=(kc == 0),
                                                                       stop=(kc == 7))) for kc in range(8)]
                            pe.op(fns, reads=[bWA, bhT], writes=[PSBUF[pb]])
                            fi = ofring.next()
                            act.op(lambda fi=fi, pb=pb: S.copy(out=of[fi][:], in_=PSB[pb][:]), reads=[PSBUF[pb]],
                                   writes=[bof[fi]])
                            store(o_dk[p, l, s0:s0 + 128, :], of[fi][:], bof[fi], B["out"])
                            pb = psr.next()
                            fns = [(lambda kc=kc, pb=pb, t=t: T.matmul(PSB[pb][:, 0:288], lhsT=hT[:, kc, t * 128:(t + 1) * 128],
                                                                       rhs=WA[:, kc, A_CKV:A_CKV + 288], start=(kc == 0),
                                                                       stop=(kc == 7))) for kc in range(8)]
                            pe.op(fns, reads=[bWA, bhT], writes=[PSBUF[pb]])
                            fi = ofring.next()
                            act.op(lambda fi=fi, pb=pb: S.copy(out=of[fi][:, 0:288], in_=PSB[pb][:, 0:288]),
                                   reads=[PSBUF[pb]], writes=[bof[fi]])
                            act.op(lambda fi=fi: S.activation(out=ctx["junk"][:, 0:256], in_=of[fi][:, 0:256],
                                                              func=AF.Square, accum_out=sst[:, 2:3]),
                                   reads=[bof[fi]], writes=[ctx["bjunk"], bsst])
                            act_pow(sst[:, 3:4], sst[:, 2:3], -0.5, 1.0 / 256, CB_EPS, [bsst], [bsst])
                            dve.op(lambda fi=fi: V.scalar_tensor_tensor(out=of[fi][:, 0:256], in0=of[fi][:, 0:256],
                                                                        scalar=sst[:, 3:4], in1=kvrow[:],
                                                                        op0=ALU.mult, op1=ALU.mult),
                                   reads=[bsst, bsm], writes=[bof[fi]])
                            store(o_ckv[p, l, s0:s0 + 128, :], of[fi][:, 0:256], bof[fi], B["out"])
                            store(o_kpe[p, l, s0:s0 + 128, :], of[fi][:, 256:288], bof[fi], B["out"])

                try:
                    _ckpt("load")
                    cache_group()
                    _ckpt("cache")
                    for g in range(NT // 512):
                        token_group(g * 512)
                        _ckpt(f"g{g + 1}")
                except _StopBuild:
                    pass
                fw.barrier()

        def phase_r(l):
            with contextlib.ExitStack() as es:
                rc = sbt(es, "rc", [128, 5, 128], F32)
                rz = sbt(es, "rz", [128, 2], F32)
                dl = sbt(es, "dl", [128, 8], F32)
                lg = sbt(es, "lg", [128, 8], F32)
                lgx = sbt(es, "lgx", [128, 4], F32)
                Dm = sbt(es, "Dm", [128, 4, 128], F32)
                e2 = sbt(es, "e2", [128, 128], F32)
                XI = sbt(es, "XI", [128, 4, 128], F32)
                ZF = sbt(es, "ZF", [128, 4, 2], F32)
                GC = sbt(es, "GC", [128, 4], F32)
                bc = Buf("rconst")
                q_ld.dma(rc[:], ret_c.rearrange("a p f -> p a f"), writes=[bc])
                q_ld.dma(rz[:], ret_z, writes=[bc])
                q_ld.dma(dl[:, 0:4], dec_f[l].partition_broadcast(128), writes=[bc])
                q_ld.dma(dl[:, 4:8], dec_b[l].partition_broadcast(128), writes=[bc])
                act.op(lambda: S.activation(out=lg[:], in_=dl[:], func=AF.Exp, scale=-1.0), reads=[bc], writes=[bc])
                act.op(lambda: S.activation(out=lg[:], in_=lg[:], func=AF.Ln, bias=cb_t[:, 1:2]), reads=[bc], writes=[bc])
                dve.op(lambda: V.tensor_scalar(out=lg[:], in0=lg[:], scalar1=-1.0, scalar2=None, op0=ALU.mult),
                       reads=[bc], writes=[bc])
                dve.op(lambda: V.tensor_copy(out=lgx[0:64, :], in_=lg[0:64, 0:4]), reads=[bc], writes=[bc])
                dve.op(lambda: V.tensor_copy(out=lgx[64:128, :], in_=lg[64:128, 4:8]), reads=[bc], writes=[bc])
                for h in range(4):
                    act.op(lambda h=h: S.activation(out=Dm[:, h, :], in_=rc[:, 0, :], func=AF.Exp, scale=lg[:, h:h + 1]),
                           reads=[bc], writes=[bc])
                    dve.op(lambda h=h: V.tensor_tensor(out=Dm[:, h, :], in0=Dm[:, h, :], in1=rc[:, 2, :], op=ALU.mult),
                           reads=[bc], writes=[bc])
                    act.op(lambda h=h: S.activation(out=e2[:], in_=rc[:, 1, :], func=AF.Exp, scale=lg[:, 4 + h:5 + h]),
                           reads=[bc], writes=[bc])
                    dve.op(lambda h=h: V.tensor_tensor(out=e2[:], in0=e2[:], in1=rc[:, 3, :], op=ALU.mult),
                           reads=[bc], writes=[bc])
                    dve.op(lambda h=h: V.tensor_tensor(out=Dm[:, h, :], in0=Dm[:, h, :], in1=e2[:], op=ALU.add),
                           reads=[bc], writes=[bc])
                    act.op(lambda h=h: S.activation(out=XI[:, h, :], in_=rc[:, 4, :], func=AF.Exp, scale=lgx[:, h:h + 1]),
                           reads=[bc], writes=[bc])
                    act.op(lambda h=h: S.activation(out=ZF[:, h, 0:1], in_=rz[:, 0:1], func=AF.Exp, scale=lg[:, h:h + 1]),
                           reads=[bc], writes=[bc])
                    act.op(lambda h=h: S.activation(out=ZF[:, h, 1:2], in_=rz[:, 1:2], func=AF.Exp,
                                                    scale=lg[:, 4 + h:5 + h]), reads=[bc], writes=[bc])
                act.op(lambda: S.activation(out=GC[:], in_=lgx[:], func=AF.Exp, scale=128.0), reads=[bc], writes=[bc])

                for (tok0, S_, key0_, NK_, nseq) in BLOCKS:
                    is_sample = (nseq == 1)
                    n = S_ // 128
                    nall = n * nseq
                    with contextlib.ExitStack() as es2:
                        RQ = sbt(es2, "RQ", [128, 4, S_ * nseq], BF16)
                        RK = sbt(es2, "RK", [64, 4, S_ * nseq], BF16)
                        RV = sbt(es2, "RV", [128, nall, 512], BF16)
                        KZ = sbt(es2, "KZ", [128, nall, 4, 128], BF16)
                        SALL = sbt(es2, "SALL", [128, nall, 4, 128], BF16)
                        ST = sbt(es2, "ST", [128, 4, 128], F32)
                        bin_ = Buf("rin"); bKZ = Buf("KZ"); bSALLf = Buf("SALLf"); bSALLb = Buf("SALLb")
                        bSTf = Buf("STf"); bSTb = Buf("STb")
                        for h in range(4):
                            q_ld.dma(RQ[:, h, :], s_rq[h, :, tok0:tok0 + S_ * nseq], reads=[B["rq"]], writes=[bin_])
                            q_ld.dma(RK[:, h, :], s_rk[h, :, tok0:tok0 + S_ * nseq], reads=[B["rk"]], writes=[bin_])
                        for i in range(0, nall, 8):
                            m = min(8, nall - i)
                            q_ld.dma(RV[:, i:i + m, :],
                                     s_rv[tok0 + i * 128:tok0 + (i + m) * 128, :].rearrange("(n p) c -> p n c", p=128),
                                     reads=[B["rv"]], writes=[bin_])
                        psr = Ring([0, 1, 2, 3])
                        for i in range(nall):
                            pb = psr.next()
                            pst = PSB[pb][:].bitcast(BF16)
                            fns = [(lambda h=h, i=i, pst=pst: T.transpose(pst[:, h * 64:(h + 1) * 64],
                                                                          RK[:, h, i * 128:(i + 1) * 128],
                                                                          ident_b[0:64, 0:64])) for h in range(4)]
                            pe.op(fns, reads=[bin_, b_const], writes=[PSBUF[pb]])
                            for h in range(4):
                                for fb in range(2):
                                    dve.op(lambda h=h, fb=fb, i=i, pst=pst: V.tensor_scalar(
                                        out=KZ[:, i, h, fb * 64:(fb + 1) * 64], in0=pst[:, h * 64:(h + 1) * 64],
                                        scalar1=ZF[:, h, fb:fb + 1], scalar2=None, op0=ALU.mult),
                                           reads=[PSBUF[pb], bc], writes=[bKZ])

                        def u_chunk(i):
                            pb = psr.next()
                            fns = [(lambda h=h, i=i, pb=pb: T.matmul(PSB[pb][:, h * 128:(h + 1) * 128], lhsT=KZ[:, i, h, :],
                                                                     rhs=RV[:, i, h * 128:(h + 1) * 128], start=True,
                                                                     stop=True)) for h in range(4)]
                            pe.op(fns, reads=[bKZ, bin_], writes=[PSBUF[pb]])
                            return pb

                        def scan_step(i, lo, hi, bST, bSALLx):
                            pb = u_chunk(i)
                            act.op(lambda: S.copy(out=SALL[lo:hi, i, :, :], in_=ST[lo:hi, :, :]), reads=[bST], writes=[bSALLx])
                            for h in range(4):
                                dve.op(lambda h=h: V.scalar_tensor_tensor(
                                    out=ST[lo:hi, h, :], in0=ST[lo:hi, h, :], scalar=GC[lo:hi, h:h + 1],
                                    in1=PSB[pb][lo:hi, h * 128:(h + 1) * 128], op0=ALU.mult, op1=ALU.add),
                                       reads=[PSBUF[pb], bc, bST], writes=[bST])

                        for si in range(nseq):
                            co = si * n
                            if is_sample:
                                q_ld.dma(ST[0:64, :, :], st_f[l].rearrange("h d e -> d h e"), writes=[bSTf])
                                q_ld.dma(ST[64:128, :, :], st_b[l].rearrange("h d e -> d h e"), writes=[bSTb])
                            else:
                                dve.op(lambda: V.memset(ST[0:64, :, :], 0.0), writes=[bSTf])
                                dve.op(lambda: V.memset(ST[64:128, :, :], 0.0), writes=[bSTb])
                            for i in range(n):
                                scan_step(co + i, 0, 64, bSTf, bSALLf)
                                scan_step(co + n - 1 - i, 64, 128, bSTb, bSALLb)
                            if not is_sample:
                                p = (tok0 - NS) // PS_ + si
                                q_st.dma(o_rf[p, l].rearrange("h d e -> d h e"), ST[0:64, :, :], reads=[bSTf], writes=[B["out"]])
                                q_st.dma(o_rb[p, l].rearrange("h d e -> d h e"), ST[64:128, :, :], reads=[bSTb], writes=[B["out"]])
                        Mt = [sbt(es2, f"Mt{i}", [128, 4, 128], BF16) for i in range(2)]
                        bMt = [Buf("Mt0"), Buf("Mt1")]
                        QX = [sbt(es2, f"QX{i}", [128, 4, 128], BF16) for i in range(2)]
                        bQX = [Buf("QX0"), Buf("QX1")]
                        stt = [sbt(es2, f"stt{i}", [128, 4, 6], F32) for i in range(2)]
                        mv = [sbt(es2, f"mv{i}", [128, 4, 2], F32) for i in range(2)]
                        bstt = [Buf("stt0"), Buf("stt1")]
                        orn = [sbt(es2, f"orn{i}", [128, 512], BF16) for i in range(2)]
                        born = [Buf("orn0"), Buf("orn1")]
                        orT = [sbt(es2, f"orT{i}", [128, 512], BF16) for i in range(2)]
                        borT = [Buf("orT0"), Buf("orT1")]
                        ps2 = Ring([4, 5])
                        ps3 = Ring([6, 7])
                        pbo_of = {}

                        def stage_a(i):
                            k = i % 2
                            pbs = psr.next()
                            fns = [(lambda h=h: T.matmul(PSB[pbs][:, h * 128:(h + 1) * 128],
                                                         lhsT=RK[0:64, h, i * 128:(i + 1) * 128],
                                                         rhs=RQ[0:64, h, i * 128:(i + 1) * 128], start=True,
                                                         stop=True)) for h in range(4)]
                            pe.op(fns, reads=[bin_], writes=[PSBUF[pbs]])
                            dve.op(lambda: V.tensor_tensor(out=Mt[k][:].rearrange("p h f -> p (h f)"), in0=PSB[pbs][:],
                                                           in1=Dm[:].rearrange("p h f -> p (h f)"), op=ALU.mult),
                                   reads=[PSBUF[pbs], bc], writes=[bMt[k]])
                            pool.op(lambda: G_.tensor_tensor(out=QX[k][:], in0=RQ[:, :, i * 128:(i + 1) * 128],
                                                             in1=XI[:], op=ALU.mult),
                                    reads=[bin_, bc], writes=[bQX[k]])

                        def stage_b(i):
                            k = i % 2
                            pbo = ps2.next()
                            pbo_of[i] = pbo
                            fns = []
                            for h in range(4):
                                fns.append(lambda h=h: T.matmul(PSB[pbo][:, h * 128:(h + 1) * 128], lhsT=Mt[k][:, h, :],
                                                                rhs=RV[:, i, h * 128:(h + 1) * 128], start=True, stop=False))
                                fns.append(lambda h=h: T.matmul(PSB[pbo][:, h * 128:(h + 1) * 128], lhsT=QX[k][:, h, :],
                                                                rhs=SALL[:, i, h, :], start=False, stop=True))
                            pe.op(fns, reads=[bMt[k], bQX[k], bin_, bSALLf, bSALLb], writes=[PSBUF[pbo]])
                            for h in range(4):
                                dve.op(lambda h=h: V.bn_stats(out=stt[k][:, h, :], in_=PSB[pbo][:, h * 128:(h + 1) * 128]),
                                       reads=[PSBUF[pbo]], writes=[bstt[k]])
                            for h in range(4):
                                dve.op(lambda h=h: V.bn_aggr(out=mv[k][:, h, :], in_=stt[k][:, h, :]),
                                       reads=[bstt[k]], writes=[bstt[k]])
                            act_pow(mv[k][:, :, 1], mv[k][:, :, 1], -0.5, 1.0, CB_EPS, [bstt[k]], [bstt[k]])
                            for h in range(4):
                                dve.op(lambda h=h: V.tensor_scalar(out=orn[k][:, h * 128:(h + 1) * 128],
                                                                   in0=PSB[pbo][:, h * 128:(h + 1) * 128],
                                                                   scalar1=mv[k][:, h, 0:1], scalar2=mv[k][:, h, 1:2],
                                                                   op0=ALU.subtract, op1=ALU.mult),
                                       reads=[PSBUF[pbo], bstt[k]], writes=[born[k]])

                        def stage_c(i):
                            k = i % 2
                            pbt = ps3.next()
                            pst = PSB[pbt][:].bitcast(BF16)
                            fns = [(lambda h=h: T.transpose(pst[:, h * 128:(h + 1) * 128], orn[k][:, h * 128:(h + 1) * 128],
                                                            ident_b[:])) for h in range(4)]
                            pe.op(fns, reads=[born[k], b_const], writes=[PSBUF[pbt]])
                            act.op(lambda: S.copy(out=orT[k][:], in_=pst[:, 0:512]), reads=[PSBUF[pbt]], writes=[borT[k]])
                            for h in range(4):
                                q_st.dma(s_oret[h, :, tok0 + i * 128:tok0 + (i + 1) * 128],
                                         orT[k][:, h * 128:(h + 1) * 128], reads=[borT[k]], writes=[B["oret"]])

                        stage_a(0)
                        for i in range(nall):
                            if i + 1 < nall:
                                stage_a(i + 1)
                            stage_b(i)
                            if i >= 1:
                                stage_c(i - 1)
                        stage_c(nall - 1)
                        fw.barrier()

        def phase_d(l):
            lam_init = 0.8 - 0.6 * math.exp(-0.3 * l)
            with contextlib.ExitStack() as es:
                dlt = sbt(es, "dlt", [128, 256], F32)
                lam = sbt(es, "lam", [128, 8], F32)
                bl = Buf("lam")
                q_ld.dma(dlt[:], dlam[l].partition_broadcast(128), writes=[bl])
                dve.op(lambda: V.tensor_tensor(out=dlt[:, 0:64], in0=dlt[:, 0:64], in1=dlt[:, 64:128], op=ALU.mult),
                       reads=[bl], writes=[bl])
                dve.op(lambda: V.tensor_tensor(out=dlt[:, 128:192], in0=dlt[:, 128:192], in1=dlt[:, 192:256], op=ALU.mult),
                       reads=[bl], writes=[bl])
                dve.op(lambda: V.tensor_reduce(out=lam[:, 0:1], in_=dlt[:, 0:64], axis=mybir.AxisListType.X, op=ALU.add), reads=[bl],
                       writes=[bl])
                dve.op(lambda: V.tensor_reduce(out=lam[:, 1:2], in_=dlt[:, 128:192], axis=mybir.AxisListType.X, op=ALU.add), reads=[bl],
                       writes=[bl])
                act.op(lambda: S.activation(out=lam[:, 2:4], in_=lam[:, 0:2], func=AF.Exp), reads=[bl], writes=[bl])
                dve.op(lambda: V.tensor_tensor(out=lam[:, 4:5], in0=lam[:, 3:4], in1=lam[:, 2:3], op=ALU.subtract),
                       reads=[bl], writes=[bl])
                dve.op(lambda: V.tensor_scalar(out=lam[:, 5:6], in0=lam[:, 4:5], scalar1=-lam_init, scalar2=None,
                                               op0=ALU.add), reads=[bl], writes=[bl])
                for (tok0, S_, key0, NK, nseq) in BLOCKS:
                    G = min(S_, 512)
                    nkt = NK // 128
                    with contextlib.ExitStack() as es2:
                        KD = sbt(es2, "KD", [128, 4, NK * nseq], BF16)
                        VD = sbt(es2, "VD", [128, nkt * nseq, 512], BF16)
                        bkv = Buf("kv")
                        for h in range(4):
                            q_ld.dma(KD[:, h, :], s_kd[h, :, key0:key0 + NK * nseq], reads=[B["kd"]], writes=[bkv])
                        for i in range(0, nkt * nseq, 6):
                            m = min(6, nkt * nseq - i)
                            q_ld.dma(VD[:, i:i + m, :],
                                     s_vd[key0 + i * 128:key0 + (i + m) * 128, :].rearrange("(n p) c -> p n c", p=128),
                                     reads=[B["vd"]], writes=[bkv])
                        QD = [sbt(es2, f"QD{i}", [128, 4, 2, G], BF16) for i in range(2)]
                        bQD = [Buf("QD0"), Buf("QD1")]
                        for i_ in range(2):
                            pool.op(lambda i_=i_: G_.memset(QD[i_][:], 0.0), writes=[bQD[i_]])
                        PT = [sbt(es2, f"PT{i}", [128, G], BF16) for i in range(6)]
                        bPT = [Buf(f"PT{i}") for i in range(6)]
                        ptr = Ring([0, 1, 2, 3, 4, 5])
                        rr = [sbt(es2, f"rr{i}", [128, G], F32) for i in range(2)]
                        za = [sbt(es2, f"za{i}", [128, G], F32) for i in range(2)]
                        bza = [Buf(f"za{i}") for i in range(2)]
                        oc = [[sbt(es2, f"oc{p_}{c_}", [128, G], F32) for c_ in range(2)] for p_ in range(2)]
                        zc = [sbt(es2, f"zc{p_}", [128, G], F32) for p_ in range(2)]
                        boc = [Buf("oc0"), Buf("oc1")]
                        tt = [sbt(es2, f"tt{i}", [128, G], F32) for i in range(2)]
                        osq = sbt(es2, "osq", [128, G], F32)
                        oo = sbt(es2, "oo", [128, G], F32)
                        yb = [sbt(es2, f"yb{i}", [128, G], BF16) for i in range(2)]
                        byb = [Buf("yb0"), Buf("yb1")]
                        bfin = Buf("fin")
                        psc = Ring([0, 1, 2, 3])
                        for qb_ in range(nseq * (S_ // G)):
                            si, qb = divmod(qb_, S_ // G)
                            q0 = tok0 + si * S_ + qb * G
                            kto = si * nkt
                            qi = qb_ % 2
                            for h in range(4):
                                q_ld.dma(QD[qi][0:64, h, 0, :], s_qd[h, 0:64, q0:q0 + G], reads=[B["qd"]], writes=[bQD[qi]])
                                q_ld.dma(QD[qi][64:128, h, 1, :], s_qd[h, 64:128, q0:q0 + G], reads=[B["qd"]], writes=[bQD[qi]])
                            tiles = [(h, kt, c) for h in range(4) for kt in range(nkt) for c in range(2)]
                            LA = 3
                            pend = {}
                            deferred = []

                            def issue_score(idx, qi=qi):
                                h, kt, c = tiles[idx]
                                pb = psc.next()
                                pe.op(lambda: T.matmul(PSB[pb][:, 0:G], lhsT=KD[:, h, (kto + kt) * 128:(kto + kt + 1) * 128],
                                                       rhs=QD[qi][:, h, c, :], start=True, stop=True),
                                      reads=[bkv, bQD[qi]], writes=[PSBUF[pb]])
                                pi = ptr.next()
                                act.op(lambda: S.activation(out=PT[pi][:], in_=PSB[pb][:, 0:G], func=AF.Exp, scale=0.125),
                                       reads=[PSBUF[pb]], writes=[bPT[pi]])
                                pend[idx] = pi

                            def issue_pv(idx):
                                h, kt, c = tiles[idx]
                                pi = pend.pop(idx)
                                if c == 0:
                                    pe.op([lambda: T.matmul(PSB[4][:, 0:G], lhsT=VD[:, kto + kt, h * 128:(h + 1) * 128], rhs=PT[pi][:],
                                                            start=(kt == 0), stop=(kt == nkt - 1)),
                                           lambda: T.matmul(PSB[6][:, 0:G], lhsT=ones_b[:], rhs=PT[pi][:], start=(kt == 0),
                                                            stop=(kt == nkt - 1))],
                                          reads=[bkv, bPT[pi], b_const], writes=[PSBUF[4], PSBUF[6]])
                                else:
                                    pe.op(lambda: T.matmul(PSB[5][:, 0:G], lhsT=VD[:, kto + kt, h * 128:(h + 1) * 128], rhs=PT[pi][:],
                                                           start=(kt == 0), stop=(kt == nkt - 1)),
                                          reads=[bkv, bPT[pi]], writes=[PSBUF[5]])
                                    zi = h % 2
                                    if kt == 0:
                                        dve.op(lambda: V.tensor_copy(out=za[zi][:], in_=PT[pi][:]), reads=[bPT[pi]], writes=[bza[zi]])
                                    else:
                                        dve.op(lambda: V.tensor_tensor(out=za[zi][:], in0=za[zi][:], in1=PT[pi][:], op=ALU.add),
                                               reads=[bPT[pi], bza[zi]], writes=[bza[zi]])

                            def fin_stage0(h):
                                p_ = h % 2
                                dve.op(lambda: V.tensor_copy(out=oc[p_][0][:], in_=PSB[4][:, 0:G]), reads=[PSBUF[4]], writes=[boc[p_]])
                                act.op(lambda: S.copy(out=oc[p_][1][:], in_=PSB[5][:, 0:G]), reads=[PSBUF[5]], writes=[boc[p_]])
                                dve.op(lambda: V.tensor_copy(out=zc[p_][:], in_=PSB[6][:, 0:G]), reads=[PSBUF[6]], writes=[boc[p_]])

                            def fin_stage1(h, q0=q0):
                                p_ = h % 2
                                pe.op(lambda: T.matmul(PSB[7][:, 0:G], lhsT=ones_f[:], rhs=za[p_][:], start=True, stop=True),
                                      reads=[bza[p_], b_const], writes=[PSBUF[7]])
                                act_pow(rr[1][:], PSB[7][:, 0:G], -1.0, 1.0, CB_ZERO, [PSBUF[7]], [bfin])
                                act_pow(rr[0][:], zc[p_][:], -1.0, 1.0, CB_ZERO, [boc[p_]], [bfin])
                                for c in range(2):
                                    dve.op(lambda c=c: V.tensor_tensor(out=tt[c][:], in0=oc[p_][c][:], in1=rr[c][:], op=ALU.mult),
                                           reads=[boc[p_], bfin], writes=[bfin])
                                dve.op(lambda: V.scalar_tensor_tensor(out=oo[:], in0=tt[1][:], scalar=lam[:, 5:6],
                                                                      in1=tt[0][:], op0=ALU.mult, op1=ALU.add),
                                       reads=[bfin, bl], writes=[bfin])
                                act.op(lambda: S.activation(out=osq[:], in_=oo[:], func=AF.Square), reads=[bfin],
                                       writes=[bfin])

                            def fin_stage2(h, q0=q0):
                                pe.op(lambda: T.matmul(PSB[7][:, 0:G], lhsT=ones_f[:], rhs=osq[:], start=True, stop=True),
                                      reads=[bfin, b_const], writes=[PSBUF[7]])
                                act_pow(rr[0][:], PSB[7][:, 0:G], -0.5, 1.0, CB_128EPS, [PSBUF[7]], [bfin])
                                yi = h % 2
                                dve.op(lambda yi=yi: V.scalar_tensor_tensor(out=yb[yi][:], in0=oo[:],
                                                                            scalar=math.sqrt(128.0) * (1.0 - lam_init),
                                                                            in1=rr[0][:], op0=ALU.mult, op1=ALU.mult),
                                       reads=[bfin], writes=[byb[yi]])
                                q_st.dma(s_yd[h, :, q0:q0 + G], yb[yi][:], reads=[byb[yi]], writes=[B["yd"]])

                            nt_ = len(tiles)
                            for idx in range(nt_ + LA):
                                if idx < nt_:
                                    issue_score(idx)
                                if idx >= LA:
                                    j = idx - LA
                                    issue_pv(j)
                                    h, kt, c = tiles[j]
                                    if kt == nkt - 1 and c == 1:
                                        fin_stage0(h)
                                        tph = 2 * nkt
                                        d1 = min(6, tph)
                                        d2 = min(20, tph + d1 - 1)
                                        deferred.append((idx + d1, fin_stage1, h))
                                        deferred.append((idx + d2, fin_stage2, h))
                                while deferred and deferred[0][0] <= idx:
                                    _, fn_, h_ = deferred.pop(0)
                                    fn_(h_)
                                deferred.sort(key=lambda x: x[0])
                            for (_, fn_, h_) in sorted(deferred, key=lambda x: x[0]):
                                fn_(h_)
                        fw.barrier()

        def phase_m(l):
            with contextlib.ExitStack() as es:
                sel = sbt(es, "sel", [65, 64], F32)
                bsel = Buf("sel")
                q_ld.dma(sel[:], sel65, writes=[bsel])
                for (tok0, S_, key0, NK1, nseq) in BLOCKS:
                    G = min(S_, 512)
                    nkt1 = NK1 // 128
                    NK = NK1 * nseq
                    nkt = nkt1 * nseq
                    with contextlib.ExitStack() as es2:
                        KM = sbt(es2, "KM", [128, 8, NK], BF16)
                        VM = sbt(es2, "VM", [128, nkt, 8, 128], BF16)
                        bkv = Buf("kv")
                        for h in range(8):
                            q_ld.dma(KM[0:64, h, :], s_km[h, 0:64, key0:key0 + NK], reads=[B["km"]], writes=[bkv])
                            q_ld.dma(KM[64:96, h, :], s_kpe[:, key0:key0 + NK], reads=[B["kpe"]], writes=[bkv])
                            q_ld.dma(KM[96:128, h, :], s_km[h, 96:128, key0:key0 + NK], reads=[B["km"]], writes=[bkv])
                        for i in range(0, nkt, 6):
                            m_ = min(6, nkt - i)
                            q_ld.dma(VM[:, i:i + m_, :, :].rearrange("p n h e -> p n (h e)"),
                                     s_vm[key0 + i * 128:key0 + (i + m_) * 128, :, :].rearrange("(n p) h e -> p n (h e)", p=128),
                                     reads=[B["vm"]], writes=[bkv])
                        QM = [sbt(es2, f"QM{i}", [128, 8, G], BF16) for i in range(2)]
                        bQM = [Buf("QM0"), Buf("QM1")]
                        PT = [sbt(es2, f"PT{i}", [128, G], BF16) for i in range(6)]
                        bPT = [Buf(f"PT{i}") for i in range(6)]
                        ptr = Ring([0, 1, 2, 3, 4, 5])
                        osb = [sbt(es2, f"osb{i}", [65, G], F32) for i in range(2)]
                        bosb = [Buf("osb0"), Buf("osb1")]
                        yb = [sbt(es2, f"yb{i}", [64, G], BF16) for i in range(2)]
                        byb = [Buf("yb0"), Buf("yb1")]
                        psc = Ring([0, 1, 2, 7])
                        pso = Ring([3, 4])
                        psb_ = Ring([5, 6])
                        nkt_all = nkt
                        nkt = nkt1
                        for qb_ in range(nseq * (S_ // G)):
                            si, qb = divmod(qb_, S_ // G)
                            q0 = tok0 + si * S_ + qb * G
                            kto = si * nkt
                            qi = qb_ % 2
                            for h in range(8):
                                q_ld.dma(QM[qi][:, h, :], s_qm[h, :, q0:q0 + G], reads=[B["qm"]], writes=[bQM[qi]])
                            tiles = [(h, kt) for h in range(8) for kt in range(nkt)]
                            LA = 3
                            pend = {}
                            pos = {}

                            def issue_score(idx, qi=qi):
                                h, kt = tiles[idx]
                                pb = psc.next()
                                pe.op(lambda: T.matmul(PSB[pb][:, 0:G], lhsT=KM[:, h, (kto + kt) * 128:(kto + kt + 1) * 128], rhs=QM[qi][:, h, :],
                                                       start=True, stop=True), reads=[bkv, bQM[qi]], writes=[PSBUF[pb]])
                                pi = ptr.next()
                                act.op(lambda: S.activation(out=PT[pi][:], in_=PSB[pb][:, 0:G], func=AF.Exp, scale=MLA_SCALE),
                                       reads=[PSBUF[pb]], writes=[bPT[pi]])
                                pend[idx] = pi

                            def issue_pv(idx):
                                h, kt = tiles[idx]
                                if kt == 0:
                                    pos[h] = pso.next()
                                po = pos[h]
                                pi = pend.pop(idx)
                                pe.op(lambda: T.matmul(PSB[po][:, 0:G], lhsT=VM[:, kto + kt, h, :], rhs=PT[pi][:], start=(kt == 0),
                                                       stop=(kt == nkt - 1)), reads=[bkv, bPT[pi]], writes=[PSBUF[po]])

                            deferred = []

                            def fin_stage1(h, q0=q0):
                                po = pos[h]
                                oi = h % 2
                                dve.op(lambda oi=oi, po=po: V.tensor_copy(out=osb[oi][0:64, :], in_=PSB[po][0:64, 0:G]),
                                       reads=[PSBUF[po]], writes=[bosb[oi]])
                                act_pow(osb[oi][64:65, :], PSB[po][64:65, 0:G], -1.0, 1.0, CB_ZERO, [PSBUF[po]], [bosb[oi]],
                                        p0=64, p1=65)

                            def fin_stage2(h, q0=q0):
                                oi = h % 2
                                pbb = psb_.next()
                                pe.op(lambda oi=oi, pbb=pbb: T.matmul(PSB[pbb][0:64, 0:G], lhsT=sel[:], rhs=osb[oi][:],
                                                                      start=True, stop=True),
                                      reads=[bosb[oi], bsel], writes=[PSBUF[pbb]])
                                dve.op(lambda oi=oi, pbb=pbb: V.tensor_tensor(out=yb[oi][:], in0=osb[oi][0:64, :],
                                                                              in1=PSB[pbb][0:64, 0:G], op=ALU.mult),
                                       reads=[bosb[oi], PSBUF[pbb]], writes=[byb[oi]])
                                q_st.dma(s_ym[h * 64:(h + 1) * 64, q0:q0 + G], yb[oi][:], reads=[byb[oi]],
                                         writes=[B["ym"]])

                            nt_ = len(tiles)
                            for idx in range(nt_ + LA):
                                if idx < nt_:
                                    issue_score(idx)
                                if idx >= LA:
                                    j = idx - LA
                                    issue_pv(j)
                                    h, kt = tiles[j]
                                    if kt == nkt - 1:
                                        fin_stage1(h)
                                        deferred.append((idx + min(6, 2 * nkt - 1), fin_stage2, h))
                                while deferred and deferred[0][0] <= idx:
                                    _, fn_, h_ = deferred.pop(0)
                                    fn_(h_)
                            for (_, fn_, h_) in deferred:
                                fn_(h_)
                        fw.barrier()

        def phase_c1(l, xsrc, bsrc):
            with contextlib.ExitStack() as es:
                WC, bWC = load_w(es, "WC", w_c1[l], 8, NC1, "wc1", l)
                WBR, bWBR = load_w(es, "WBR", w_br[l], 12, D, "wbr", l)
                WO, bWO = load_w(es, "WO", w_o[l], 8, D, "wo", l)
                ctx = make_norm_ctx(es, nx=8, nh=2)
                hT, bhT = ctx["hT"], ctx["bhT"]
                xt, bx = ctx["xt"], ctx["bx"]
                g1b = sbt(es, "g1b", [128, 2, D], F32)
                bg1b = Buf("g1b")
                for c in range(2):
                    q_ld.dma(g1b[:, c, :], s_gb[0, c], reads=[B["gb"]], writes=[bg1b])
                YB = sbt(es, "YB", [128, 12, 512], BF16)
                bYB = Buf("YB")
                bYR = Buf("YR")
                srg = sbt(es, "srg", [128, 512], BF16)
                bsrg = Buf("srg")
                gate = [sbt(es, f"gate{i}", [128, 512], F32) for i in range(3)]
                bgate = [Buf(f"gate{i}") for i in range(3)]
                tm = [sbt(es, f"tm{i}", [128, 512], F32) for i in range(3)]
                btm = [Buf(f"tm{i}") for i in range(3)]
                mrg = sbt(es, "mrg", [128, 8, 512], BF16)
                bmrg = Buf("mrg")
                psr = Ring([0, 1, 2, 3, 4, 5])
                pstr = Ring([6, 7])

                def proj_c(pb, col0):
                    fns = [(lambda kc=kc: T.matmul(PSB[pb][:], lhsT=WC[:, kc, col0:col0 + 128], rhs=hT[:, kc, :],
                                                   start=(kc == 0), stop=(kc == 7))) for kc in range(8)]
                    pe.op(fns, reads=[bWC, bhT], writes=[PSBUF[pb]])

                for g in range(NT // 512):
                    g0 = g * 512
                    cidx = 0 if g0 < NS else 1
                    slot = g % 2
                    xo = 4 * slot
                    if g == 0:
                        norm_to_hT(ctx, xsrc, bsrc, g0, cidx, 0, pstr, slot=slot, xo=xo)
                    hT, bhT = ctx["hTs"][slot], ctx["bhTs"][slot]
                    for h in range(4):
                        q_ld.dma(YB[:, h, :], s_oret[h, :, g0:g0 + 512], reads=[B["oret"]], writes=[bYR])
                        q_ld.dma(YB[:, 4 + h, :], s_yd[h, :, g0:g0 + 512], reads=[B["yd"]], writes=[bYB])
                        q_ld.dma(YB[:, 8 + h, :], s_ym[h * 128:(h + 1) * 128, g0:g0 + 512], reads=[B["ym"]], writes=[bYB])
                    for j in range(4):
                        pb = psr.next()
                        proj_c(pb, j * 128)
                        act.op(lambda pb=pb: S.activation(out=srg[:], in_=PSB[pb][:], func=AF.Silu), reads=[PSBUF[pb]],
                               writes=[bsrg])
                        pool.op(lambda j=j: G_.tensor_tensor(out=YB[:, j, :], in0=YB[:, j, :], in1=srg[:], op=ALU.mult),
                                reads=[bsrg, bYR], writes=[bYR])
                    if g0 + 512 < NT:
                        norm_to_hT(ctx, xsrc, bsrc, g0 + 512, 0 if g0 + 512 < NS else 1, 0, pstr, slot=1 - slot, xo=4 - xo)
                    for oc in range(8):
                        for gi in range(3):
                            pb = psr.next()
                            proj_c(pb, 512 + gi * D + oc * 128)
                            act.op(lambda pb=pb, gi=gi: S.activation(out=gate[gi][:], in_=PSB[pb][:], func=AF.Sigmoid),
                                   reads=[PSBUF[pb]], writes=[bgate[gi]])
                            pb2 = psr.next()
                            fns = [(lambda kc=kc, gi=gi, oc=oc, pb2=pb2: T.matmul(
                                PSB[pb2][:], lhsT=WBR[:, gi * 4 + kc, oc * 128:(oc + 1) * 128], rhs=YB[:, gi * 4 + kc, :],
                                start=(kc == 0), stop=(kc == 3))) for kc in range(4)]
                            pe.op(fns, reads=[bWBR, bYB, bYR], writes=[PSBUF[pb2]])
                            dve.op(lambda gi=gi, pb2=pb2: V.tensor_tensor(out=tm[gi][:], in0=PSB[pb2][:], in1=gate[gi][:],
                                                                          op=ALU.mult),
                                   reads=[PSBUF[pb2], bgate[gi]], writes=[btm[gi]])
                        dve.op(lambda: V.tensor_tensor(out=tm[0][:], in0=tm[0][:], in1=tm[1][:], op=ALU.add),
                               reads=[btm[0], btm[1]], writes=[btm[0]])
                        pool.op(lambda oc=oc: G_.tensor_tensor(out=mrg[:, oc, :], in0=tm[0][:], in1=tm[2][:], op=ALU.add),
                                reads=[btm[0], btm[2]], writes=[bmrg])
                    for t in range(4):
                        for half in range(2):
                            pb = psr.next()
                            fns = [(lambda kc=kc, t=t, half=half, pb=pb: T.matmul(
                                PSB[pb][:], lhsT=mrg[:, kc, t * 128:(t + 1) * 128], rhs=WO[:, kc, half * 512:(half + 1) * 512],
                                start=(kc == 0), stop=(kc == 7))) for kc in range(8)]
                            pe.op(fns, reads=[bWO, bmrg], writes=[PSBUF[pb]])
                            ti = half
                            dve.op(lambda ti=ti, pb=pb, half=half: V.tensor_tensor(
                                out=tm[ti][:], in0=PSB[pb][:], in1=g1b[:, cidx, half * 512:(half + 1) * 512], op=ALU.mult),
                                   reads=[PSBUF[pb], bg1b], writes=[btm[ti]])
                            (dve if half == 0 else pool).op(lambda ti=ti, t=t, half=half, xo=xo: (V if half == 0 else G_).tensor_tensor(
                                out=xt[xo + t][:, half * 512:(half + 1) * 512], in0=xt[xo + t][:, half * 512:(half + 1) * 512],
                                in1=tm[ti][:], op=ALU.add), reads=[btm[ti], bx[xo + t]], writes=[bx[xo + t]])
                        q_st.dma(s_xmid[g0 + t * 128:g0 + (t + 1) * 128, :], xt[xo + t][:], reads=[bx[xo + t]], writes=[B["xmid"]])
                fw.barrier()

        def phase_c2(l, last):
            with contextlib.ExitStack() as es:
                W1, bW1 = load_w(es, "W1", w_f1[l], 8, 2 * DFF, "wf1", l)
                W2, bW2 = load_w(es, "W2", w_f2[l], 22, D, "wf2", l)
                ctx = make_norm_ctx(es)
                hT, bhT = ctx["hT"], ctx["bhT"]
                xt, bx = ctx["xt"], ctx["bx"]
                g2b = sbt(es, "g2b", [128, D], F32)
                bg2b = Buf("g2b")
                cur_c = [-1]
                bfg = Buf("fg")
                if last:
                    fg = sbt(es, "fg", [128, D], F32)
                    q_ld.dma(fg[:], fin_g.partition_broadcast(128), writes=[bfg])
                uT = sbt(es, "uT", [128, 22, 512], BF16)
                buT = Buf("uT")
                sa = [sbt(es, f"sa{i}", [128, 512], BF16) for i in range(2)]
                bsa = [Buf("sa0"), Buf("sa1")]
                tm = [sbt(es, f"tm{i}", [128, 512], F32) for i in range(2)]
                btm = [Buf("tm0"), Buf("tm1")]
                fs = sbt(es, "fs", [128, 8], F32)
                bfs = Buf("fs")
                psr = Ring([0, 1, 2, 3, 4, 5])
                pstr = Ring([6, 7])
                for g in range(NT // 512):
                    g0 = g * 512
                    cidx = 0 if g0 < NS else 1
                    if cur_c[0] != cidx:
                        cur_c[0] = cidx
                        q_ld.dma(g2b[:], s_gb[1, cidx], reads=[B["gb"]], writes=[bg2b])
                    norm_to_hT(ctx, s_xmid, B["xmid"], g0, cidx, 1, pstr)
                    for j in range(22):
                        pa, pb = psr.next(), psr.next()
                        for (pp, col0) in [(pa, j * 128), (pb, DFF + j * 128)]:
                            fns = [(lambda kc=kc, pp=pp, col0=col0: T.matmul(PSB[pp][:], lhsT=W1[:, kc, col0:col0 + 128],
                                                                            rhs=hT[:, kc, :], start=(kc == 0),
                                                                            stop=(kc == 7))) for kc in range(8)]
                            pe.op(fns, reads=[bW1, bhT], writes=[PSBUF[pp]])
                        si = j % 2
                        act.op(lambda si=si, pa=pa: S.activation(out=sa[si][:], in_=PSB[pa][:], func=AF.Silu),
                               reads=[PSBUF[pa]], writes=[bsa[si]])
                        dve.op(lambda si=si, pb=pb, j=j: V.tensor_tensor(out=uT[:, j, :], in0=PSB[pb][:], in1=sa[si][:],
                                                                         op=ALU.mult),
                               reads=[PSBUF[pb], bsa[si]], writes=[buT])
                    for t in range(4):
                        for half in range(2):
                            pb = psr.next()
                            fns = [(lambda j=j, t=t, half=half, pb=pb: T.matmul(
                                PSB[pb][:], lhsT=uT[:, j, t * 128:(t + 1) * 128], rhs=W2[:, j, half * 512:(half + 1) * 512],
                                start=(j == 0), stop=(j == 21))) for j in range(22)]
                            pe.op(fns, reads=[bW2, buT], writes=[PSBUF[pb]])
                            ti = half
                            dve.op(lambda ti=ti, pb=pb, half=half: V.tensor_tensor(
                                out=tm[ti][:], in0=PSB[pb][:], in1=g2b[:, half * 512:(half + 1) * 512], op=ALU.mult),
                                   reads=[PSBUF[pb], bg2b], writes=[btm[ti]])
                            (dve if half == 0 else pool).op(lambda ti=ti, t=t, half=half: (V if half == 0 else G_).tensor_tensor(
                                out=xt[t][:, half * 512:(half + 1) * 512], in0=xt[t][:, half * 512:(half + 1) * 512],
                                in1=tm[ti][:], op=ALU.add), reads=[btm[ti], bx[t]], writes=[bx[t]])
                        if not last:
                            q_st.dma(s_x1[g0 + t * 128:g0 + (t + 1) * 128, :], xt[t][:], reads=[bx[t]], writes=[B["x1"]])
                        else:
                            junk, bjunk = ctx["junk"], ctx["bjunk"]
                            act.op(lambda t=t: S.activation(out=junk[:], in_=xt[t][:], func=AF.Square,
                                                            accum_out=fs[:, t:t + 1]), reads=[bx[t]],
                                   writes=[bjunk, bfs])
                            act_pow(fs[:, 4 + t:5 + t], fs[:, t:t + 1], -0.5, 1.0 / D, CB_EPS, [bfs], [bfs])
                            dve.op(lambda t=t: V.scalar_tensor_tensor(out=xt[t][:], in0=xt[t][:], scalar=fs[:, 4 + t:5 + t],
                                                                      in1=fg[:], op0=ALU.mult, op1=ALU.mult),
                                   reads=[bx[t], bfs, bfg], writes=[bx[t]])
                            q_st.dma(y_out[g0 + t * 128:g0 + (t + 1) * 128, :], xt[t][:], reads=[bx[t]], writes=[B["out"]])
                fw.barrier()

        plist = []
        for l in range(DEPTH):
            plist += [(l, p) for p in ["ada", "A", "R", "D", "M", "C1", "C2"]]
        for (l, p) in plist:
            xsrc, bsrc = (x_in, Buf("x_in")) if l == 0 else (s_x1, B["x1"])
            if p == "ada":
                ada_phase(l)
            elif p == "A":
                phase_a(l, xsrc, bsrc)
            elif p == "R":
                phase_r(l)
            elif p == "D":
                phase_d(l)
            elif p == "M":
                phase_m(l)
            elif p == "C1":
                phase_c1(l, xsrc, bsrc)
            elif p == "C2":
                phase_c2(l, last=(l == DEPTH - 1))
            if stop_after is not None and (l, p) == stop_after:
                break
        fw.barrier()
        nc._fw_stats = (fw.ninst, fw.ndma)
    return nc


def _swap_idx(n):
    h = n // 2
    return list(range(h, n)) + list(range(0, h))


def _perm_a():
    idx = []
    sw64 = _swap_idx(64)
    def sec(off, n):
        return list(range(off, off + n))
    def sec_sw(off, n, blk):
        out = []
        sw = _swap_idx(blk)
        for b in range(n // blk):
            out += [off + b * blk + s for s in sw]
        return out
    idx += sec(O_DQ, 512)
    idx += sec_sw(O_DQ, 512, 64)
    idx += sec(O_DK, 512)
    idx += sec_sw(O_DK, 512, 64)
    for h in range(4):
        idx += sec(O_RQ + h * 64, 64) * 2
    for h in range(4):
        idx += sec_sw(O_RQ + h * 64, 64, 64) * 2
    idx += sec(O_RK, 256)
    idx += sec_sw(O_RK, 256, 64)
    idx += sec(O_CQ, 384)
    idx += sec(O_CKV, 256)
    idx += sec(O_KPE, 32)
    idx += sec_sw(O_KPE, 32, 32)
    idx += sec(O_DV, 512)
    idx += sec(O_RV, 512)
    assert len(idx) == NA
    return np.array(idx)


def _rope_tables():
    t = np.arange(NS)
    row = (t // GRID_W).astype(np.float32)
    col = (t % GRID_W).astype(np.float32)

    def ang_tab(rot_dim):
        nf = rot_dim // 4
        inv = (np.float32(10000.0) ** (-np.arange(nf, dtype=np.float32) / np.float32(nf))).astype(np.float32)
        ang = np.concatenate([row[:, None] * inv, col[:, None] * inv], axis=-1).astype(np.float32)
        return np.cos(ang).astype(np.float32), np.sin(ang).astype(np.float32)

    c64, s64 = ang_tab(64)
    c32, s32 = ang_tab(32)
    C64 = np.zeros((128, NS), np.float32)
    S64 = np.zeros((128, NS), np.float32)
    for p in range(128):
        d = p % 64
        i = d % 32
        C64[p] = c64[:, i]
        S64[p] = -s64[:, i] if d < 32 else s64[:, i]
    C96 = np.ones((128, NS), np.float32)
    S96 = np.zeros((128, NS), np.float32)
    for j in range(32):
        i = j % 16
        C96[64 + j] = c32[:, i]
        S96[64 + j] = -s32[:, i] if j < 16 else s32[:, i]
    return C64, S64, C96, S96


def _host_inputs(inp):
    f = np.float32
    pa = _perm_a()
    w_in = inp["w_in"]
    w_a = np.ascontiguousarray(w_in[:, :, pa])
    w_c1 = np.ascontiguousarray(np.concatenate([w_in[:, :, O_RG:O_RG + 512], w_in[:, :, O_GATE:O_GATE + 3072]], axis=2))
    uq = inp["w_uq"]
    idx = []
    for h in range(8):
        idx += list(range(h * 96, h * 96 + 64)) + [h * 96 + 64 + s for s in _swap_idx(32)]
    w_uq2 = np.ascontiguousarray(np.concatenate([uq, uq[:, :, np.array(idx)]], axis=2))
    ukv = inp["w_ukv"]
    idxk = []
    idxv = []
    for h in range(8):
        idxk += list(range(h * 128, h * 128 + 64))
        idxv += list(range(h * 128 + 64, h * 128 + 128))
    w_ukv2 = np.ascontiguousarray(ukv[:, :, np.array(idxk + idxv)])
    C64, S64, C96, S96 = _rope_tables()
    C32 = np.ones((128, NS), f); S32 = np.zeros((128, NS), f)
    C32[0:32] = C96[64:96]; S32[0:32] = S96[64:96]
    rope_t = np.stack([C64, S64, (C64 * 0.125).astype(f), (S64 * 0.125).astype(f), C96, S96, C32, S32]).astype(f)
    i = np.arange(128, dtype=f)
    jj = i[:, None]
    ii = i[None, :]
    DF = np.maximum(ii - jj, 0.0)
    DB = np.maximum(jj - ii, 0.0)
    MF = (ii >= jj).astype(f)
    MB = (jj > ii).astype(f)
    XIc = np.zeros((128, 128), f)
    XIc[0:64, :] = (i + 1.0)[None, :]
    XIc[64:128, :] = (128.0 - i)[None, :]
    ret_c = np.stack([DF, DB, MF, MB, XIc]).astype(f)
    ret_z = np.stack([127.0 - i, i], axis=1).astype(f)
    sel65 = np.zeros((65, 64), f)
    sel65[64, :] = 1.0
    common = {
        "n1g": np.ascontiguousarray(inp["norm1_g"].reshape(DEPTH, 8, 128).transpose(0, 2, 1)),
        "n2g": np.ascontiguousarray(inp["norm2_g"].reshape(DEPTH, 8, 128).transpose(0, 2, 1)),
        "w_ada": inp["w_ada"],
        "b_ada_fm": np.ascontiguousarray(inp["b_ada"].reshape(DEPTH, 48, 128).transpose(0, 2, 1)),
        "b_ada": inp["b_ada"],
        "w_a": w_a, "w_c1": w_c1,
        "dec_f": inp["ret_decay_fwd"], "dec_b": inp["ret_decay_bwd"],
        "dlam": np.ascontiguousarray(inp["diff_lambda"].reshape(DEPTH, 256)),
        "qng": np.ascontiguousarray(inp["mla_q_norm"].reshape(DEPTH, 3, 128).transpose(0, 2, 1)),
        "kvng": np.ascontiguousarray(inp["mla_kv_norm"].reshape(DEPTH, 2, 128).transpose(0, 2, 1)),
        "kvn_row": inp["mla_kv_norm"],
        "w_uq": w_uq2, "w_ukv": w_ukv2,
        "w_br": np.ascontiguousarray(inp["w_branch"].reshape(DEPTH, 1536, D)),
        "w_o": inp["w_out"], "w_f1": inp["w_ffn_in"], "w_f2": inp["w_ffn_out"],
        "fin_g": inp["final_g"],
        "ident_in": np.eye(128, dtype=f),
        "rope_t": rope_t, "ret_c": ret_c, "ret_z": ret_z, "sel65": sel65,
    }
    maps = []
    for c in range(8):
        m = dict(common)
        m["x_in"] = np.ascontiguousarray(np.concatenate(
            [inp["x_sample"][c], inp["x_prompt"][4 * c:4 * c + 4].reshape(NPSEQ * PS_, D)], axis=0))
        m["st_f"] = np.ascontiguousarray(inp["state_ret_fwd"][c])
        m["st_b"] = np.ascontiguousarray(inp["state_ret_bwd"][c])
        m["c_dk"] = np.ascontiguousarray(inp["cache_diff_k"][c].reshape(DEPTH, LCTX, 512))
        m["c_dv"] = np.ascontiguousarray(inp["cache_diff_v"][c].reshape(DEPTH, LCTX, 512))
        m["c_ckv"] = np.ascontiguousarray(inp["cache_mla_ckv"][c])
        m["c_kpe"] = np.ascontiguousarray(inp["cache_mla_kpe"][c])
        cond = np.stack([inp["c"][c], inp["c_ctx"]], axis=1)
        m["cond_fm"] = np.ascontiguousarray(cond.reshape(8, 128, 2).transpose(1, 0, 2))
        maps.append(m)
    return maps


_NC_CACHE = {}


def kernel(**inputs):
    inp = {k: np.asarray(v) for k, v in inputs.items()}
    if "nc" not in _NC_CACHE:
        _NC_CACHE["nc"] = build_program()
    nc = _NC_CACHE["nc"]
    maps = _host_inputs(inp)
    res = run_bass_kernel_spmd(nc, maps, core_ids=list(range(8)))
    R = res.results
    y_sample = np.stack([R[c]["y_out"][:NS] for c in range(8)]).astype(np.float32)
    y_prompt = np.concatenate([R[c]["y_out"][NS:].reshape(NPSEQ, PS_, D) for c in range(8)]).astype(np.float32)
    rf = np.concatenate([R[c]["o_rf"] for c in range(8)]).astype(np.float32)
    rb = np.concatenate([R[c]["o_rb"] for c in range(8)]).astype(np.float32)
    dk = np.concatenate([R[c]["o_dk"] for c in range(8)]).reshape(32, DEPTH, PS_, 4, 128).astype(np.float32)
    dv = np.concatenate([R[c]["o_dv"] for c in range(8)]).reshape(32, DEPTH, PS_, 4, 128).astype(np.float32)
    ckv = np.concatenate([R[c]["o_ckv"] for c in range(8)]).astype(np.float32)
    kpe = np.concatenate([R[c]["o_kpe"] for c in range(8)]).astype(np.float32)
    return (y_prompt, y_sample, rf, rb, dk, dv, ckv, kpe)
```

```python
import contextlib
import math
import numpy as np
import concourse.bass as bass
import concourse.mybir as mybir
from concourse.bass_utils import run_bass_kernel_spmd

F32 = mybir.dt.float32
BF16 = mybir.dt.bfloat16
AF = mybir.ActivationFunctionType
ALU = mybir.AluOpType

D = 1024
DEPTH = 2
NS = 4096
NPSEQ = 4
PS_ = 256
NT = NS + NPSEQ * PS_
LCTX = 512
NKEY = LCTX + NT
DFF = 2816
EPS = 1e-6
MLA_SCALE = 96 ** -0.5
GRID_W = 64

O_RQ, O_RK, O_RV, O_RG, O_DQ, O_DK, O_DV, O_CQ, O_CKV, O_KPE, O_GATE = 0, 256, 512, 1024, 1536, 2048, 2560, 3072, 3456, 3712, 3744
A_DQ, A_DQS, A_DK, A_DKS, A_RQ2, A_RQ2S, A_RK, A_RKS, A_CQ, A_CKV, A_KPE, A_KPES, A_DV, A_RV = (
    0, 512, 1024, 1536, 2048, 2560, 3072, 3328, 3584, 3968, 4224, 4256, 4288, 4800)
NA = 5312
NC1 = 512 + 3072


class Buf:
    __slots__ = ("name", "w", "r", "toks")

    def __init__(self, name=""):
        self.name = name
        self.w = None
        self.r = {}


class Eng:
    def __init__(self, fw, name, eng, sem, self_sync=True):
        self.fw = fw
        self.name = name
        self.e = eng
        self.sem = sem
        self.count = 0
        self.known = {}
        self.self_sync = self_sync

    def wait_tok(self, tok):
        if tok is None:
            return
        sem, val = tok
        if sem is self.sem and not self.self_sync:
            return
        k = id(sem)
        if self.known.get(k, 0) >= val:
            return
        self.e.wait_ge(sem, val)
        self.known[k] = val

    def deps(self, reads, writes):
        for b in reads:
            self.wait_tok(b.w)
        for b in writes:
            self.wait_tok(b.w)
            for tok in list(b.r.values()):
                self.wait_tok(tok)

    def mark(self, tok, reads, writes):
        sem, val = tok
        for b in reads:
            old = b.r.get(id(sem))
            if old is None or old[1] < val:
                b.r[id(sem)] = (sem, val)
        for b in writes:
            b.w = tok
            b.r = {}

    def op(self, fns, reads=(), writes=()):
        self.deps(reads, writes)
        if callable(fns):
            fns = [fns]
        ins = None
        for f in fns:
            ins = f()
        self.count += 1
        ins.then_inc(self.sem, 1)
        tok = (self.sem, self.count)
        self.mark(tok, reads, writes)
        self.fw.ninst += len(fns)
        return tok


class DmaQ:
    def __init__(self, fw, name, eng, sems):
        self.fw = fw
        self.eng = eng
        self.sems = sems
        self.tot = [0] * len(sems)
        self.i = 0

    def dma(self, out, in_, reads=(), writes=()):
        q = self.eng
        q.deps(reads, writes)
        i = self.i
        self.i = (self.i + 1) % len(self.sems)
        sem = self.sems[i]
        if self.tot[i] > 0:
            q.wait_tok((sem, self.tot[i]))
        self.tot[i] += 16
        q.e.dma_start(out=out, in_=in_).then_inc(sem, 16)
        tok = (sem, self.tot[i])
        q.mark(tok, reads, writes)
        self.fw.ndma += 1
        return tok


class FW:
    def __init__(self, nc, es):
        self.nc = nc
        self.ninst = 0
        self.ndma = 0
        mk = lambda n: es.enter_context(nc.semaphore(n))
        self.pe = Eng(self, "pe", nc.tensor, mk("s_pe"), self_sync=False)
        self.act = Eng(self, "act", nc.scalar, mk("s_act"))
        self.dve = Eng(self, "dve", nc.vector, mk("s_dve"))
        self.pool = Eng(self, "pool", nc.gpsimd, mk("s_pool"))
        self.sp = Eng(self, "sp", nc.sync, mk("s_sp"))
        self.engs = [self.pe, self.act, self.dve, self.pool, self.sp]
        self.q_ld = DmaQ(self, "q_ld", self.sp, [mk(f"s_ld{i}") for i in range(12)])
        self.q_st = DmaQ(self, "q_st", self.pool, [mk(f"s_st{i}") for i in range(12)])
        self.qs = [self.q_ld, self.q_st]

    def barrier(self):
        toks = [(e.sem, e.count) for e in self.engs if e.count > 0]
        for q in self.qs:
            for s, t in zip(q.sems, q.tot):
                if t > 0:
                    toks.append((s, t))
        for e in self.engs:
            for tok in toks:
                if tok[0] is e.sem:
                    continue
                e.wait_tok(tok)


class _StopBuild(Exception):
    pass


def _ckpt(name):
    import os
    if os.environ.get("ASTOP", "") == name:
        raise _StopBuild(name)


def build_program(stop_after=None):
    nc = bass.Bass("TRN2", target_bir_lowering=False)

    def din(name, shape, dt=F32):
        return nc.dram_tensor(name, list(shape), dt, kind="ExternalInput").ap()

    def dout(name, shape, dt=F32):
        return nc.dram_tensor(name, list(shape), dt, kind="ExternalOutput").ap()

    def dscr(name, shape, dt=BF16):
        return nc.dram_tensor(name, list(shape), dt, kind="Internal").ap()

    x_in = din("x_in", [NT, D])
    st_f = din("st_f", [DEPTH, 4, 64, 128])
    st_b = din("st_b", [DEPTH, 4, 64, 128])
    c_dk = din("c_dk", [DEPTH, LCTX, 512])
    c_dv = din("c_dv", [DEPTH, LCTX, 512])
    c_ckv = din("c_ckv", [DEPTH, LCTX, 256])
    c_kpe = din("c_kpe", [DEPTH, LCTX, 32])
    cond_fm = din("cond_fm", [128, 8, 2])
    n1g = din("n1g", [DEPTH, 128, 8])
    n2g = din("n2g", [DEPTH, 128, 8])
    w_ada = din("w_ada", [DEPTH, D, 6 * D])
    b_ada_fm = din("b_ada_fm", [DEPTH, 128, 48])
    b_ada = din("b_ada", [DEPTH, 6 * D])
    w_a = din("w_a", [DEPTH, D, NA])
    w_c1 = din("w_c1", [DEPTH, D, NC1])
    dec_f = din("dec_f", [DEPTH, 4])
    dec_b = din("dec_b", [DEPTH, 4])
    dlam = din("dlam", [DEPTH, 256])
    qng = din("qng", [DEPTH, 128, 3])
    kvng = din("kvng", [DEPTH, 128, 2])
    kvn_row = din("kvn_row", [DEPTH, 256])
    w_uq = din("w_uq", [DEPTH, 384, 1536])
    w_ukv = din("w_ukv", [DEPTH, 256, 1024])
    w_br = din("w_br", [DEPTH, 1536, D])
    w_o = din("w_o", [DEPTH, D, D])
    w_f1 = din("w_f1", [DEPTH, D, 2 * DFF])
    w_f2 = din("w_f2", [DEPTH, DFF, D])
    fin_g = din("fin_g", [D])
    ident_in = din("ident_in", [128, 128])
    rope_t = din("rope_t", [8, 128, NS])
    ret_c = din("ret_c", [5, 128, 128])
    ret_z = din("ret_z", [128, 2])
    sel65 = din("sel65", [65, 64])

    y_out = dout("y_out", [NT, D])
    o_rf = dout("o_rf", [NPSEQ, DEPTH, 4, 64, 128])
    o_rb = dout("o_rb", [NPSEQ, DEPTH, 4, 64, 128])
    o_dk = dout("o_dk", [NPSEQ, DEPTH, PS_, 512])
    o_dv = dout("o_dv", [NPSEQ, DEPTH, PS_, 512])
    o_ckv = dout("o_ckv", [NPSEQ, DEPTH, PS_, 256])
    o_kpe = dout("o_kpe", [NPSEQ, DEPTH, PS_, 32])

    s_qd = dscr("s_qd", [4, 128, NT])
    s_kd = dscr("s_kd", [4, 128, NKEY])
    s_vd = dscr("s_vd", [NKEY, 512])
    s_rq = dscr("s_rq", [4, 128, NT])
    s_rk = dscr("s_rk", [4, 64, NT])
    s_rv = dscr("s_rv", [NT, 512])
    s_qm = dscr("s_qm", [8, 128, NT])
    s_km = dscr("s_km", [8, 128, NKEY])
    s_kpe = dscr("s_kpe", [32, NKEY])
    s_vm = dscr("s_vm", [NKEY, 8, 128])
    s_oret = dscr("s_oret", [4, 128, NT])
    s_yd = dscr("s_yd", [4, 128, NT])
    s_ym = dscr("s_ym", [512, NT])
    s_xmid = dscr("s_xmid", [NT, D], F32)
    s_x1 = dscr("s_x1", [NT, D], F32)
    s_gb = dscr("s_gb", [2, 2, 128, D], F32)

    SEQS = [(0, NS, 0, LCTX + NS, True)] + [
        (NS + p * PS_, PS_, LCTX + NS + p * PS_, PS_, False) for p in range(NPSEQ)]

    BLOCKS = [(0, NS, 0, LCTX + NS, 1), (NS, PS_, LCTX + NS, PS_, NPSEQ)]

    with contextlib.ExitStack() as ges:
        fw = FW(nc, ges)
        pe, act, dve, pool, q_ld, q_st = fw.pe, fw.act, fw.dve, fw.pool, fw.q_ld, fw.q_st
        V, S, T, G_ = nc.vector, nc.scalar, nc.tensor, nc.gpsimd

        _uid = [0]

        def sbt(es, name, shape, dt):
            _uid[0] += 1
            return es.enter_context(nc.sbuf_tensor(f"{name}_{_uid[0]}", list(shape), dt))

        PSB = [ges.enter_context(nc.psum_tensor(f"psb{i}", [128, 512], F32)) for i in range(8)]
        PSBUF = [Buf(f"psb{i}") for i in range(8)]

        class Ring:
            def __init__(self, items):
                self.items = items
                self.i = 0

            def next(self):
                it = self.items[self.i]
                self.i = (self.i + 1) % len(self.items)
                return it

        B = {}
        for nm in ["wa", "wc1", "wuq", "wukv", "wbr", "wo", "wf1", "wf2"]:
            for l in range(DEPTH):
                B[(nm, l)] = Buf(nm)
        for nm in ["qd", "kd", "vd", "rq", "rk", "rv", "qm", "km", "kpe", "vm", "oret", "yd", "ym", "xmid", "x1", "gb",
                   "out"]:
            B[nm] = Buf(nm)

        ident_f = sbt(ges, "ident_f", [128, 128], F32)
        ident_b = sbt(ges, "ident_b", [128, 128], BF16)
        ones_b = sbt(ges, "ones_b", [128, 128], BF16)
        ones_f = sbt(ges, "ones_f", [128, 128], F32)
        mods = sbt(ges, "mods", [128, 2, 4, 8], F32)
        b_mods = Buf("mods")
        b_const = Buf("const")
        q_ld.dma(ident_f[:], ident_in, writes=[b_const])
        dve.op(lambda: V.tensor_copy(out=ident_b[:], in_=ident_f[:]), reads=[b_const], writes=[b_const])
        dve.op(lambda: V.memset(ones_b[:], 1.0), writes=[b_const])
        dve.op(lambda: V.memset(ones_f[:], 1.0), writes=[b_const])
        cb_t = sbt(ges, "cb_t", [128, 4], F32)
        for i_, v_ in enumerate([EPS, 1.0, 128.0 * EPS, 0.0]):
            dve.op(lambda i_=i_, v_=v_: V.memset(cb_t[:, i_:i_ + 1], v_), writes=[b_const])
        CB_EPS, CB_ONE, CB_128EPS, CB_ZERO = 0, 1, 2, 3

        def act_pow(out_ap, in_ap, p, mul, cbi, reads, writes, p0=0, p1=128):
            act.op(lambda: S.activation(out=out_ap, in_=in_ap, func=AF.Ln, scale=mul, bias=cb_t[p0:p1, cbi:cbi + 1]),
                   reads=list(reads) + [b_const], writes=writes)
            act.op(lambda: S.activation(out=out_ap, in_=out_ap, func=AF.Exp, scale=p), reads=writes, writes=writes)

        with contextlib.ExitStack() as ies:
            zt = sbt(ies, "zt", [32, NKEY], BF16)
            vt = sbt(ies, "vt", [128, 8, 64], BF16)
            bzt = Buf("zt")
            dve.op(lambda: V.memset(zt[:], 0.0), writes=[bzt])
            dve.op(lambda: V.memset(vt[:], 0.0), writes=[bzt])
            dve.op(lambda: V.memset(vt[:, :, 0:1], 1.0), writes=[bzt])
            for h_ in range(8):
                q_ld.dma(s_km[h_, 96:128, :], zt[:, 0:NKEY], reads=[bzt], writes=[B["km"]])
                q_ld.dma(s_qm[h_, 96:128, :], zt[:, 0:NT], reads=[bzt], writes=[B["qm"]])
            fw.barrier()

        def ada_phase(l):
            with contextlib.ExitStack() as es:
                cond = sbt(es, "cond", [128, 8, 2], F32)
                scond = sbt(es, "scond", [128, 8, 2], F32)
                scb = sbt(es, "scb", [128, 2, 8, 128], F32)
                wblk = [sbt(es, f"wblk{i}", [128, 8, 512], F32) for i in range(3)]
                b_wblk = [Buf("wblk0"), Buf("wblk1"), Buf("wblk2")]
                bfm = sbt(es, "bfm", [128, 48], F32)
                brow = sbt(es, "brow", [1, 6 * D], F32)
                g1t = sbt(es, "g1t", [128, 8], F32)
                g2t = sbt(es, "g2t", [128, 8], F32)
                modT = sbt(es, "modT", [128, 48, 2], F32)
                gbt = [sbt(es, f"gbt{i}", [128, 512], F32) for i in range(2)]
                b_gbt = [Buf("gbt0"), Buf("gbt1")]
                b_c = Buf("cond"); b_sc = Buf("scond"); b_scb = Buf("scb"); b_misc = Buf("misc"); b_modT = Buf("modT")
                q_ld.dma(cond[:], cond_fm, writes=[b_c])
                q_ld.dma(bfm[:], b_ada_fm[l], writes=[b_misc])
                q_ld.dma(brow[:], b_ada[l:l + 1, :], writes=[b_misc])
                q_ld.dma(g1t[:], n1g[l], writes=[b_misc])
                q_ld.dma(g2t[:], n2g[l], writes=[b_misc])
                act.op(lambda: S.activation(out=scond[:], in_=cond[:], func=AF.Silu), reads=[b_c], writes=[b_sc])
                for c in range(2):
                    for kc in range(8):
                        dve.op(lambda c=c, kc=kc: V.tensor_scalar(out=scb[:, c, kc, :], in0=ones_f[:],
                                                                  scalar1=scond[:, kc, c:c + 1], scalar2=None,
                                                                  op0=ALU.mult),
                               reads=[b_sc, b_const], writes=[b_scb])
                pm = PSB[7]
                bpm = PSBUF[7]
                pmv = pm[:, 0:96].rearrange("p (j c) -> p j c", c=2)
                gring = Ring([0, 1])
                for cb in range(12):
                    wi = cb % 3
                    wt = wblk[wi]
                    for kc in range(8):
                        q_ld.dma(wt[:, kc, :], w_ada[l, kc * 128:(kc + 1) * 128, cb * 512:(cb + 1) * 512],
                                 writes=[b_wblk[wi]])
                    if cb in (4, 5, 10, 11):
                        gi = 0 if cb < 6 else 1
                        half = cb % 2
                        for c in range(2):
                            pb = gring.next()
                            ps_, bps = PSB[pb], PSBUF[pb]
                            fns = [(lambda kc=kc, c=c, ps_=ps_, wt=wt: T.matmul(ps_[:], lhsT=scb[:, c, kc, :],
                                                                            rhs=wt[:, kc, :], start=(kc == 0),
                                                                            stop=False)) for kc in range(8)]
                            fns.append(lambda ps_=ps_, cb=cb: T.matmul(ps_[:], lhsT=ones_f[0:1, :],
                                                                      rhs=brow[0:1, cb * 512:(cb + 1) * 512],
                                                                      start=False, stop=True))
                            pe.op(fns, reads=[b_scb, b_wblk[wi], b_misc, b_const], writes=[bps])
                            gt = gbt[pb]
                            act.op(lambda gt=gt, ps_=ps_: S.copy(out=gt[:], in_=ps_[:]), reads=[bps], writes=[b_gbt[pb]])
                            q_st.dma(s_gb[gi, c, :, half * 512:(half + 1) * 512], gt[:], reads=[b_gbt[pb]],
                                     writes=[B["gb"]])
                    for jj in range(4):
                        j = cb * 4 + jj
                        fns = [(lambda kc=kc, j=j, jj=jj, wt=wt: T.matmul(pmv[:, j, :],
                                                                       lhsT=wt[:, kc, jj * 128:(jj + 1) * 128],
                                                                       rhs=scond[:, kc, :], start=(kc == 0),
                                                                       stop=(kc == 7))) for kc in range(8)]
                        pe.op(fns, reads=[b_sc, b_wblk[wi]], writes=[bpm])
                for c in range(2):
                    dve.op(lambda c=c: V.tensor_tensor(out=modT[:, :, c], in0=pmv[:, :, c], in1=bfm[:], op=ALU.add),
                           reads=[bpm, b_misc], writes=[b_modT])
                for c in range(2):
                    dve.op(lambda c=c: V.scalar_tensor_tensor(out=mods[:, c, 0, :], in0=modT[:, 8:16, c], scalar=1.0,
                                                              in1=g1t[:], op0=ALU.add, op1=ALU.mult),
                           reads=[b_modT, b_misc], writes=[b_mods])
                    dve.op(lambda c=c: V.tensor_copy(out=mods[:, c, 1, :], in_=modT[:, 0:8, c]),
                           reads=[b_modT], writes=[b_mods])
                    dve.op(lambda c=c: V.scalar_tensor_tensor(out=mods[:, c, 2, :], in0=modT[:, 32:40, c], scalar=1.0,
                                                              in1=g2t[:], op0=ALU.add, op1=ALU.mult),
                           reads=[b_modT, b_misc], writes=[b_mods])
                    dve.op(lambda c=c: V.tensor_copy(out=mods[:, c, 3, :], in_=modT[:, 24:32, c]),
                           reads=[b_modT], writes=[b_mods])
                fw.barrier()

        def make_norm_ctx(es, nx=4, nh=1):
            ctx = {}
            ctx["xt"] = [sbt(es, f"xt{i}", [128, D], F32) for i in range(nx)]
            ctx["bx"] = [Buf(f"xt{i}") for i in range(nx)]
            ctx["xn"] = [sbt(es, f"xn{i}", [128, D], BF16) for i in range(4)]
            ctx["bxn"] = [Buf(f"xn{i}") for i in range(4)]
            ctx["junk"] = sbt(es, "junk", [128, D], BF16)
            ctx["bjunk"] = Buf("junk")
            ctx["ss"] = sbt(es, "ss", [128, 8], F32)
            ctx["bss"] = [Buf(f"ss{i}") for i in range(4)]
            ctx["hTs"] = [sbt(es, f"hT{i}", [128, 8, 512], BF16) for i in range(nh)]
            ctx["bhTs"] = [Buf(f"hT{i}") for i in range(nh)]
            ctx["hT"] = ctx["hTs"][0]
            ctx["bhT"] = ctx["bhTs"][0]
            return ctx

        def norm_to_hT(ctx, xsrc, bsrc, g0, cidx, which, psbanks, slot=0, xo=0):
            xn, bxn, ss, bss = (ctx[k] for k in ["xn", "bxn", "ss", "bss"])
            xt, bx = ctx["xt"][xo:xo + 4], ctx["bx"][xo:xo + 4]
            hT, bhT = ctx["hTs"][slot], ctx["bhTs"][slot]
            junk, bjunk = ctx["junk"], ctx["bjunk"]
            for t in range(4):
                q_ld.dma(xt[t][:], xsrc[g0 + t * 128:g0 + (t + 1) * 128, :], reads=[bsrc], writes=[bx[t]])
                act.op(lambda t=t: S.activation(out=junk[:], in_=xt[t][:], func=AF.Square, accum_out=ss[:, t:t + 1]),
                       reads=[bx[t]], writes=[bjunk, bss[t]])
                act_pow(ss[:, 4 + t:5 + t], ss[:, t:t + 1], -0.5, 1.0 / D, CB_EPS, [bss[t]], [bss[t]])
                dve.op(lambda t=t: V.tensor_scalar(out=xn[t][:], in0=xt[t][:], scalar1=ss[:, 4 + t:5 + t], scalar2=None,
                                                   op0=ALU.mult),
                       reads=[bx[t], bss[t]], writes=[bxn[t]])
            ai, bi = (0, 1) if which == 0 else (2, 3)
            for j in range(8):
                pb = psbanks.next()
                pst = PSB[pb][:].bitcast(BF16)
                fns = [(lambda t=t, j=j, pst=pst: T.transpose(pst[:, t * 128:(t + 1) * 128],
                                                              xn[t][:, j * 128:(j + 1) * 128], ident_b[:]))
                       for t in range(4)]
                pe.op(fns, reads=bxn + [b_const], writes=[PSBUF[pb]])
                c = cidx
                if j % 2 == 0:
                    act.op(lambda j=j, pst=pst, c=c: S.activation(out=hT[:, j, :], in_=pst[:, 0:512], func=AF.Identity,
                                                                  scale=mods[:, c, ai, j:j + 1],
                                                                  bias=mods[:, c, bi, j:j + 1]),
                           reads=[PSBUF[pb], b_mods], writes=[bhT])
                else:
                    dve.op(lambda j=j, pst=pst, c=c: V.tensor_scalar(out=hT[:, j, :], in0=pst[:, 0:512],
                                                                     scalar1=mods[:, c, ai, j:j + 1],
                                                                     scalar2=mods[:, c, bi, j:j + 1], op0=ALU.mult,
                                                                     op1=ALU.add),
                           reads=[PSBUF[pb], b_mods], writes=[bhT])

        cast_rr = [0]

        def load_w(es, name, src, kchunks, ncols, key, l, krows=128):
            wt = sbt(es, name, [krows, kchunks, ncols], BF16)
            bw = Buf(name)
            CW = 2048
            with contextlib.ExitStack() as ses:
                stg = [sbt(ses, f"stg{i}", [128, CW], F32) for i in range(3)]
                bstg = [Buf(f"stg{i}") for i in range(3)]
                k = 0
                for kc in range(kchunks):
                    for c0 in range(0, ncols, CW):
                        cw = min(CW, ncols - c0)
                        i = k % 3
                        k += 1
                        q_ld.dma(stg[i][0:krows, 0:cw], src[kc * krows:(kc + 1) * krows, c0:c0 + cw], writes=[bstg[i]])
                        e = (0, 1, 0, 1, 2)[cast_rr[0] % 5]
                        cast_rr[0] += 1
                        if e == 0:
                            act.op(lambda i=i, kc=kc, c0=c0, cw=cw: S.copy(out=wt[:, kc, c0:c0 + cw], in_=stg[i][0:krows, 0:cw]),
                                   reads=[bstg[i]], writes=[bw])
                        elif e == 1:
                            dve.op(lambda i=i, kc=kc, c0=c0, cw=cw: V.tensor_copy(out=wt[:, kc, c0:c0 + cw], in_=stg[i][0:krows, 0:cw]),
                                   reads=[bstg[i]], writes=[bw])
                        else:
                            pool.op(lambda i=i, kc=kc, c0=c0, cw=cw: G_.tensor_copy(out=wt[:, kc, c0:c0 + cw], in_=stg[i][0:krows, 0:cw]),
                                    reads=[bstg[i]], writes=[bw])
                fw.barrier()
            return wt, bw

        def phase_a(l, xsrc, bsrc):
            with contextlib.ExitStack() as es:
                WA, bWA = load_w(es, "WA", w_a[l], 8, NA, "wa", l)
                WUQ, bWUQ = load_w(es, "WUQ", w_uq[l], 3, 1536, "wuq", l)
                WUKV, bWUKV = load_w(es, "WUKV", w_ukv[l], 2, 1024, "wukv", l)
                ctx = make_norm_ctx(es, nh=2)
                hT, bhT = ctx["hT"], ctx["bhT"]
                normed = [-1]
                rt = sbt(es, "rt", [128, 8, 512], F32)
                brt = Buf("rt")
                qg = sbt(es, "qg", [128, 3], F32)
                kvg = sbt(es, "kvg", [128, 2], F32)
                kvrow = sbt(es, "kvrow", [128, 256], F32)
                bsm = Buf("small")
                q_ld.dma(qg[:], qng[l], writes=[bsm])
                q_ld.dma(kvg[:], kvng[l], writes=[bsm])
                q_ld.dma(kvrow[:], kvn_row[l].partition_broadcast(128), writes=[bsm])
                NTMP = 4
                tmpf = [sbt(es, f"tmpf{i}", [128, 512], F32) for i in range(NTMP)]
                btmpf = [Buf(f"tmpf{i}") for i in range(NTMP)]
                tring = Ring(list(range(NTMP)))
                NOB = 6
                ob = [sbt(es, f"ob{i}", [128, 512], BF16) for i in range(NOB)]
                bob = [Buf(f"ob{i}") for i in range(NOB)]
                oring = Ring(list(range(NOB)))
                obv = [sbt(es, f"obv{i}", [128, 8, 128], BF16) for i in range(2)]
                bobv = [Buf(f"obv{i}") for i in range(2)]
                obvring = Ring([0, 1])
                for i_ in range(2):
                    dve.op(lambda i_=i_: V.memset(obv[i_][:], 0.0), writes=[bobv[i_]])
                    dve.op(lambda i_=i_: V.memset(obv[i_][:, :, 64:65], 1.0), writes=[bobv[i_]])
                of = [sbt(es, f"of{i}", [128, 512], F32) for i in range(2)]
                bof = [Buf(f"of{i}") for i in range(2)]
                ofring = Ring([0, 1])
                cqg = sbt(es, "cqg", [128, 3, 512], BF16)
                bcqg = Buf("cqg")
                sq = sbt(es, "sq", [128, 3, 512], BF16)
                bsq = Buf("sq")
                rstd_q = sbt(es, "rstd_q", [128, 512], F32)
                b_rq_ = Buf("rstd_q")
                rstd_k = sbt(es, "rstd_k", [128, 512], F32)
                b_rk_ = Buf("rstd_k")
                ckvn = sbt(es, "ckvn", [128, 2, 512], BF16)
                bckvn = Buf("ckvn")
                cqf = sbt(es, "cqf", [128, 3, 512], F32)
                bcqf = Buf("cqf")
                sst = sbt(es, "sst", [128, 4], F32)
                bsst = Buf("sst")
                psr = Ring([0, 1, 2, 3, 4, 5])
                pstr = Ring([6, 7])

                def proj_fm(pb, col0, M, G=512):
                    ps_ = PSB[pb]
                    fns = [(lambda kc=kc: T.matmul(ps_[0:M, 0:G], lhsT=WA[:, kc, col0:col0 + M], rhs=hT[:, kc, 0:G],
                                                   start=(kc == 0), stop=(kc == 7))) for kc in range(8)]
                    pe.op(fns, reads=[bWA, bhT], writes=[PSBUF[pb]])

                def store(dst, src_ap, bsrc_, bdst):
                    q_st.dma(dst, src_ap, reads=[bsrc_], writes=[bdst])

                rope_rr = [0]

                def evac_rope(pbx, pbs, M, ci, si, dst, bdst, rstd=None, dsts=None):
                    i1, i2 = tring.next(), tring.next()
                    t1, t2 = tmpf[i1], tmpf[i2]
                    dve.op(lambda: V.tensor_tensor(out=t1[0:M, :], in0=PSB[pbx][0:M, :], in1=rt[0:M, ci, :],
                                                   op=ALU.mult), reads=[PSBUF[pbx], brt], writes=[btmpf[i1]])
                    dve.op(lambda: V.tensor_tensor(out=t2[0:M, :], in0=PSB[pbs][0:M, :], in1=rt[0:M, si, :],
                                                   op=ALU.mult), reads=[PSBUF[pbs], brt], writes=[btmpf[i2]])
                    oi = oring.next()
                    o = ob[oi]
                    if rstd is None:
                        rope_rr[0] += 1
                        if rope_rr[0] % 3 == 0:
                            dve.op(lambda: V.tensor_tensor(out=o[0:M, :], in0=t1[0:M, :], in1=t2[0:M, :], op=ALU.add),
                                   reads=[btmpf[i1], btmpf[i2]], writes=[bob[oi]])
                        else:
                            pool.op(lambda: G_.tensor_tensor(out=o[0:M, :], in0=t1[0:M, :], in1=t2[0:M, :], op=ALU.add),
                                    reads=[btmpf[i1], btmpf[i2]], writes=[bob[oi]])
                    else:
                        pool.op(lambda: G_.tensor_tensor(out=t1[0:M, :], in0=t1[0:M, :], in1=t2[0:M, :], op=ALU.add),
                                reads=[btmpf[i1], btmpf[i2]], writes=[btmpf[i1]])
                        dve.op(lambda: V.tensor_tensor(out=o[0:M, :], in0=t1[0:M, :], in1=rstd[0:M, :], op=ALU.mult),
                               reads=[btmpf[i1], b_rq_], writes=[bob[oi]])
                    if dsts is not None:
                        for d_ in dsts:
                            store(d_, o[0:M, :], bob[oi], bdst)
                    else:
                        store(dst, o[0:M, :], bob[oi], bdst)

                def evac_plain(pbx, M, dst, bdst, scale=1.0, use_act=True, G=512, v3=False, dsts=None):
                    oi = oring.next()
                    o = ob[oi]
                    if use_act:
                        act.op(lambda: S.activation(out=o[0:M, 0:G], in_=PSB[pbx][0:M, 0:G], func=AF.Copy, scale=scale),
                               reads=[PSBUF[pbx]], writes=[bob[oi]])
                    else:
                        dve.op(lambda: V.tensor_scalar(out=o[0:M, 0:G], in0=PSB[pbx][0:M, 0:G], scalar1=scale,
                                                       scalar2=None, op0=ALU.mult),
                               reads=[PSBUF[pbx]], writes=[bob[oi]])
                    if v3:
                        store(dst, o[0:M, 0:G].rearrange("p (h e) -> p h e", e=64), bob[oi], bdst)
                    elif dsts is not None:
                        for d_ in dsts:
                            store(d_, o[0:M, 0:G], bob[oi], bdst)
                    else:
                        store(dst, o[0:M, 0:G], bob[oi], bdst)

                def fm_chunk(col, cols, M, dst, bdst, rope, ci=0, si=1, scale=1.0):
                    pbx = psr.next()
                    proj_fm(pbx, col, M)
                    if rope:
                        pbs = psr.next()
                        proj_fm(pbs, cols, M)
                        evac_rope(pbx, pbs, M, ci, si, dst, bdst)
                    else:
                        evac_plain(pbx, M, dst, bdst, scale=scale)

                def rms_fm(src_t, bsrc_t, nchunks, nfeat, rstd_t, brstd):
                    for i in range(nchunks):
                        act.op(lambda i=i: S.activation(out=sq[:, i, :], in_=src_t[:, i, :], func=AF.Square),
                               reads=[bsrc_t], writes=[bsq])
                    _ckpt("sq")
                    pb = psr.next()
                    fns = [(lambda i=i: T.matmul(PSB[pb][:], lhsT=ones_b[:], rhs=sq[:, i, :], start=(i == 0),
                                                 stop=(i == nchunks - 1))) for i in range(nchunks)]
                    pe.op(fns, reads=[bsq, b_const], writes=[PSBUF[pb]])
                    _ckpt("onesmm")
                    act_pow(rstd_t[:], PSB[pb][:], -0.5, 1.0 / nfeat, CB_EPS, [PSBUF[pb]], [brstd])

                def mla_from_ckvn(k0):
                    for h in range(8):
                        pb = psr.next()
                        fns = [(lambda kc=kc, h=h, pb=pb: T.matmul(PSB[pb][0:64, :], lhsT=WUKV[:, kc, h * 64:(h + 1) * 64],
                                                                   rhs=ckvn[:, kc, :], start=(kc == 0), stop=(kc == 1)))
                               for kc in range(2)]
                        pe.op(fns, reads=[bWUKV, bckvn], writes=[PSBUF[pb]])
                        evac_plain(pb, 64, s_km[h, 0:64, k0:k0 + 512], B["km"], use_act=(h % 2 == 0))
                    for t in range(4):
                        pb = psr.next()
                        fns = [(lambda kc=kc, t=t, pb=pb: T.matmul(PSB[pb][:], lhsT=ckvn[:, kc, t * 128:(t + 1) * 128],
                                                                   rhs=WUKV[:, kc, 512:1024], start=(kc == 0),
                                                                   stop=(kc == 1))) for kc in range(2)]
                        pe.op(fns, reads=[bWUKV, bckvn], writes=[PSBUF[pb]])
                        vi = obvring.next()
                        if t % 2 == 1:
                            act.op(lambda vi=vi, pb=pb: S.copy(out=obv[vi][:, :, 0:64],
                                                               in_=PSB[pb][:].rearrange("p (h e) -> p h e", e=64)),
                                   reads=[PSBUF[pb]], writes=[bobv[vi]])
                        else:
                            dve.op(lambda vi=vi, pb=pb: V.tensor_copy(out=obv[vi][:, :, 0:64],
                                                                      in_=PSB[pb][:].rearrange("p (h e) -> p h e", e=64)),
                                   reads=[PSBUF[pb]], writes=[bobv[vi]])
                        store(s_vm[k0 + t * 128:k0 + (t + 1) * 128, :, :], obv[vi][:], bobv[vi], B["vm"])

                def cache_group():
                    ck = [sbt(es, f"ck{i}", [128, 512], F32) for i in range(2)]
                    bck = [Buf("ck0"), Buf("ck1")]
                    ckring = Ring([0, 1])
                    for t in range(4):
                        i = ckring.next()
                        q_ld.dma(ck[i][:], c_dv[l, t * 128:(t + 1) * 128, :], writes=[bck[i]])
                        oi = oring.next()
                        act.op(lambda oi=oi, i=i: S.copy(out=ob[oi][:], in_=ck[i][:]), reads=[bck[i]], writes=[bob[oi]])
                        store(s_vd[t * 128:(t + 1) * 128, :], ob[oi][:], bob[oi], B["vd"])
                    for t in range(4):
                        i = ckring.next()
                        q_ld.dma(ck[i][:], c_dk[l, t * 128:(t + 1) * 128, :], writes=[bck[i]])
                        pb = psr.next()
                        fns = [(lambda h=h, i=i, pb=pb: T.transpose(PSB[pb][:, h * 128:(h + 1) * 128],
                                                                    ck[i][:, h * 128:(h + 1) * 128], ident_f[:]))
                               for h in range(4)]
                        pe.op(fns, reads=[bck[i], b_const], writes=[PSBUF[pb]])
                        oi = oring.next()
                        act.op(lambda oi=oi, pb=pb: S.copy(out=ob[oi][:], in_=PSB[pb][:]), reads=[PSBUF[pb]],
                               writes=[bob[oi]])
                        for h in range(4):
                            store(s_kd[h, :, t * 128:(t + 1) * 128], ob[oi][:, h * 128:(h + 1) * 128], bob[oi], B["kd"])
                    pbk = psr.next()
                    for t in range(4):
                        i = ckring.next()
                        q_ld.dma(ck[i][:, 0:256], c_ckv[l, t * 128:(t + 1) * 128, :], writes=[bck[i]])
                        q_ld.dma(ck[i][:, 256:288], c_kpe[l, t * 128:(t + 1) * 128, :], writes=[bck[i]])
                        pb = psr.next()
                        fns = [(lambda kc=kc, i=i, pb=pb: T.transpose(PSB[pb][:, kc * 128:(kc + 1) * 128],
                                                                      ck[i][:, kc * 128:(kc + 1) * 128], ident_f[:]))
                               for kc in range(2)]
                        pe.op(fns, reads=[bck[i], b_const], writes=[PSBUF[pb]])
                        for kc in range(2):
                            dve.op(lambda kc=kc, t=t, pb=pb: V.tensor_copy(out=ckvn[:, kc, t * 128:(t + 1) * 128],
                                                                           in_=PSB[pb][:, kc * 128:(kc + 1) * 128]),
                                   reads=[PSBUF[pb]], writes=[bckvn])
                        pe.op(lambda t=t, i=i: T.transpose(PSB[pbk][0:32, t * 128:(t + 1) * 128], ck[i][:, 256:288],
                                                           ident_f[:]), reads=[bck[i], b_const], writes=[PSBUF[pbk]])
                    evac_plain(pbk, 32, s_kpe[:, 0:LCTX], B["kpe"])
                    mla_from_ckvn(0)

                def token_group(g0):
                    is_sample = g0 < NS
                    cidx = 0 if is_sample else 1
                    k0 = LCTX + g0
                    rope = is_sample
                    nonlocal hT, bhT
                    slot = (g0 // 512) % 2
                    if normed[0] != g0:
                        norm_to_hT(ctx, xsrc, bsrc, g0, cidx, 0, pstr, slot=slot)
                    hT, bhT = ctx["hTs"][slot], ctx["bhTs"][slot]
                    _ckpt("norm")
                    if rope:
                        for i in range(8):
                            q_ld.dma(rt[:, i, :], rope_t[i, :, g0:g0 + 512], writes=[brt])
                    for h in range(4):
                        fm_chunk(A_DQ + h * 128, A_DQS + h * 128, 128, s_qd[h, :, g0:g0 + 512], B["qd"], rope, 0, 1)
                    for h in range(4):
                        fm_chunk(A_DK + h * 128, A_DKS + h * 128, 128, s_kd[h, :, k0:k0 + 512], B["kd"], rope, 0, 1)
                    _ckpt("dqk")
                    if g0 + 512 < NT:
                        norm_to_hT(ctx, xsrc, bsrc, g0 + 512, 0 if g0 + 512 < NS else 1, 0, pstr, slot=1 - slot)
                        normed[0] = g0 + 512
                    for h in range(4):
                        fm_chunk(A_RQ2 + h * 128, A_RQ2S + h * 128, 128, s_rq[h, :, g0:g0 + 512], B["rq"], rope, 0, 1)
                    for h in range(4):
                        fm_chunk(A_RK + h * 64, A_RKS + h * 64, 64, s_rk[h, :, g0:g0 + 512], B["rk"], rope, 2, 3,
                                 scale=0.125)
                    _ckpt("rqk")
                    if rope:
                        pbx, pbs = psr.next(), psr.next()
                        proj_fm(pbx, A_KPE, 32)
                        proj_fm(pbs, A_KPES, 32)
                        evac_rope(pbx, pbs, 32, 6, 7, s_kpe[:, k0:k0 + 512], B["kpe"])
                    else:
                        pbx = psr.next()
                        proj_fm(pbx, A_KPE, 32)
                        evac_plain(pbx, 32, s_kpe[:, k0:k0 + 512], B["kpe"])
                    _ckpt("kpe")
                    for i in range(3):
                        pb = psr.next()
                        proj_fm(pb, A_CQ + i * 128, 128)
                        act.op(lambda i=i, pb=pb: S.copy(out=cqf[:, i, :], in_=PSB[pb][:]), reads=[PSBUF[pb]], writes=[bcqf])
                        dve.op(lambda i=i: V.tensor_scalar(out=cqg[:, i, :], in0=cqf[:, i, :], scalar1=qg[:, i:i + 1],
                                                           scalar2=None, op0=ALU.mult),
                               reads=[bcqf, bsm], writes=[bcqg])
                    _ckpt("cq")
                    rms_fm(cqf, bcqf, 3, 384, rstd_q, b_rq_)
                    _ckpt("rms")
                    for h in range(8):
                        if h == 1:
                            _ckpt("uq1e")
                        pbx = psr.next()
                        fns = [(lambda kc=kc, h=h, pbx=pbx: T.matmul(PSB[pbx][0:96, :], lhsT=WUQ[:, kc, h * 96:(h + 1) * 96],
                                                                     rhs=cqg[:, kc, :], start=(kc == 0), stop=(kc == 2)))
                               for kc in range(3)]
                        pe.op(fns, reads=[bWUQ, bcqg], writes=[PSBUF[pbx]])
                        _ckpt("uq1")
                        if rope:
                            pbs = psr.next()
                            fns = [(lambda kc=kc, h=h, pbs=pbs: T.matmul(PSB[pbs][0:96, :],
                                                                         lhsT=WUQ[:, kc, 768 + h * 96:768 + (h + 1) * 96],
                                                                         rhs=cqg[:, kc, :], start=(kc == 0),
                                                                         stop=(kc == 2))) for kc in range(3)]
                            pe.op(fns, reads=[bWUQ, bcqg], writes=[PSBUF[pbs]])
                            evac_rope(pbx, pbs, 96, 4, 5, s_qm[h, 0:96, g0:g0 + 512], B["qm"], rstd=rstd_q)
                        else:
                            oi = oring.next()
                            dve.op(lambda oi=oi, pbx=pbx: V.tensor_tensor(out=ob[oi][0:96, :], in0=PSB[pbx][0:96, :],
                                                                          in1=rstd_q[0:96, :], op=ALU.mult),
                                   reads=[PSBUF[pbx], b_rq_], writes=[bob[oi]])
                            store(s_qm[h, 0:96, g0:g0 + 512], ob[oi][0:96, :], bob[oi], B["qm"])
                    _ckpt("qmla")
                    for i in range(2):
                        pb = psr.next()
                        proj_fm(pb, A_CKV + i * 128, 128)
                        act.op(lambda i=i, pb=pb: S.copy(out=cqf[:, i, :], in_=PSB[pb][:]), reads=[PSBUF[pb]], writes=[bcqf])
                    rms_fm(cqf, bcqf, 2, 256, rstd_k, b_rk_)
                    for i in range(2):
                        dve.op(lambda i=i: V.scalar_tensor_tensor(out=ckvn[:, i, :], in0=cqf[:, i, :], scalar=kvg[:, i:i + 1],
                                                                  in1=rstd_k[:], op0=ALU.mult, op1=ALU.mult),
                               reads=[bcqf, b_rk_, bsm], writes=[bckvn])
                    mla_from_ckvn(k0)
                    _ckpt("ckv")
                    for t in range(4):
                        tk = g0 + t * 128
                        for (col, dst, bdst, is_dv) in [(A_DV, s_vd[LCTX + tk:LCTX + tk + 128, :], B["vd"], True),
                                                        (A_RV, s_rv[tk:tk + 128, :], B["rv"], False)]:
                            pb = psr.next()
                            fns = [(lambda kc=kc, pb=pb, col=col, t=t: T.matmul(PSB[pb][:], lhsT=hT[:, kc, t * 128:(t + 1) * 128],
                                                                                rhs=WA[:, kc, col:col + 512],
                                                                                start=(kc == 0), stop=(kc == 7)))
                                   for kc in range(8)]
                            pe.op(fns, reads=[bWA, bhT], writes=[PSBUF[pb]])
                            evac_plain(pb, 128, dst, bdst, use_act=is_dv)
                            if is_dv and not is_sample:
                                p = (tk - NS) // PS_
                                s0 = (tk - NS) % PS_
                                fi = ofring.next()
                                dve.op(lambda fi=fi, pb=pb: V.tensor_copy(out=of[fi][:], in_=PSB[pb][:]),
                                       reads=[PSBUF[pb]], writes=[bof[fi]])
                                store(o_dv[p, l, s0:s0 + 128, :], of[fi][:], bof[fi], B["out"])
                        if not is_sample:
                            p = (tk - NS) // PS_
                            s0 = (tk - NS) % PS_
                            pb = psr.next()
                            fns = [(lambda kc=kc, pb=pb, t=t: T.matmul(PSB[pb][:], lhsT=hT[:, kc, t * 128:(t + 1) * 128],
                                                                       rhs=WA[:, kc, A_DK:A_DK + 512], start=(kc == 0),
                                                                       stop=(kc == 7))) for kc in range(8)]
                            pe.op(fns, reads=[bWA, bhT], writes=[PSBUF[pb]])
                            fi = ofring.next()
                            act.op(lambda fi=fi, pb=pb: S.copy(out=of[fi][:], in_=PSB[pb][:]), reads=[PSBUF[pb]],
                                   writes=[bof[fi]])
                            store(o_dk[p, l, s0:s0 + 128, :], of[fi][:], bof[fi], B["out"])
                            pb = psr.next()
                            fns = [(lambda kc=kc, pb=pb, t=t: T.matmul(PSB[pb][:, 0:288], lhsT=hT[:, kc, t * 128:(t + 1) * 128],
                                                                       rhs=WA[:, kc, A_CKV:A_CKV + 288], start=(kc == 0),
                                                                       stop=(kc == 7))) for kc in range(8)]
                            pe.op(fns, reads=[bWA, bhT], writes=[PSBUF[pb]])
                            fi = ofring.next()
                            act.op(lambda fi=fi, pb=pb: S.copy(out=of[fi][:, 0:288], in_=PSB[pb][:, 0:288]),
                                   reads=[PSBUF[pb]], writes=[bof[fi]])
                            act.op(lambda fi=fi: S.activation(out=ctx["junk"][:, 0:256], in_=of[fi][:, 0:256],
                                                              func=AF.Square, accum_out=sst[:, 2:3]),
                                   reads=[bof[fi]], writes=[ctx["bjunk"], bsst])
                            act_pow(sst[:, 3:4], sst[:, 2:3], -0.5, 1.0 / 256, CB_EPS, [bsst], [bsst])
                            dve.op(lambda fi=fi: V.scalar_tensor_tensor(out=of[fi][:, 0:256], in0=of[fi][:, 0:256],
                                                                        scalar=sst[:, 3:4], in1=kvrow[:],
                                                                        op0=ALU.mult, op1=ALU.mult),
                                   reads=[bsst, bsm], writes=[bof[fi]])
                            store(o_ckv[p, l, s0:s0 + 128, :], of[fi][:, 0:256], bof[fi], B["out"])
                            store(o_kpe[p, l, s0:s0 + 128, :], of[fi][:, 256:288], bof[fi], B["out"])

                try:
                    _ckpt("load")
                    cache_group()
                    _ckpt("cache")
                    for g in range(NT // 512):
                        token_group(g * 512)
                        _ckpt(f"g{g + 1}")
                except _StopBuild:
                    pass
                fw.barrier()

        def phase_r(l):
            with contextlib.ExitStack() as es:
                rc = sbt(es, "rc", [128, 5, 128], F32)
                rz = sbt(es, "rz", [128, 2], F32)
                dl = sbt(es, "dl", [128, 8], F32)
                lg = sbt(es, "lg", [128, 8], F32)
                lgx = sbt(es, "lgx", [128, 4], F32)
                Dm = sbt(es, "Dm", [128, 4, 128], F32)
                e2 = sbt(es, "e2", [128, 128], F32)
                XI = sbt(es, "XI", [128, 4, 128], F32)
                ZF = sbt(es, "ZF", [128, 4, 2], F32)
                GC = sbt(es, "GC", [128, 4], F32)
                bc = Buf("rconst")
                q_ld.dma(rc[:], ret_c.rearrange("a p f -> p a f"), writes=[bc])
                q_ld.dma(rz[:], ret_z, writes=[bc])
                q_ld.dma(dl[:, 0:4], dec_f[l].partition_broadcast(128), writes=[bc])
                q_ld.dma(dl[:, 4:8], dec_b[l].partition_broadcast(128), writes=[bc])
                act.op(lambda: S.activation(out=lg[:], in_=dl[:], func=AF.Exp, scale=-1.0), reads=[bc], writes=[bc])
                act.op(lambda: S.activation(out=lg[:], in_=lg[:], func=AF.Ln, bias=cb_t[:, 1:2]), reads=[bc], writes=[bc])
                dve.op(lambda: V.tensor_scalar(out=lg[:], in0=lg[:], scalar1=-1.0, scalar2=None, op0=ALU.mult),
                       reads=[bc], writes=[bc])
                dve.op(lambda: V.tensor_copy(out=lgx[0:64, :], in_=lg[0:64, 0:4]), reads=[bc], writes=[bc])
                dve.op(lambda: V.tensor_copy(out=lgx[64:128, :], in_=lg[64:128, 4:8]), reads=[bc], writes=[bc])
                for h in range(4):
                    act.op(lambda h=h: S.activation(out=Dm[:, h, :], in_=rc[:, 0, :], func=AF.Exp, scale=lg[:, h:h + 1]),
                           reads=[bc], writes=[bc])
                    dve.op(lambda h=h: V.tensor_tensor(out=Dm[:, h, :], in0=Dm[:, h, :], in1=rc[:, 2, :], op=ALU.mult),
                           reads=[bc], writes=[bc])
                    act.op(lambda h=h: S.activation(out=e2[:], in_=rc[:, 1, :], func=AF.Exp, scale=lg[:, 4 + h:5 + h]),
                           reads=[bc], writes=[bc])
                    dve.op(lambda h=h: V.tensor_tensor(out=e2[:], in0=e2[:], in1=rc[:, 3, :], op=ALU.mult),
                           reads=[bc], writes=[bc])
                    dve.op(lambda h=h: V.tensor_tensor(out=Dm[:, h, :], in0=Dm[:, h, :], in1=e2[:], op=ALU.add),
                           reads=[bc], writes=[bc])
                    act.op(lambda h=h: S.activation(out=XI[:, h, :], in_=rc[:, 4, :], func=AF.Exp, scale=lgx[:, h:h + 1]),
                           reads=[bc], writes=[bc])
                    act.op(lambda h=h: S.activation(out=ZF[:, h, 0:1], in_=rz[:, 0:1], func=AF.Exp, scale=lg[:, h:h + 1]),
                           reads=[bc], writes=[bc])
                    act.op(lambda h=h: S.activation(out=ZF[:, h, 1:2], in_=rz[:, 1:2], func=AF.Exp,
                                                    scale=lg[:, 4 + h:5 + h]), reads=[bc], writes=[bc])
                act.op(lambda: S.activation(out=GC[:], in_=lgx[:], func=AF.Exp, scale=128.0), reads=[bc], writes=[bc])

                for (tok0, S_, key0_, NK_, nseq) in BLOCKS:
                    is_sample = (nseq == 1)
                    n = S_ // 128
                    nall = n * nseq
                    with contextlib.ExitStack() as es2:
                        RQ = sbt(es2, "RQ", [128, 4, S_ * nseq], BF16)
                        RK = sbt(es2, "RK", [64, 4, S_ * nseq], BF16)
                        RV = sbt(es2, "RV", [128, nall, 512], BF16)
                        KZ = sbt(es2, "KZ", [128, nall, 4, 128], BF16)
                        SALL = sbt(es2, "SALL", [128, nall, 4, 128], BF16)
                        ST = sbt(es2, "ST", [128, 4, 128], F32)
                        bin_ = Buf("rin"); bKZ = Buf("KZ"); bSALLf = Buf("SALLf"); bSALLb = Buf("SALLb")
                        bSTf = Buf("STf"); bSTb = Buf("STb")
                        for h in range(4):
                            q_ld.dma(RQ[:, h, :], s_rq[h, :, tok0:tok0 + S_ * nseq], reads=[B["rq"]], writes=[bin_])
                            q_ld.dma(RK[:, h, :], s_rk[h, :, tok0:tok0 + S_ * nseq], reads=[B["rk"]], writes=[bin_])
                        for i in range(0, nall, 8):
                            m = min(8, nall - i)
                            q_ld.dma(RV[:, i:i + m, :],
                                     s_rv[tok0 + i * 128:tok0 + (i + m) * 128, :].rearrange("(n p) c -> p n c", p=128),
                                     reads=[B["rv"]], writes=[bin_])
                        psr = Ring([0, 1, 2, 3])
                        for i in range(nall):
                            pb = psr.next()
                            pst = PSB[pb][:].bitcast(BF16)
                            fns = [(lambda h=h, i=i, pst=pst: T.transpose(pst[:, h * 64:(h + 1) * 64],
                                                                          RK[:, h, i * 128:(i + 1) * 128],
                                                                          ident_b[0:64, 0:64])) for h in range(4)]
                            pe.op(fns, reads=[bin_, b_const], writes=[PSBUF[pb]])
                            for h in range(4):
                                for fb in range(2):
                                    dve.op(lambda h=h, fb=fb, i=i, pst=pst: V.tensor_scalar(
                                        out=KZ[:, i, h, fb * 64:(fb + 1) * 64], in0=pst[:, h * 64:(h + 1) * 64],
                                        scalar1=ZF[:, h, fb:fb + 1], scalar2=None, op0=ALU.mult),
                                           reads=[PSBUF[pb], bc], writes=[bKZ])

                        def u_chunk(i):
                            pb = psr.next()
                            fns = [(lambda h=h, i=i, pb=pb: T.matmul(PSB[pb][:, h * 128:(h + 1) * 128], lhsT=KZ[:, i, h, :],
                                                                     rhs=RV[:, i, h * 128:(h + 1) * 128], start=True,
                                                                     stop=True)) for h in range(4)]
                            pe.op(fns, reads=[bKZ, bin_], writes=[PSBUF[pb]])
                            return pb

                        def scan_step(i, lo, hi, bST, bSALLx):
                            pb = u_chunk(i)
                            act.op(lambda: S.copy(out=SALL[lo:hi, i, :, :], in_=ST[lo:hi, :, :]), reads=[bST], writes=[bSALLx])
                            for h in range(4):
                                dve.op(lambda h=h: V.scalar_tensor_tensor(
                                    out=ST[lo:hi, h, :], in0=ST[lo:hi, h, :], scalar=GC[lo:hi, h:h + 1],
                                    in1=PSB[pb][lo:hi, h * 128:(h + 1) * 128], op0=ALU.mult, op1=ALU.add),
                                       reads=[PSBUF[pb], bc, bST], writes=[bST])

                        for si in range(nseq):
                            co = si * n
                            if is_sample:
                                q_ld.dma(ST[0:64, :, :], st_f[l].rearrange("h d e -> d h e"), writes=[bSTf])
                                q_ld.dma(ST[64:128, :, :], st_b[l].rearrange("h d e -> d h e"), writes=[bSTb])
                            else:
                                dve.op(lambda: V.memset(ST[0:64, :, :], 0.0), writes=[bSTf])
                                dve.op(lambda: V.memset(ST[64:128, :, :], 0.0), writes=[bSTb])
                            for i in range(n):
                                scan_step(co + i, 0, 64, bSTf, bSALLf)
                                scan_step(co + n - 1 - i, 64, 128, bSTb, bSALLb)
                            if not is_sample:
                                p = (tok0 - NS) // PS_ + si
                                q_st.dma(o_rf[p, l].rearrange("h d e -> d h e"), ST[0:64, :, :], reads=[bSTf], writes=[B["out"]])
                                q_st.dma(o_rb[p, l].rearrange("h d e -> d h e"), ST[64:128, :, :], reads=[bSTb], writes=[B["out"]])
                        Mt = [sbt(es2, f"Mt{i}", [128, 4, 128], BF16) for i in range(2)]
                        bMt = [Buf("Mt0"), Buf("Mt1")]
                        QX = [sbt(es2, f"QX{i}", [128, 4, 128], BF16) for i in range(2)]
                        bQX = [Buf("QX0"), Buf("QX1")]
                        stt = [sbt(es2, f"stt{i}", [128, 4, 6], F32) for i in range(2)]
                        mv = [sbt(es2, f"mv{i}", [128, 4, 2], F32) for i in range(2)]
                        bstt = [Buf("stt0"), Buf("stt1")]
                        orn = [sbt(es2, f"orn{i}", [128, 512], BF16) for i in range(2)]
                        born = [Buf("orn0"), Buf("orn1")]
                        orT = [sbt(es2, f"orT{i}", [128, 512], BF16) for i in range(2)]
                        borT = [Buf("orT0"), Buf("orT1")]
                        ps2 = Ring([4, 5])
                        ps3 = Ring([6, 7])
                        pbo_of = {}

                        def stage_a(i):
                            k = i % 2
                            pbs = psr.next()
                            fns = [(lambda h=h: T.matmul(PSB[pbs][:, h * 128:(h + 1) * 128],
                                                         lhsT=RK[0:64, h, i * 128:(i + 1) * 128],
                                                         rhs=RQ[0:64, h, i * 128:(i + 1) * 128], start=True,
                                                         stop=True)) for h in range(4)]
                            pe.op(fns, reads=[bin_], writes=[PSBUF[pbs]])
                            dve.op(lambda: V.tensor_tensor(out=Mt[k][:].rearrange("p h f -> p (h f)"), in0=PSB[pbs][:],
                                                           in1=Dm[:].rearrange("p h f -> p (h f)"), op=ALU.mult),
                                   reads=[PSBUF[pbs], bc], writes=[bMt[k]])
                            pool.op(lambda: G_.tensor_tensor(out=QX[k][:], in0=RQ[:, :, i * 128:(i + 1) * 128],
                                                             in1=XI[:], op=ALU.mult),
                                    reads=[bin_, bc], writes=[bQX[k]])

                        def stage_b(i):
                            k = i % 2
                            pbo = ps2.next()
                            pbo_of[i] = pbo
                            fns = []
                            for h in range(4):
                                fns.append(lambda h=h: T.matmul(PSB[pbo][:, h * 128:(h + 1) * 128], lhsT=Mt[k][:, h, :],
                                                                rhs=RV[:, i, h * 128:(h + 1) * 128], start=True, stop=False))
                                fns.append(lambda h=h: T.matmul(PSB[pbo][:, h * 128:(h + 1) * 128], lhsT=QX[k][:, h, :],
                                                                rhs=SALL[:, i, h, :], start=False, stop=True))
                            pe.op(fns, reads=[bMt[k], bQX[k], bin_, bSALLf, bSALLb], writes=[PSBUF[pbo]])
                            for h in range(4):
                                dve.op(lambda h=h: V.bn_stats(out=stt[k][:, h, :], in_=PSB[pbo][:, h * 128:(h + 1) * 128]),
                                       reads=[PSBUF[pbo]], writes=[bstt[k]])
                            for h in range(4):
                                dve.op(lambda h=h: V.bn_aggr(out=mv[k][:, h, :], in_=stt[k][:, h, :]),
                                       reads=[bstt[k]], writes=[bstt[k]])
                            act_pow(mv[k][:, :, 1], mv[k][:, :, 1], -0.5, 1.0, CB_EPS, [bstt[k]], [bstt[k]])
                            for h in range(4):
                                dve.op(lambda h=h: V.tensor_scalar(out=orn[k][:, h * 128:(h + 1) * 128],
                                                                   in0=PSB[pbo][:, h * 128:(h + 1) * 128],
                                                                   scalar1=mv[k][:, h, 0:1], scalar2=mv[k][:, h, 1:2],
                                                                   op0=ALU.subtract, op1=ALU.mult),
                                       reads=[PSBUF[pbo], bstt[k]], writes=[born[k]])

                        def stage_c(i):
                            k = i % 2
                            pbt = ps3.next()
                            pst = PSB[pbt][:].bitcast(BF16)
                            fns = [(lambda h=h: T.transpose(pst[:, h * 128:(h + 1) * 128], orn[k][:, h * 128:(h + 1) * 128],
                                                            ident_b[:])) for h in range(4)]
                            pe.op(fns, reads=[born[k], b_const], writes=[PSBUF[pbt]])
                            act.op(lambda: S.copy(out=orT[k][:], in_=pst[:, 0:512]), reads=[PSBUF[pbt]], writes=[borT[k]])
                            for h in range(4):
                                q_st.dma(s_oret[h, :, tok0 + i * 128:tok0 + (i + 1) * 128],
                                         orT[k][:, h * 128:(h + 1) * 128], reads=[borT[k]], writes=[B["oret"]])

                        stage_a(0)
                        for i in range(nall):
                            if i + 1 < nall:
                                stage_a(i + 1)
                            stage_b(i)
                            if i >= 1:
                                stage_c(i - 1)
                        stage_c(nall - 1)
                        fw.barrier()

        def phase_d(l):
            lam_init = 0.8 - 0.6 * math.exp(-0.3 * l)
            with contextlib.ExitStack() as es:
                dlt = sbt(es, "dlt", [128, 256], F32)
                lam = sbt(es, "lam", [128, 8], F32)
                bl = Buf("lam")
                q_ld.dma(dlt[:], dlam[l].partition_broadcast(128), writes=[bl])
                dve.op(lambda: V.tensor_tensor(out=dlt[:, 0:64], in0=dlt[:, 0:64], in1=dlt[:, 64:128], op=ALU.mult),
                       reads=[bl], writes=[bl])
                dve.op(lambda: V.tensor_tensor(out=dlt[:, 128:192], in0=dlt[:, 128:192], in1=dlt[:, 192:256], op=ALU.mult),
                       reads=[bl], writes=[bl])
                dve.op(lambda: V.tensor_reduce(out=lam[:, 0:1], in_=dlt[:, 0:64], axis=mybir.AxisListType.X, op=ALU.add), reads=[bl],
                       writes=[bl])
                dve.op(lambda: V.tensor_reduce(out=lam[:, 1:2], in_=dlt[:, 128:192], axis=mybir.AxisListType.X, op=ALU.add), reads=[bl],
                       writes=[bl])
                act.op(lambda: S.activation(out=lam[:, 2:4], in_=lam[:, 0:2], func=AF.Exp), reads=[bl], writes=[bl])
                dve.op(lambda: V.tensor_tensor(out=lam[:, 4:5], in0=lam[:, 3:4], in1=lam[:, 2:3], op=ALU.subtract),
                       reads=[bl], writes=[bl])
                dve.op(lambda: V.tensor_scalar(out=lam[:, 5:6], in0=lam[:, 4:5], scalar1=-lam_init, scalar2=None,
                                               op0=ALU.add), reads=[bl], writes=[bl])
                for (tok0, S_, key0, NK, nseq) in BLOCKS:
                    G = min(S_, 512)
                    nkt = NK // 128
                    with contextlib.ExitStack() as es2:
                        KD = sbt(es2, "KD", [128, 4, NK * nseq], BF16)
                        VD = sbt(es2, "VD", [128, nkt * nseq, 512], BF16)
                        bkv = Buf("kv")
                        for h in range(4):
                            q_ld.dma(KD[:, h, :], s_kd[h, :, key0:key0 + NK * nseq], reads=[B["kd"]], writes=[bkv])
                        for i in range(0, nkt * nseq, 6):
                            m = min(6, nkt * nseq - i)
                            q_ld.dma(VD[:, i:i + m, :],
                                     s_vd[key0 + i * 128:key0 + (i + m) * 128, :].rearrange("(n p) c -> p n c", p=128),
                                     reads=[B["vd"]], writes=[bkv])
                        QD = [sbt(es2, f"QD{i}", [128, 4, 2, G], BF16) for i in range(2)]
                        bQD = [Buf("QD0"), Buf("QD1")]
                        for i_ in range(2):
                            pool.op(lambda i_=i_: G_.memset(QD[i_][:], 0.0), writes=[bQD[i_]])
                        PT = [sbt(es2, f"PT{i}", [128, G], BF16) for i in range(6)]
                        bPT = [Buf(f"PT{i}") for i in range(6)]
                        ptr = Ring([0, 1, 2, 3, 4, 5])
                        rr = [sbt(es2, f"rr{i}", [128, G], F32) for i in range(2)]
                        za = [sbt(es2, f"za{i}", [128, G], F32) for i in range(2)]
                        bza = [Buf(f"za{i}") for i in range(2)]
                        oc = [[sbt(es2, f"oc{p_}{c_}", [128, G], F32) for c_ in range(2)] for p_ in range(2)]
                        zc = [sbt(es2, f"zc{p_}", [128, G], F32) for p_ in range(2)]
                        boc = [Buf("oc0"), Buf("oc1")]
                        tt = [sbt(es2, f"tt{i}", [128, G], F32) for i in range(2)]
                        osq = sbt(es2, "osq", [128, G], F32)
                        oo = sbt(es2, "oo", [128, G], F32)
                        yb = [sbt(es2, f"yb{i}", [128, G], BF16) for i in range(2)]
                        byb = [Buf("yb0"), Buf("yb1")]
                        bfin = Buf("fin")
                        psc = Ring([0, 1, 2, 3])
                        for qb_ in range(nseq * (S_ // G)):
                            si, qb = divmod(qb_, S_ // G)
                            q0 = tok0 + si * S_ + qb * G
                            kto = si * nkt
                            qi = qb_ % 2
                            for h in range(4):
                                q_ld.dma(QD[qi][0:64, h, 0, :], s_qd[h, 0:64, q0:q0 + G], reads=[B["qd"]], writes=[bQD[qi]])
                                q_ld.dma(QD[qi][64:128, h, 1, :], s_qd[h, 64:128, q0:q0 + G], reads=[B["qd"]], writes=[bQD[qi]])
                            tiles = [(h, kt, c) for h in range(4) for kt in range(nkt) for c in range(2)]
                            LA = 3
                            pend = {}
                            deferred = []

                            def issue_score(idx, qi=qi):
                                h, kt, c = tiles[idx]
                                pb = psc.next()
                                pe.op(lambda: T.matmul(PSB[pb][:, 0:G], lhsT=KD[:, h, (kto + kt) * 128:(kto + kt + 1) * 128],
                                                       rhs=QD[qi][:, h, c, :], start=True, stop=True),
                                      reads=[bkv, bQD[qi]], writes=[PSBUF[pb]])
                                pi = ptr.next()
                                act.op(lambda: S.activation(out=PT[pi][:], in_=PSB[pb][:, 0:G], func=AF.Exp, scale=0.125),
                                       reads=[PSBUF[pb]], writes=[bPT[pi]])
                                pend[idx] = pi

                            def issue_pv(idx):
                                h, kt, c = tiles[idx]
                                pi = pend.pop(idx)
                                if c == 0:
                                    pe.op([lambda: T.matmul(PSB[4][:, 0:G], lhsT=VD[:, kto + kt, h * 128:(h + 1) * 128], rhs=PT[pi][:],
                                                            start=(kt == 0), stop=(kt == nkt - 1)),
                                           lambda: T.matmul(PSB[6][:, 0:G], lhsT=ones_b[:], rhs=PT[pi][:], start=(kt == 0),
                                                            stop=(kt == nkt - 1))],
                                          reads=[bkv, bPT[pi], b_const], writes=[PSBUF[4], PSBUF[6]])
                                else:
                                    pe.op(lambda: T.matmul(PSB[5][:, 0:G], lhsT=VD[:, kto + kt, h * 128:(h + 1) * 128], rhs=PT[pi][:],
                                                           start=(kt == 0), stop=(kt == nkt - 1)),
                                          reads=[bkv, bPT[pi]], writes=[PSBUF[5]])
                                    zi = h % 2
                                    if kt == 0:
                                        dve.op(lambda: V.tensor_copy(out=za[zi][:], in_=PT[pi][:]), reads=[bPT[pi]], writes=[bza[zi]])
                                    else:
                                        dve.op(lambda: V.tensor_tensor(out=za[zi][:], in0=za[zi][:], in1=PT[pi][:], op=ALU.add),
                                               reads=[bPT[pi], bza[zi]], writes=[bza[zi]])

                            def fin_stage0(h):
                                p_ = h % 2
                                dve.op(lambda: V.tensor_copy(out=oc[p_][0][:], in_=PSB[4][:, 0:G]), reads=[PSBUF[4]], writes=[boc[p_]])
                                dve.op(lambda: V.tensor_copy(out=oc[p_][1][:], in_=PSB[5][:, 0:G]), reads=[PSBUF[5]], writes=[boc[p_]])
                                dve.op(lambda: V.tensor_copy(out=zc[p_][:], in_=PSB[6][:, 0:G]), reads=[PSBUF[6]], writes=[boc[p_]])

                            def fin_stage1(h, q0=q0):
                                p_ = h % 2
                                pe.op(lambda: T.matmul(PSB[7][:, 0:G], lhsT=ones_f[:], rhs=za[p_][:], start=True, stop=True),
                                      reads=[bza[p_], b_const], writes=[PSBUF[7]])
                                act_pow(rr[1][:], PSB[7][:, 0:G], -1.0, 1.0, CB_ZERO, [PSBUF[7]], [bfin])
                                act_pow(rr[0][:], zc[p_][:], -1.0, 1.0, CB_ZERO, [boc[p_]], [bfin])
                                for c in range(2):
                                    dve.op(lambda c=c: V.tensor_tensor(out=tt[c][:], in0=oc[p_][c][:], in1=rr[c][:], op=ALU.mult),
                                           reads=[boc[p_], bfin], writes=[bfin])
                                dve.op(lambda: V.scalar_tensor_tensor(out=oo[:], in0=tt[1][:], scalar=lam[:, 5:6],
                                                                      in1=tt[0][:], op0=ALU.mult, op1=ALU.add),
                                       reads=[bfin, bl], writes=[bfin])
                                dve.op(lambda: V.tensor_tensor(out=osq[:], in0=oo[:], in1=oo[:], op=ALU.mult), reads=[bfin],
                                       writes=[bfin])

                            def fin_stage2(h, q0=q0):
                                pe.op(lambda: T.matmul(PSB[7][:, 0:G], lhsT=ones_f[:], rhs=osq[:], start=True, stop=True),
                                      reads=[bfin, b_const], writes=[PSBUF[7]])
                                act_pow(rr[0][:], PSB[7][:, 0:G], -0.5, 1.0, CB_128EPS, [PSBUF[7]], [bfin])
                                yi = h % 2
                                dve.op(lambda yi=yi: V.scalar_tensor_tensor(out=yb[yi][:], in0=oo[:],
                                                                            scalar=math.sqrt(128.0) * (1.0 - lam_init),
                                                                            in1=rr[0][:], op0=ALU.mult, op1=ALU.mult),
                                       reads=[bfin], writes=[byb[yi]])
                                q_st.dma(s_yd[h, :, q0:q0 + G], yb[yi][:], reads=[byb[yi]], writes=[B["yd"]])

                            nt_ = len(tiles)
                            for idx in range(nt_ + LA):
                                if idx < nt_:
                                    issue_score(idx)
                                if idx >= LA:
                                    j = idx - LA
                                    issue_pv(j)
                                    h, kt, c = tiles[j]
                                    if kt == nkt - 1 and c == 1:
                                        fin_stage0(h)
                                        tph = 2 * nkt
                                        d1 = min(6, tph)
                                        d2 = min(20, tph + d1 - 1)
                                        deferred.append((idx + d1, fin_stage1, h))
                                        deferred.append((idx + d2, fin_stage2, h))
                                while deferred and deferred[0][0] <= idx:
                                    _, fn_, h_ = deferred.pop(0)
                                    fn_(h_)
                                deferred.sort(key=lambda x: x[0])
                            for (_, fn_, h_) in sorted(deferred, key=lambda x: x[0]):
                                fn_(h_)
                        fw.barrier()

        def phase_m(l):
            with contextlib.ExitStack() as es:
                sel = sbt(es, "sel", [65, 64], F32)
                bsel = Buf("sel")
                q_ld.dma(sel[:], sel65, writes=[bsel])
                for (tok0, S_, key0, NK1, nseq) in BLOCKS:
                    G = min(S_, 512)
                    nkt1 = NK1 // 128
                    NK = NK1 * nseq
                    nkt = nkt1 * nseq
                    with contextlib.ExitStack() as es2:
                        KM = sbt(es2, "KM", [128, 8, NK], BF16)
                        VM = sbt(es2, "VM", [128, nkt, 8, 128], BF16)
                        bkv = Buf("kv")
                        for h in range(8):
                            q_ld.dma(KM[0:64, h, :], s_km[h, 0:64, key0:key0 + NK], reads=[B["km"]], writes=[bkv])
                            q_ld.dma(KM[64:96, h, :], s_kpe[:, key0:key0 + NK], reads=[B["kpe"]], writes=[bkv])
                            q_ld.dma(KM[96:128, h, :], s_km[h, 96:128, key0:key0 + NK], reads=[B["km"]], writes=[bkv])
                        for i in range(0, nkt, 6):
                            m_ = min(6, nkt - i)
                            q_ld.dma(VM[:, i:i + m_, :, :].rearrange("p n h e -> p n (h e)"),
                                     s_vm[key0 + i * 128:key0 + (i + m_) * 128, :, :].rearrange("(n p) h e -> p n (h e)", p=128),
                                     reads=[B["vm"]], writes=[bkv])
                        QM = [sbt(es2, f"QM{i}", [128, 8, G], BF16) for i in range(2)]
                        bQM = [Buf("QM0"), Buf("QM1")]
                        PT = [sbt(es2, f"PT{i}", [128, G], BF16) for i in range(6)]
                        bPT = [Buf(f"PT{i}") for i in range(6)]
                        ptr = Ring([0, 1, 2, 3, 4, 5])
                        osb = [sbt(es2, f"osb{i}", [65, G], F32) for i in range(2)]
                        bosb = [Buf("osb0"), Buf("osb1")]
                        yb = [sbt(es2, f"yb{i}", [64, G], BF16) for i in range(2)]
                        byb = [Buf("yb0"), Buf("yb1")]
                        psc = Ring([0, 1, 2, 7])
                        pso = Ring([3, 4])
                        psb_ = Ring([5, 6])
                        nkt_all = nkt
                        nkt = nkt1
                        for qb_ in range(nseq * (S_ // G)):
                            si, qb = divmod(qb_, S_ // G)
                            q0 = tok0 + si * S_ + qb * G
                            kto = si * nkt
                            qi = qb_ % 2
                            for h in range(8):
                                q_ld.dma(QM[qi][:, h, :], s_qm[h, :, q0:q0 + G], reads=[B["qm"]], writes=[bQM[qi]])
                            tiles = [(h, kt) for h in range(8) for kt in range(nkt)]
                            LA = 3
                            pend = {}
                            pos = {}

                            def issue_score(idx, qi=qi):
                                h, kt = tiles[idx]
                                pb = psc.next()
                                pe.op(lambda: T.matmul(PSB[pb][:, 0:G], lhsT=KM[:, h, (kto + kt) * 128:(kto + kt + 1) * 128], rhs=QM[qi][:, h, :],
                                                       start=True, stop=True), reads=[bkv, bQM[qi]], writes=[PSBUF[pb]])
                                pi = ptr.next()
                                act.op(lambda: S.activation(out=PT[pi][:], in_=PSB[pb][:, 0:G], func=AF.Exp, scale=MLA_SCALE),
                                       reads=[PSBUF[pb]], writes=[bPT[pi]])
                                pend[idx] = pi

                            def issue_pv(idx):
                                h, kt = tiles[idx]
                                if kt == 0:
                                    pos[h] = pso.next()
                                po = pos[h]
                                pi = pend.pop(idx)
                                pe.op(lambda: T.matmul(PSB[po][:, 0:G], lhsT=VM[:, kto + kt, h, :], rhs=PT[pi][:], start=(kt == 0),
                                                       stop=(kt == nkt - 1)), reads=[bkv, bPT[pi]], writes=[PSBUF[po]])

                            deferred = []

                            def fin_stage1(h, q0=q0):
                                po = pos[h]
                                oi = h % 2
                                dve.op(lambda oi=oi, po=po: V.tensor_copy(out=osb[oi][0:64, :], in_=PSB[po][0:64, 0:G]),
                                       reads=[PSBUF[po]], writes=[bosb[oi]])
                                act_pow(osb[oi][64:65, :], PSB[po][64:65, 0:G], -1.0, 1.0, CB_ZERO, [PSBUF[po]], [bosb[oi]],
                                        p0=64, p1=65)

                            def fin_stage2(h, q0=q0):
                                oi = h % 2
                                pbb = psb_.next()
                                pe.op(lambda oi=oi, pbb=pbb: T.matmul(PSB[pbb][0:64, 0:G], lhsT=sel[:], rhs=osb[oi][:],
                                                                      start=True, stop=True),
                                      reads=[bosb[oi], bsel], writes=[PSBUF[pbb]])
                                dve.op(lambda oi=oi, pbb=pbb: V.tensor_tensor(out=yb[oi][:], in0=osb[oi][0:64, :],
                                                                              in1=PSB[pbb][0:64, 0:G], op=ALU.mult),
                                       reads=[bosb[oi], PSBUF[pbb]], writes=[byb[oi]])
                                q_st.dma(s_ym[h * 64:(h + 1) * 64, q0:q0 + G], yb[oi][:], reads=[byb[oi]],
                                         writes=[B["ym"]])

                            nt_ = len(tiles)
                            for idx in range(nt_ + LA):
                                if idx < nt_:
                                    issue_score(idx)
                                if idx >= LA:
                                    j = idx - LA
                                    issue_pv(j)
                                    h, kt = tiles[j]
                                    if kt == nkt - 1:
                                        fin_stage1(h)
                                        deferred.append((idx + min(6, 2 * nkt - 1), fin_stage2, h))
                                while deferred and deferred[0][0] <= idx:
                                    _, fn_, h_ = deferred.pop(0)
                                    fn_(h_)
                            for (_, fn_, h_) in deferred:
                                fn_(h_)
                        fw.barrier()

        def phase_c1(l, xsrc, bsrc):
            with contextlib.ExitStack() as es:
                WC, bWC = load_w(es, "WC", w_c1[l], 8, NC1, "wc1", l)
                WBR, bWBR = load_w(es, "WBR", w_br[l], 12, D, "wbr", l)
                WO, bWO = load_w(es, "WO", w_o[l], 8, D, "wo", l)
                ctx = make_norm_ctx(es, nx=8, nh=2)
                hT, bhT = ctx["hT"], ctx["bhT"]
                xt, bx = ctx["xt"], ctx["bx"]
                g1b = sbt(es, "g1b", [128, 2, D], F32)
                bg1b = Buf("g1b")
                for c in range(2):
                    q_ld.dma(g1b[:, c, :], s_gb[0, c], reads=[B["gb"]], writes=[bg1b])
                YB = sbt(es, "YB", [128, 12, 512], BF16)
                bYB = Buf("YB")
                bYR = Buf("YR")
                srg = sbt(es, "srg", [128, 512], BF16)
                bsrg = Buf("srg")
                gate = [sbt(es, f"gate{i}", [128, 512], F32) for i in range(3)]
                bgate = [Buf(f"gate{i}") for i in range(3)]
                tm = [sbt(es, f"tm{i}", [128, 512], F32) for i in range(3)]
                btm = [Buf(f"tm{i}") for i in range(3)]
                mrg = sbt(es, "mrg", [128, 8, 512], BF16)
                bmrg = Buf("mrg")
                psr = Ring([0, 1, 2, 3, 4, 5])
                pstr = Ring([6, 7])

                def proj_c(pb, col0):
                    fns = [(lambda kc=kc: T.matmul(PSB[pb][:], lhsT=WC[:, kc, col0:col0 + 128], rhs=hT[:, kc, :],
                                                   start=(kc == 0), stop=(kc == 7))) for kc in range(8)]
                    pe.op(fns, reads=[bWC, bhT], writes=[PSBUF[pb]])

                for g in range(NT // 512):
                    g0 = g * 512
                    cidx = 0 if g0 < NS else 1
                    slot = g % 2
                    xo = 4 * slot
                    if g == 0:
                        norm_to_hT(ctx, xsrc, bsrc, g0, cidx, 0, pstr, slot=slot, xo=xo)
                    hT, bhT = ctx["hTs"][slot], ctx["bhTs"][slot]
                    for h in range(4):
                        q_ld.dma(YB[:, h, :], s_oret[h, :, g0:g0 + 512], reads=[B["oret"]], writes=[bYR])
                        q_ld.dma(YB[:, 4 + h, :], s_yd[h, :, g0:g0 + 512], reads=[B["yd"]], writes=[bYB])
                        q_ld.dma(YB[:, 8 + h, :], s_ym[h * 128:(h + 1) * 128, g0:g0 + 512], reads=[B["ym"]], writes=[bYB])
                    for j in range(4):
                        pb = psr.next()
                        proj_c(pb, j * 128)
                        act.op(lambda pb=pb: S.activation(out=srg[:], in_=PSB[pb][:], func=AF.Silu), reads=[PSBUF[pb]],
                               writes=[bsrg])
                        pool.op(lambda j=j: G_.tensor_tensor(out=YB[:, j, :], in0=YB[:, j, :], in1=srg[:], op=ALU.mult),
                                reads=[bsrg, bYR], writes=[bYR])
                    if g0 + 512 < NT:
                        norm_to_hT(ctx, xsrc, bsrc, g0 + 512, 0 if g0 + 512 < NS else 1, 0, pstr, slot=1 - slot, xo=4 - xo)
                    for oc in range(8):
                        for gi in range(3):
                            pb = psr.next()
                            proj_c(pb, 512 + gi * D + oc * 128)
                            act.op(lambda pb=pb, gi=gi: S.activation(out=gate[gi][:], in_=PSB[pb][:], func=AF.Sigmoid),
                                   reads=[PSBUF[pb]], writes=[bgate[gi]])
                            pb2 = psr.next()
                            fns = [(lambda kc=kc, gi=gi, oc=oc, pb2=pb2: T.matmul(
                                PSB[pb2][:], lhsT=WBR[:, gi * 4 + kc, oc * 128:(oc + 1) * 128], rhs=YB[:, gi * 4 + kc, :],
                                start=(kc == 0), stop=(kc == 3))) for kc in range(4)]
                            pe.op(fns, reads=[bWBR, bYB, bYR], writes=[PSBUF[pb2]])
                            dve.op(lambda gi=gi, pb2=pb2: V.tensor_tensor(out=tm[gi][:], in0=PSB[pb2][:], in1=gate[gi][:],
                                                                          op=ALU.mult),
                                   reads=[PSBUF[pb2], bgate[gi]], writes=[btm[gi]])
                        pool.op(lambda: G_.tensor_tensor(out=tm[0][:], in0=tm[0][:], in1=tm[1][:], op=ALU.add),
                                reads=[btm[0], btm[1]], writes=[btm[0]])
                        pool.op(lambda oc=oc: G_.tensor_tensor(out=mrg[:, oc, :], in0=tm[0][:], in1=tm[2][:], op=ALU.add),
                                reads=[btm[0], btm[2]], writes=[bmrg])
                    for t in range(4):
                        for half in range(2):
                            pb = psr.next()
                            fns = [(lambda kc=kc, t=t, half=half, pb=pb: T.matmul(
                                PSB[pb][:], lhsT=mrg[:, kc, t * 128:(t + 1) * 128], rhs=WO[:, kc, half * 512:(half + 1) * 512],
                                start=(kc == 0), stop=(kc == 7))) for kc in range(8)]
                            pe.op(fns, reads=[bWO, bmrg], writes=[PSBUF[pb]])
                            ti = half
                            dve.op(lambda ti=ti, pb=pb, half=half: V.tensor_tensor(
                                out=tm[ti][:], in0=PSB[pb][:], in1=g1b[:, cidx, half * 512:(half + 1) * 512], op=ALU.mult),
                                   reads=[PSBUF[pb], bg1b], writes=[btm[ti]])
                            (dve if half == 0 else pool).op(lambda ti=ti, t=t, half=half, xo=xo: (V if half == 0 else G_).tensor_tensor(
                                out=xt[xo + t][:, half * 512:(half + 1) * 512], in0=xt[xo + t][:, half * 512:(half + 1) * 512],
                                in1=tm[ti][:], op=ALU.add), reads=[btm[ti], bx[xo + t]], writes=[bx[xo + t]])
                        q_st.dma(s_xmid[g0 + t * 128:g0 + (t + 1) * 128, :], xt[xo + t][:], reads=[bx[xo + t]], writes=[B["xmid"]])
                fw.barrier()

        def phase_c2(l, last):
            with contextlib.ExitStack() as es:
                W1, bW1 = load_w(es, "W1", w_f1[l], 8, 2 * DFF, "wf1", l)
                W2, bW2 = load_w(es, "W2", w_f2[l], 22, D, "wf2", l)
                ctx = make_norm_ctx(es)
                hT, bhT = ctx["hT"], ctx["bhT"]
                xt, bx = ctx["xt"], ctx["bx"]
                g2b = sbt(es, "g2b", [128, D], F32)
                bg2b = Buf("g2b")
                cur_c = [-1]
                bfg = Buf("fg")
                if last:
                    fg = sbt(es, "fg", [128, D], F32)
                    q_ld.dma(fg[:], fin_g.partition_broadcast(128), writes=[bfg])
                uT = sbt(es, "uT", [128, 22, 512], BF16)
                buT = Buf("uT")
                sa = [sbt(es, f"sa{i}", [128, 512], BF16) for i in range(2)]
                bsa = [Buf("sa0"), Buf("sa1")]
                tm = [sbt(es, f"tm{i}", [128, 512], F32) for i in range(2)]
                btm = [Buf("tm0"), Buf("tm1")]
                fs = sbt(es, "fs", [128, 8], F32)
                bfs = Buf("fs")
                psr = Ring([0, 1, 2, 3, 4, 5])
                pstr = Ring([6, 7])
                for g in range(NT // 512):
                    g0 = g * 512
                    cidx = 0 if g0 < NS else 1
                    if cur_c[0] != cidx:
                        cur_c[0] = cidx
                        q_ld.dma(g2b[:], s_gb[1, cidx], reads=[B["gb"]], writes=[bg2b])
                    norm_to_hT(ctx, s_xmid, B["xmid"], g0, cidx, 1, pstr)
                    for j in range(22):
                        pa, pb = psr.next(), psr.next()
                        for (pp, col0) in [(pa, j * 128), (pb, DFF + j * 128)]:
                            fns = [(lambda kc=kc, pp=pp, col0=col0: T.matmul(PSB[pp][:], lhsT=W1[:, kc, col0:col0 + 128],
                                                                            rhs=hT[:, kc, :], start=(kc == 0),
                                                                            stop=(kc == 7))) for kc in range(8)]
                            pe.op(fns, reads=[bW1, bhT], writes=[PSBUF[pp]])
                        si = j % 2
                        act.op(lambda si=si, pa=pa: S.activation(out=sa[si][:], in_=PSB[pa][:], func=AF.Silu),
                               reads=[PSBUF[pa]], writes=[bsa[si]])
                        dve.op(lambda si=si, pb=pb, j=j: V.tensor_tensor(out=uT[:, j, :], in0=PSB[pb][:], in1=sa[si][:],
                                                                         op=ALU.mult),
                               reads=[PSBUF[pb], bsa[si]], writes=[buT])
                    for t in range(4):
                        for half in range(2):
                            pb = psr.next()
                            fns = [(lambda j=j, t=t, half=half, pb=pb: T.matmul(
                                PSB[pb][:], lhsT=uT[:, j, t * 128:(t + 1) * 128], rhs=W2[:, j, half * 512:(half + 1) * 512],
                                start=(j == 0), stop=(j == 21))) for j in range(22)]
                            pe.op(fns, reads=[bW2, buT], writes=[PSBUF[pb]])
                            ti = half
                            dve.op(lambda ti=ti, pb=pb, half=half: V.tensor_tensor(
                                out=tm[ti][:], in0=PSB[pb][:], in1=g2b[:, half * 512:(half + 1) * 512], op=ALU.mult),
                                   reads=[PSBUF[pb], bg2b], writes=[btm[ti]])
                            (dve if half == 0 else pool).op(lambda ti=ti, t=t, half=half: (V if half == 0 else G_).tensor_tensor(
                                out=xt[t][:, half * 512:(half + 1) * 512], in0=xt[t][:, half * 512:(half + 1) * 512],
                                in1=tm[ti][:], op=ALU.add), reads=[btm[ti], bx[t]], writes=[bx[t]])
                        if not last:
                            q_st.dma(s_x1[g0 + t * 128:g0 + (t + 1) * 128, :], xt[t][:], reads=[bx[t]], writes=[B["x1"]])
                        else:
                            junk, bjunk = ctx["junk"], ctx["bjunk"]
                            act.op(lambda t=t: S.activation(out=junk[:], in_=xt[t][:], func=AF.Square,
                                                            accum_out=fs[:, t:t + 1]), reads=[bx[t]],
                                   writes=[bjunk, bfs])
                            act_pow(fs[:, 4 + t:5 + t], fs[:, t:t + 1], -0.5, 1.0 / D, CB_EPS, [bfs], [bfs])
                            dve.op(lambda t=t: V.scalar_tensor_tensor(out=xt[t][:], in0=xt[t][:], scalar=fs[:, 4 + t:5 + t],
                                                                      in1=fg[:], op0=ALU.mult, op1=ALU.mult),
                                   reads=[bx[t], bfs, bfg], writes=[bx[t]])
                            q_st.dma(y_out[g0 + t * 128:g0 + (t + 1) * 128, :], xt[t][:], reads=[bx[t]], writes=[B["out"]])
                fw.barrier()

        plist = []
        for l in range(DEPTH):
            plist += [(l, p) for p in ["ada", "A", "R", "D", "M", "C1", "C2"]]
        for (l, p) in plist:
            xsrc, bsrc = (x_in, Buf("x_in")) if l == 0 else (s_x1, B["x1"])
            if p == "ada":
                ada_phase(l)
            elif p == "A":
                phase_a(l, xsrc, bsrc)
            elif p == "R":
                phase_r(l)
            elif p == "D":
                phase_d(l)
            elif p == "M":
                phase_m(l)
            elif p == "C1":
                phase_c1(l, xsrc, bsrc)
            elif p == "C2":
                phase_c2(l, last=(l == DEPTH - 1))
            if stop_after is not None and (l, p) == stop_after:
                break
        fw.barrier()
        nc._fw_stats = (fw.ninst, fw.ndma)
    return nc


def _swap_idx(n):
    h = n // 2
    return list(range(h, n)) + list(range(0, h))


def _perm_a():
    idx = []
    sw64 = _swap_idx(64)
    def sec(off, n):
        return list(range(off, off + n))
    def sec_sw(off, n, blk):
        out = []
        sw = _swap_idx(blk)
        for b in range(n // blk):
            out += [off + b * blk + s for s in sw]
        return out
    idx += sec(O_DQ, 512)
    idx += sec_sw(O_DQ, 512, 64)
    idx += sec(O_DK, 512)
    idx += sec_sw(O_DK, 512, 64)
    for h in range(4):
        idx += sec(O_RQ + h * 64, 64) * 2
    for h in range(4):
        idx += sec_sw(O_RQ + h * 64, 64, 64) * 2
    idx += sec(O_RK, 256)
    idx += sec_sw(O_RK, 256, 64)
    idx += sec(O_CQ, 384)
    idx += sec(O_CKV, 256)
    idx += sec(O_KPE, 32)
    idx += sec_sw(O_KPE, 32, 32)
    idx += sec(O_DV, 512)
    idx += sec(O_RV, 512)
    assert len(idx) == NA
    return np.array(idx)


def _rope_tables():
    t = np.arange(NS)
    row = (t // GRID_W).astype(np.float32)
    col = (t % GRID_W).astype(np.float32)

    def ang_tab(rot_dim):
        nf = rot_dim // 4
        inv = (np.float32(10000.0) ** (-np.arange(nf, dtype=np.float32) / np.float32(nf))).astype(np.float32)
        ang = np.concatenate([row[:, None] * inv, col[:, None] * inv], axis=-1).astype(np.float32)
        return np.cos(ang).astype(np.float32), np.sin(ang).astype(np.float32)

    c64, s64 = ang_tab(64)
    c32, s32 = ang_tab(32)
    C64 = np.zeros((128, NS), np.float32)
    S64 = np.zeros((128, NS), np.float32)
    for p in range(128):
        d = p % 64
        i = d % 32
        C64[p] = c64[:, i]
        S64[p] = -s64[:, i] if d < 32 else s64[:, i]
    C96 = np.ones((128, NS), np.float32)
    S96 = np.zeros((128, NS), np.float32)
    for j in range(32):
        i = j % 16
        C96[64 + j] = c32[:, i]
        S96[64 + j] = -s32[:, i] if j < 16 else s32[:, i]
    return C64, S64, C96, S96


def _host_inputs(inp):
    f = np.float32
    pa = _perm_a()
    w_in = inp["w_in"]
    w_a = np.ascontiguousarray(w_in[:, :, pa])
    w_c1 = np.ascontiguousarray(np.concatenate([w_in[:, :, O_RG:O_RG + 512], w_in[:, :, O_GATE:O_GATE + 3072]], axis=2))
    uq = inp["w_uq"]
    idx = []
    for h in range(8):
        idx += list(range(h * 96, h * 96 + 64)) + [h * 96 + 64 + s for s in _swap_idx(32)]
    w_uq2 = np.ascontiguousarray(np.concatenate([uq, uq[:, :, np.array(idx)]], axis=2))
    ukv = inp["w_ukv"]
    idxk = []
    idxv = []
    for h in range(8):
        idxk += list(range(h * 128, h * 128 + 64))
        idxv += list(range(h * 128 + 64, h * 128 + 128))
    w_ukv2 = np.ascontiguousarray(ukv[:, :, np.array(idxk + idxv)])
    C64, S64, C96, S96 = _rope_tables()
    C32 = np.ones((128, NS), f); S32 = np.zeros((128, NS), f)
    C32[0:32] = C96[64:96]; S32[0:32] = S96[64:96]
    rope_t = np.stack([C64, S64, (C64 * 0.125).astype(f), (S64 * 0.125).astype(f), C96, S96, C32, S32]).astype(f)
    i = np.arange(128, dtype=f)
    jj = i[:, None]
    ii = i[None, :]
    DF = np.maximum(ii - jj, 0.0)
    DB = np.maximum(jj - ii, 0.0)
    MF = (ii >= jj).astype(f)
    MB = (jj > ii).astype(f)
    XIc = np.zeros((128, 128), f)
    XIc[0:64, :] = (i + 1.0)[None, :]
    XIc[64:128, :] = (128.0 - i)[None, :]
    ret_c = np.stack([DF, DB, MF, MB, XIc]).astype(f)
    ret_z = np.stack([127.0 - i, i], axis=1).astype(f)
    sel65 = np.zeros((65, 64), f)
    sel65[64, :] = 1.0
    common = {
        "n1g": np.ascontiguousarray(inp["norm1_g"].reshape(DEPTH, 8, 128).transpose(0, 2, 1)),
        "n2g": np.ascontiguousarray(inp["norm2_g"].reshape(DEPTH, 8, 128).transpose(0, 2, 1)),
        "w_ada": inp["w_ada"],
        "b_ada_fm": np.ascontiguousarray(inp["b_ada"].reshape(DEPTH, 48, 128).transpose(0, 2, 1)),
        "b_ada": inp["b_ada"],
        "w_a": w_a, "w_c1": w_c1,
        "dec_f": inp["ret_decay_fwd"], "dec_b": inp["ret_decay_bwd"],
        "dlam": np.ascontiguousarray(inp["diff_lambda"].reshape(DEPTH, 256)),
        "qng": np.ascontiguousarray(inp["mla_q_norm"].reshape(DEPTH, 3, 128).transpose(0, 2, 1)),
        "kvng": np.ascontiguousarray(inp["mla_kv_norm"].reshape(DEPTH, 2, 128).transpose(0, 2, 1)),
        "kvn_row": inp["mla_kv_norm"],
        "w_uq": w_uq2, "w_ukv": w_ukv2,
        "w_br": np.ascontiguousarray(inp["w_branch"].reshape(DEPTH, 1536, D)),
        "w_o": inp["w_out"], "w_f1": inp["w_ffn_in"], "w_f2": inp["w_ffn_out"],
        "fin_g": inp["final_g"],
        "ident_in": np.eye(128, dtype=f),
        "rope_t": rope_t, "ret_c": ret_c, "ret_z": ret_z, "sel65": sel65,
    }
    maps = []
    for c in range(8):
        m = dict(common)
        m["x_in"] = np.ascontiguousarray(np.concatenate(
            [inp["x_sample"][c], inp["x_prompt"][4 * c:4 * c + 4].reshape(NPSEQ * PS_, D)], axis=0))
        m["st_f"] = np.ascontiguousarray(inp["state_ret_fwd"][c])
        m["st_b"] = np.ascontiguousarray(inp["state_ret_bwd"][c])
        m["c_dk"] = np.ascontiguousarray(inp["cache_diff_k"][c].reshape(DEPTH, LCTX, 512))
        m["c_dv"] = np.ascontiguousarray(inp["cache_diff_v"][c].reshape(DEPTH, LCTX, 512))
        m["c_ckv"] = np.ascontiguousarray(inp["cache_mla_ckv"][c])
        m["c_kpe"] = np.ascontiguousarray(inp["cache_mla_kpe"][c])
        cond = np.stack([inp["c"][c], inp["c_ctx"]], axis=1)
        m["cond_fm"] = np.ascontiguousarray(cond.reshape(8, 128, 2).transpose(1, 0, 2))
        maps.append(m)
    return maps


_NC_CACHE = {}


def kernel(**inputs):
    inp = {k: np.asarray(v) for k, v in inputs.items()}
    if "nc" not in _NC_CACHE:
        _NC_CACHE["nc"] = build_program()
    nc = _NC_CACHE["nc"]
    maps = _host_inputs(inp)
    res = run_bass_kernel_spmd(nc, maps, core_ids=list(range(8)))
    R = res.results
    y_sample = np.stack([R[c]["y_out"][:NS] for c in range(8)]).astype(np.float32)
    y_prompt = np.concatenate([R[c]["y_out"][NS:].reshape(NPSEQ, PS_, D) for c in range(8)]).astype(np.float32)
    rf = np.concatenate([R[c]["o_rf"] for c in range(8)]).astype(np.float32)
    rb = np.concatenate([R[c]["o_rb"] for c in range(8)]).astype(np.float32)
    dk = np.concatenate([R[c]["o_dk"] for c in range(8)]).reshape(32, DEPTH, PS_, 4, 128).astype(np.float32)
    dv = np.concatenate([R[c]["o_dv"] for c in range(8)]).reshape(32, DEPTH, PS_, 4, 128).astype(np.float32)
    ckv = np.concatenate([R[c]["o_ckv"] for c in range(8)]).astype(np.float32)
    kpe = np.concatenate([R[c]["o_kpe"] for c in range(8)]).astype(np.float32)
    return (y_prompt, y_sample, rf, rb, dk, dv, ckv, kpe)
```

```python
import contextlib
import math
import numpy as np
import concourse.bass as bass
import concourse.mybir as mybir
from concourse.bass_utils import run_bass_kernel_spmd

F32 = mybir.dt.float32
BF16 = mybir.dt.bfloat16
AF = mybir.ActivationFunctionType
ALU = mybir.AluOpType

D = 1024
DEPTH = 2
NS = 4096
NPSEQ = 4
PS_ = 256
NT = NS + NPSEQ * PS_
LCTX = 512
NKEY = LCTX + NT
DFF = 2816
EPS = 1e-6
MLA_SCALE = 96 ** -0.5
GRID_W = 64

O_RQ, O_RK, O_RV, O_RG, O_DQ, O_DK, O_DV, O_CQ, O_CKV, O_KPE, O_GATE = 0, 256, 512, 1024, 1536, 2048, 2560, 3072, 3456, 3712, 3744
A_DQ, A_DQS, A_DK, A_DKS, A_RQ2, A_RQ2S, A_RK, A_RKS, A_CQ, A_CKV, A_KPE, A_KPES, A_DV, A_RV = (
    0, 512, 1024, 1536, 2048, 2560, 3072, 3328, 3584, 3968, 4224, 4256, 4288, 4800)
NA = 5312
NC1 = 512 + 3072


class Buf:
    __slots__ = ("name", "w", "r", "toks")

    def __init__(self, name=""):
        self.name = name
        self.w = None
        self.r = {}


class Eng:
    def __init__(self, fw, name, eng, sem, self_sync=True):
        self.fw = fw
        self.name = name
        self.e = eng
        self.sem = sem
        self.count = 0
        self.known = {}
        self.self_sync = self_sync

    def wait_tok(self, tok):
        if tok is None:
            return
        sem, val = tok
        if sem is self.sem and not self.self_sync:
            return
        k = id(sem)
        if self.known.get(k, 0) >= val:
            return
        self.e.wait_ge(sem, val)
        self.known[k] = val

    def deps(self, reads, writes):
        for b in reads:
            self.wait_tok(b.w)
        for b in writes:
            self.wait_tok(b.w)
            for tok in list(b.r.values()):
                self.wait_tok(tok)

    def mark(self, tok, reads, writes):
        sem, val = tok
        for b in reads:
            old = b.r.get(id(sem))
            if old is None or old[1] < val:
                b.r[id(sem)] = (sem, val)
        for b in writes:
            b.w = tok
            b.r = {}

    def op(self, fns, reads=(), writes=()):
        self.deps(reads, writes)
        if callable(fns):
            fns = [fns]
        ins = None
        for f in fns:
            ins = f()
        self.count += 1
        ins.then_inc(self.sem, 1)
        tok = (self.sem, self.count)
        self.mark(tok, reads, writes)
        self.fw.ninst += len(fns)
        return tok


class DmaQ:
    def __init__(self, fw, name, eng, sems):
        self.fw = fw
        self.eng = eng
        self.sems = sems
        self.tot = [0] * len(sems)
        self.i = 0

    def dma(self, out, in_, reads=(), writes=()):
        q = self.eng
        q.deps(reads, writes)
        i = self.i
        self.i = (self.i + 1) % len(self.sems)
        sem = self.sems[i]
        if self.tot[i] > 0:
            q.wait_tok((sem, self.tot[i]))
        self.tot[i] += 16
        q.e.dma_start(out=out, in_=in_).then_inc(sem, 16)
        tok = (sem, self.tot[i])
        q.mark(tok, reads, writes)
        self.fw.ndma += 1
        return tok


class FW:
    def __init__(self, nc, es):
        self.nc = nc
        self.ninst = 0
        self.ndma = 0
        mk = lambda n: es.enter_context(nc.semaphore(n))
        self.pe = Eng(self, "pe", nc.tensor, mk("s_pe"), self_sync=False)
        self.act = Eng(self, "act", nc.scalar, mk("s_act"))
        self.dve = Eng(self, "dve", nc.vector, mk("s_dve"))
        self.pool = Eng(self, "pool", nc.gpsimd, mk("s_pool"))
        self.sp = Eng(self, "sp", nc.sync, mk("s_sp"))
        self.engs = [self.pe, self.act, self.dve, self.pool, self.sp]
        self.q_ld = DmaQ(self, "q_ld", self.sp, [mk(f"s_ld{i}") for i in range(12)])
        self.q_st = DmaQ(self, "q_st", self.pool, [mk(f"s_st{i}") for i in range(12)])
        self.qs = [self.q_ld, self.q_st]

    def barrier(self):
        toks = [(e.sem, e.count) for e in self.engs if e.count > 0]
        for q in self.qs:
            for s, t in zip(q.sems, q.tot):
                if t > 0:
                    toks.append((s, t))
        for e in self.engs:
            for tok in toks:
                if tok[0] is e.sem:
                    continue
                e.wait_tok(tok)


class _StopBuild(Exception):
    pass


def _ckpt(name):
    import os
    if os.environ.get("ASTOP", "") == name:
        raise _StopBuild(name)


def build_program(stop_after=None):
    nc = bass.Bass("TRN2", target_bir_lowering=False)

    def din(name, shape, dt=F32):
        return nc.dram_tensor(name, list(shape), dt, kind="ExternalInput").ap()

    def dout(name, shape, dt=F32):
        return nc.dram_tensor(name, list(shape), dt, kind="ExternalOutput").ap()

    def dscr(name, shape, dt=BF16):
        return nc.dram_tensor(name, list(shape), dt, kind="Internal").ap()

    x_in = din("x_in", [NT, D])
    st_f = din("st_f", [DEPTH, 4, 64, 128])
    st_b = din("st_b", [DEPTH, 4, 64, 128])
    c_dk = din("c_dk", [DEPTH, LCTX, 512])
    c_dv = din("c_dv", [DEPTH, LCTX, 512])
    c_ckv = din("c_ckv", [DEPTH, LCTX, 256])
    c_kpe = din("c_kpe", [DEPTH, LCTX, 32])
    cond_fm = din("cond_fm", [128, 8, 2])
    n1g = din("n1g", [DEPTH, 128, 8])
    n2g = din("n2g", [DEPTH, 128, 8])
    w_ada = din("w_ada", [DEPTH, D, 6 * D])
    b_ada_fm = din("b_ada_fm", [DEPTH, 128, 48])
    b_ada = din("b_ada", [DEPTH, 6 * D])
    w_a = din("w_a", [DEPTH, D, NA])
    w_c1 = din("w_c1", [DEPTH, D, NC1])
    dec_f = din("dec_f", [DEPTH, 4])
    dec_b = din("dec_b", [DEPTH, 4])
    dlam = din("dlam", [DEPTH, 256])
    qng = din("qng", [DEPTH, 128, 3])
    kvng = din("kvng", [DEPTH, 128, 2])
    kvn_row = din("kvn_row", [DEPTH, 256])
    w_uq = din("w_uq", [DEPTH, 384, 1536])
    w_ukv = din("w_ukv", [DEPTH, 256, 1024])
    w_br = din("w_br", [DEPTH, 1536, D])
    w_o = din("w_o", [DEPTH, D, D])
    w_f1 = din("w_f1", [DEPTH, D, 2 * DFF])
    w_f2 = din("w_f2", [DEPTH, DFF, D])
    fin_g = din("fin_g", [D])
    ident_in = din("ident_in", [128, 128])
    rope_t = din("rope_t", [8, 128, NS])
    ret_c = din("ret_c", [5, 128, 128])
    ret_z = din("ret_z", [128, 2])
    sel65 = din("sel65", [65, 64])

    y_out = dout("y_out", [NT, D])
    o_rf = dout("o_rf", [NPSEQ, DEPTH, 4, 64, 128])
    o_rb = dout("o_rb", [NPSEQ, DEPTH, 4, 64, 128])
    o_dk = dout("o_dk", [NPSEQ, DEPTH, PS_, 512])
    o_dv = dout("o_dv", [NPSEQ, DEPTH, PS_, 512])
    o_ckv = dout("o_ckv", [NPSEQ, DEPTH, PS_, 256])
    o_kpe = dout("o_kpe", [NPSEQ, DEPTH, PS_, 32])

    s_qd = dscr("s_qd", [4, 128, NT])
    s_kd = dscr("s_kd", [4, 128, NKEY])
    s_vd = dscr("s_vd", [NKEY, 512])
    s_rq = dscr("s_rq", [4, 128, NT])
    s_rk = dscr("s_rk", [4, 64, NT])
    s_rv = dscr("s_rv", [NT, 512])
    s_qm = dscr("s_qm", [8, 128, NT])
    s_km = dscr("s_km", [8, 128, NKEY])
    s_kpe = dscr("s_kpe", [32, NKEY])
    s_vm = dscr("s_vm", [NKEY, 8, 128])
    s_oret = dscr("s_oret", [4, 128, NT])
    s_yd = dscr("s_yd", [4, 128, NT])
    s_ym = dscr("s_ym", [512, NT])
    s_xmid = dscr("s_xmid", [NT, D], F32)
    s_x1 = dscr("s_x1", [NT, D], F32)
    s_gb = dscr("s_gb", [2, 2, 128, D], F32)

    SEQS = [(0, NS, 0, LCTX + NS, True)] + [
        (NS + p * PS_, PS_, LCTX + NS + p * PS_, PS_, False) for p in range(NPSEQ)]

    BLOCKS = [(0, NS, 0, LCTX + NS, 1), (NS, PS_, LCTX + NS, PS_, NPSEQ)]

    with contextlib.ExitStack() as ges:
        fw = FW(nc, ges)
        pe, act, dve, pool, q_ld, q_st = fw.pe, fw.act, fw.dve, fw.pool, fw.q_ld, fw.q_st
        V, S, T, G_ = nc.vector, nc.scalar, nc.tensor, nc.gpsimd

        _uid = [0]

        def sbt(es, name, shape, dt):
            _uid[0] += 1
            return es.enter_context(nc.sbuf_tensor(f"{name}_{_uid[0]}", list(shape), dt))

        PSB = [ges.enter_context(nc.psum_tensor(f"psb{i}", [128, 512], F32)) for i in range(8)]
        PSBUF = [Buf(f"psb{i}") for i in range(8)]

        class Ring:
            def __init__(self, items):
                self.items = items
                self.i = 0

            def next(self):
                it = self.items[self.i]
                self.i = (self.i + 1) % len(self.items)
                return it

        B = {}
        for nm in ["wa", "wc1", "wuq", "wukv", "wbr", "wo", "wf1", "wf2"]:
            for l in range(DEPTH):
                B[(nm, l)] = Buf(nm)
        for nm in ["qd", "kd", "vd", "rq", "rk", "rv", "qm", "km", "kpe", "vm", "oret", "yd", "ym", "xmid", "x1", "gb",
                   "out"]:
            B[nm] = Buf(nm)

        ident_f = sbt(ges, "ident_f", [128, 128], F32)
        ident_b = sbt(ges, "ident_b", [128, 128], BF16)
        ones_b = sbt(ges, "ones_b", [128, 128], BF16)
        ones_f = sbt(ges, "ones_f", [128, 128], F32)
        mods = sbt(ges, "mods", [128, 2, 4, 8], F32)
        b_mods = Buf("mods")
        b_const = Buf("const")
        q_ld.dma(ident_f[:], ident_in, writes=[b_const])
        dve.op(lambda: V.tensor_copy(out=ident_b[:], in_=ident_f[:]), reads=[b_const], writes=[b_const])
        dve.op(lambda: V.memset(ones_b[:], 1.0), writes=[b_const])
        dve.op(lambda: V.memset(ones_f[:], 1.0), writes=[b_const])
        cb_t = sbt(ges, "cb_t", [128, 4], F32)
        for i_, v_ in enumerate([EPS, 1.0, 128.0 * EPS, 0.0]):
            dve.op(lambda i_=i_, v_=v_: V.memset(cb_t[:, i_:i_ + 1], v_), writes=[b_const])
        CB_EPS, CB_ONE, CB_128EPS, CB_ZERO = 0, 1, 2, 3

        def act_pow(out_ap, in_ap, p, mul, cbi, reads, writes, p0=0, p1=128):
            act.op(lambda: S.activation(out=out_ap, in_=in_ap, func=AF.Ln, scale=mul, bias=cb_t[p0:p1, cbi:cbi + 1]),
                   reads=list(reads) + [b_const], writes=writes)
            act.op(lambda: S.activation(out=out_ap, in_=out_ap, func=AF.Exp, scale=p), reads=writes, writes=writes)

        with contextlib.ExitStack() as ies:
            zt = sbt(ies, "zt", [32, NKEY], BF16)
            vt = sbt(ies, "vt", [128, 8, 64], BF16)
            bzt = Buf("zt")
            dve.op(lambda: V.memset(zt[:], 0.0), writes=[bzt])
            dve.op(lambda: V.memset(vt[:], 0.0), writes=[bzt])
            dve.op(lambda: V.memset(vt[:, :, 0:1], 1.0), writes=[bzt])
            for h_ in range(8):
                q_ld.dma(s_km[h_, 96:128, :], zt[:, 0:NKEY], reads=[bzt], writes=[B["km"]])
                q_ld.dma(s_qm[h_, 96:128, :], zt[:, 0:NT], reads=[bzt], writes=[B["qm"]])
            fw.barrier()

        def ada_phase(l):
            with contextlib.ExitStack() as es:
                cond = sbt(es, "cond", [128, 8, 2], F32)
                scond = sbt(es, "scond", [128, 8, 2], F32)
                scb = sbt(es, "scb", [128, 2, 8, 128], F32)
                wblk = [sbt(es, f"wblk{i}", [128, 8, 1024], F32) for i in range(2)]
                b_wblk = [Buf("wblk0"), Buf("wblk1")]
                bfm = sbt(es, "bfm", [128, 48], F32)
                brow = sbt(es, "brow", [1, 6 * D], F32)
                g1t = sbt(es, "g1t", [128, 8], F32)
                g2t = sbt(es, "g2t", [128, 8], F32)
                modT = sbt(es, "modT", [128, 48, 2], F32)
                gbt = [sbt(es, f"gbt{i}", [128, 512], F32) for i in range(2)]
                b_gbt = [Buf("gbt0"), Buf("gbt1")]
                b_c = Buf("cond"); b_sc = Buf("scond"); b_scb = Buf("scb"); b_misc = Buf("misc"); b_modT = Buf("modT")
                q_ld.dma(cond[:], cond_fm, writes=[b_c])
                q_ld.dma(bfm[:], b_ada_fm[l], writes=[b_misc])
                q_ld.dma(brow[:], b_ada[l:l + 1, :], writes=[b_misc])
                q_ld.dma(g1t[:], n1g[l], writes=[b_misc])
                q_ld.dma(g2t[:], n2g[l], writes=[b_misc])
                act.op(lambda: S.activation(out=scond[:], in_=cond[:], func=AF.Silu), reads=[b_c], writes=[b_sc])
                for c in range(2):
                    for kc in range(8):
                        dve.op(lambda c=c, kc=kc: V.tensor_scalar(out=scb[:, c, kc, :], in0=ones_f[:],
                                                                  scalar1=scond[:, kc, c:c + 1], scalar2=None,
                                                                  op0=ALU.mult),
                               reads=[b_sc, b_const], writes=[b_scb])
                pm = PSB[7]
                bpm = PSBUF[7]
                pmv = pm[:, 0:96].rearrange("p (j c) -> p j c", c=2)
                gring = Ring([0, 1])
                for cb in range(12):
                    wi = (cb // 2) % 2
                    wt = wblk[wi][:, :, (cb % 2) * 512:(cb % 2 + 1) * 512]
                    if cb % 2 == 0:
                        for kc in range(8):
                            q_ld.dma(wblk[wi][:, kc, :], w_ada[l, kc * 128:(kc + 1) * 128, cb * 512:(cb + 2) * 512],
                                     writes=[b_wblk[wi]])
                    if cb in (4, 5, 10, 11):
                        gi = 0 if cb < 6 else 1
                        half = cb % 2
                        for c in range(2):
                            pb = gring.next()
                            ps_, bps = PSB[pb], PSBUF[pb]
                            fns = [(lambda kc=kc, c=c, ps_=ps_, wt=wt: T.matmul(ps_[:], lhsT=scb[:, c, kc, :],
                                                                            rhs=wt[:, kc, :], start=(kc == 0),
                                                                            stop=False)) for kc in range(8)]
                            fns.append(lambda ps_=ps_, cb=cb: T.matmul(ps_[:], lhsT=ones_f[0:1, :],
                                                                      rhs=brow[0:1, cb * 512:(cb + 1) * 512],
                                                                      start=False, stop=True))
                            pe.op(fns, reads=[b_scb, b_wblk[wi], b_misc, b_const], writes=[bps])
                            gt = gbt[pb]
                            act.op(lambda gt=gt, ps_=ps_: S.copy(out=gt[:], in_=ps_[:]), reads=[bps], writes=[b_gbt[pb]])
                            q_st.dma(s_gb[gi, c, :, half * 512:(half + 1) * 512], gt[:], reads=[b_gbt[pb]],
                                     writes=[B["gb"]])
                    for jj in range(4):
                        j = cb * 4 + jj
                        fns = [(lambda kc=kc, j=j, jj=jj, wt=wt: T.matmul(pmv[:, j, :],
                                                                       lhsT=wt[:, kc, jj * 128:(jj + 1) * 128],
                                                                       rhs=scond[:, kc, :], start=(kc == 0),
                                                                       stop=(kc == 7))) for kc in range(8)]
                        pe.op(fns, reads=[b_sc, b_wblk[wi]], writes=[bpm])
                for c in range(2):
                    dve.op(lambda c=c: V.tensor_tensor(out=modT[:, :, c], in0=pmv[:, :, c], in1=bfm[:], op=ALU.add),
                           reads=[bpm, b_misc], writes=[b_modT])
                for c in range(2):
                    dve.op(lambda c=c: V.scalar_tensor_tensor(out=mods[:, c, 0, :], in0=modT[:, 8:16, c], scalar=1.0,
                                                              in1=g1t[:], op0=ALU.add, op1=ALU.mult),
                           reads=[b_modT, b_misc], writes=[b_mods])
                    dve.op(lambda c=c: V.tensor_copy(out=mods[:, c, 1, :], in_=modT[:, 0:8, c]),
                           reads=[b_modT], writes=[b_mods])
                    dve.op(lambda c=c: V.scalar_tensor_tensor(out=mods[:, c, 2, :], in0=modT[:, 32:40, c], scalar=1.0,
                                                              in1=g2t[:], op0=ALU.add, op1=ALU.mult),
                           reads=[b_modT, b_misc], writes=[b_mods])
                    dve.op(lambda c=c: V.tensor_copy(out=mods[:, c, 3, :], in_=modT[:, 24:32, c]),
                           reads=[b_modT], writes=[b_mods])
                fw.barrier()

        def make_norm_ctx(es, nx=4, nh=1):
            ctx = {}
            ctx["xt"] = [sbt(es, f"xt{i}", [128, D], F32) for i in range(nx)]
            ctx["bx"] = [Buf(f"xt{i}") for i in range(nx)]
            ctx["xn"] = [sbt(es, f"xn{i}", [128, D], BF16) for i in range(4)]
            ctx["bxn"] = [Buf(f"xn{i}") for i in range(4)]
            ctx["junk"] = sbt(es, "junk", [128, D], BF16)
            ctx["bjunk"] = Buf("junk")
            ctx["ss"] = sbt(es, "ss", [128, 8], F32)
            ctx["bss"] = [Buf(f"ss{i}") for i in range(4)]
            ctx["hTs"] = [sbt(es, f"hT{i}", [128, 8, 512], BF16) for i in range(nh)]
            ctx["bhTs"] = [Buf(f"hT{i}") for i in range(nh)]
            ctx["hT"] = ctx["hTs"][0]
            ctx["bhT"] = ctx["bhTs"][0]
            return ctx

        def norm_to_hT(ctx, xsrc, bsrc, g0, cidx, which, psbanks, slot=0, xo=0):
            xn, bxn, ss, bss = (ctx[k] for k in ["xn", "bxn", "ss", "bss"])
            xt, bx = ctx["xt"][xo:xo + 4], ctx["bx"][xo:xo + 4]
            hT, bhT = ctx["hTs"][slot], ctx["bhTs"][slot]
            junk, bjunk = ctx["junk"], ctx["bjunk"]
            for t in range(4):
                q_ld.dma(xt[t][:], xsrc[g0 + t * 128:g0 + (t + 1) * 128, :], reads=[bsrc], writes=[bx[t]])
                act.op(lambda t=t: S.activation(out=junk[:], in_=xt[t][:], func=AF.Square, accum_out=ss[:, t:t + 1]),
                       reads=[bx[t]], writes=[bjunk, bss[t]])
                act_pow(ss[:, 4 + t:5 + t], ss[:, t:t + 1], -0.5, 1.0 / D, CB_EPS, [bss[t]], [bss[t]])
                dve.op(lambda t=t: V.tensor_scalar(out=xn[t][:], in0=xt[t][:], scalar1=ss[:, 4 + t:5 + t], scalar2=None,
                                                   op0=ALU.mult),
                       reads=[bx[t], bss[t]], writes=[bxn[t]])
            ai, bi = (0, 1) if which == 0 else (2, 3)
            for j in range(8):
                pb = psbanks.next()
                pst = PSB[pb][:].bitcast(BF16)
                fns = [(lambda t=t, j=j, pst=pst: T.transpose(pst[:, t * 128:(t + 1) * 128],
                                                              xn[t][:, j * 128:(j + 1) * 128], ident_b[:]))
                       for t in range(4)]
                pe.op(fns, reads=bxn + [b_const], writes=[PSBUF[pb]])
                c = cidx
                if j % 2 == 0:
                    act.op(lambda j=j, pst=pst, c=c: S.activation(out=hT[:, j, :], in_=pst[:, 0:512], func=AF.Identity,
                                                                  scale=mods[:, c, ai, j:j + 1],
                                                                  bias=mods[:, c, bi, j:j + 1]),
                           reads=[PSBUF[pb], b_mods], writes=[bhT])
                else:
                    dve.op(lambda j=j, pst=pst, c=c: V.tensor_scalar(out=hT[:, j, :], in0=pst[:, 0:512],
                                                                     scalar1=mods[:, c, ai, j:j + 1],
                                                                     scalar2=mods[:, c, bi, j:j + 1], op0=ALU.mult,
                                                                     op1=ALU.add),
                           reads=[PSBUF[pb], b_mods], writes=[bhT])

        cast_rr = [0]

        def load_w(es, name, src, kchunks, ncols, key, l, krows=128):
            wt = sbt(es, name, [krows, kchunks, ncols], BF16)
            bw = Buf(name)
            CW = 2048
            with contextlib.ExitStack() as ses:
                stg = [sbt(ses, f"stg{i}", [128, CW], F32) for i in range(3)]
                bstg = [Buf(f"stg{i}") for i in range(3)]
                k = 0
                for kc in range(kchunks):
                    for c0 in range(0, ncols, CW):
                        cw = min(CW, ncols - c0)
                        i = k % 3
                        k += 1
                        q_ld.dma(stg[i][0:krows, 0:cw], src[kc * krows:(kc + 1) * krows, c0:c0 + cw], writes=[bstg[i]])
                        e = (0, 1, 0, 1, 2)[cast_rr[0] % 5]
                        cast_rr[0] += 1
                        if e == 0:
                            act.op(lambda i=i, kc=kc, c0=c0, cw=cw: S.copy(out=wt[:, kc, c0:c0 + cw], in_=stg[i][0:krows, 0:cw]),
                                   reads=[bstg[i]], writes=[bw])
                        elif e == 1:
                            dve.op(lambda i=i, kc=kc, c0=c0, cw=cw: V.tensor_copy(out=wt[:, kc, c0:c0 + cw], in_=stg[i][0:krows, 0:cw]),
                                   reads=[bstg[i]], writes=[bw])
                        else:
                            pool.op(lambda i=i, kc=kc, c0=c0, cw=cw: G_.tensor_copy(out=wt[:, kc, c0:c0 + cw], in_=stg[i][0:krows, 0:cw]),
                                    reads=[bstg[i]], writes=[bw])
                fw.barrier()
            return wt, bw

        def phase_a(l, xsrc, bsrc):
            with contextlib.ExitStack() as es:
                WA, bWA = load_w(es, "WA", w_a[l], 8, NA, "wa", l)
                WUQ, bWUQ = load_w(es, "WUQ", w_uq[l], 3, 1536, "wuq", l)
                WUKV, bWUKV = load_w(es, "WUKV", w_ukv[l], 2, 1024, "wukv", l)
                ctx = make_norm_ctx(es, nh=2)
                hT, bhT = ctx["hT"], ctx["bhT"]
                normed = [-1]
                rt = sbt(es, "rt", [128, 8, 512], F32)
                brt = Buf("rt")
                qg = sbt(es, "qg", [128, 3], F32)
                kvg = sbt(es, "kvg", [128, 2], F32)
                kvrow = sbt(es, "kvrow", [128, 256], F32)
                bsm = Buf("small")
                q_ld.dma(qg[:], qng[l], writes=[bsm])
                q_ld.dma(kvg[:], kvng[l], writes=[bsm])
                q_ld.dma(kvrow[:], kvn_row[l].partition_broadcast(128), writes=[bsm])
                NTMP = 4
                tmpf = [sbt(es, f"tmpf{i}", [128, 512], F32) for i in range(NTMP)]
                btmpf = [Buf(f"tmpf{i}") for i in range(NTMP)]
                tring = Ring(list(range(NTMP)))
                NOB = 6
                ob = [sbt(es, f"ob{i}", [128, 512], BF16) for i in range(NOB)]
                bob = [Buf(f"ob{i}") for i in range(NOB)]
                oring = Ring(list(range(NOB)))
                obv = [sbt(es, f"obv{i}", [128, 8, 128], BF16) for i in range(2)]
                bobv = [Buf(f"obv{i}") for i in range(2)]
                obvring = Ring([0, 1])
                for i_ in range(2):
                    dve.op(lambda i_=i_: V.memset(obv[i_][:], 0.0), writes=[bobv[i_]])
                    dve.op(lambda i_=i_: V.memset(obv[i_][:, :, 64:65], 1.0), writes=[bobv[i_]])
                of = [sbt(es, f"of{i}", [128, 512], F32) for i in range(2)]
                bof = [Buf(f"of{i}") for i in range(2)]
                ofring = Ring([0, 1])
                cqg = sbt(es, "cqg", [128, 3, 512], BF16)
                bcqg = Buf("cqg")
                sq = sbt(es, "sq", [128, 3, 512], BF16)
                bsq = Buf("sq")
                rstd_q = sbt(es, "rstd_q", [128, 512], F32)
                b_rq_ = Buf("rstd_q")
                rstd_k = sbt(es, "rstd_k", [128, 512], F32)
                b_rk_ = Buf("rstd_k")
                ckvn = sbt(es, "ckvn", [128, 2, 512], BF16)
                bckvn = Buf("ckvn")
                cqf = sbt(es, "cqf", [128, 3, 512], F32)
                bcqf = Buf("cqf")
                sst = sbt(es, "sst", [128, 4], F32)
                bsst = Buf("sst")
                psr = Ring([0, 1, 2, 3, 4, 5])
                pstr = Ring([6, 7])

                def proj_fm(pb, col0, M, G=512):
                    ps_ = PSB[pb]
                    fns = [(lambda kc=kc: T.matmul(ps_[0:M, 0:G], lhsT=WA[:, kc, col0:col0 + M], rhs=hT[:, kc, 0:G],
                                                   start=(kc == 0), stop=(kc == 7))) for kc in range(8)]
                    pe.op(fns, reads=[bWA, bhT], writes=[PSBUF[pb]])

                def store(dst, src_ap, bsrc_, bdst):
                    q_st.dma(dst, src_ap, reads=[bsrc_], writes=[bdst])

                rope_rr = [0]

                def evac_rope(pbx, pbs, M, ci, si, dst, bdst, rstd=None, dsts=None):
                    i1, i2 = tring.next(), tring.next()
                    t1, t2 = tmpf[i1], tmpf[i2]
                    dve.op(lambda: V.tensor_tensor(out=t1[0:M, :], in0=PSB[pbx][0:M, :], in1=rt[0:M, ci, :],
                                                   op=ALU.mult), reads=[PSBUF[pbx], brt], writes=[btmpf[i1]])
                    dve.op(lambda: V.tensor_tensor(out=t2[0:M, :], in0=PSB[pbs][0:M, :], in1=rt[0:M, si, :],
                                                   op=ALU.mult), reads=[PSBUF[pbs], brt], writes=[btmpf[i2]])
                    oi = oring.next()
                    o = ob[oi]
                    if rstd is None:
                        rope_rr[0] += 1
                        if rope_rr[0] % 3 == 0:
                            dve.op(lambda: V.tensor_tensor(out=o[0:M, :], in0=t1[0:M, :], in1=t2[0:M, :], op=ALU.add),
                                   reads=[btmpf[i1], btmpf[i2]], writes=[bob[oi]])
                        else:
                            pool.op(lambda: G_.tensor_tensor(out=o[0:M, :], in0=t1[0:M, :], in1=t2[0:M, :], op=ALU.add),
                                    reads=[btmpf[i1], btmpf[i2]], writes=[bob[oi]])
                    else:
                        pool.op(lambda: G_.tensor_tensor(out=t1[0:M, :], in0=t1[0:M, :], in1=t2[0:M, :], op=ALU.add),
                                reads=[btmpf[i1], btmpf[i2]], writes=[btmpf[i1]])
                        dve.op(lambda: V.tensor_tensor(out=o[0:M, :], in0=t1[0:M, :], in1=rstd[0:M, :], op=ALU.mult),
                               reads=[btmpf[i1], b_rq_], writes=[bob[oi]])
                    if dsts is not None:
                        for d_ in dsts:
                            store(d_, o[0:M, :], bob[oi], bdst)
                    else:
                        store(dst, o[0:M, :], bob[oi], bdst)

                def evac_plain(pbx, M, dst, bdst, scale=1.0, use_act=True, G=512, v3=False, dsts=None):
                    oi = oring.next()
                    o = ob[oi]
                    if use_act:
                        act.op(lambda: S.activation(out=o[0:M, 0:G], in_=PSB[pbx][0:M, 0:G], func=AF.Copy, scale=scale),
                               reads=[PSBUF[pbx]], writes=[bob[oi]])
                    else:
                        dve.op(lambda: V.tensor_scalar(out=o[0:M, 0:G], in0=PSB[pbx][0:M, 0:G], scalar1=scale,
                                                       scalar2=None, op0=ALU.mult),
                               reads=[PSBUF[pbx]], writes=[bob[oi]])
                    if v3:
                        store(dst, o[0:M, 0:G].rearrange("p (h e) -> p h e", e=64), bob[oi], bdst)
                    elif dsts is not None:
                        for d_ in dsts:
                            store(d_, o[0:M, 0:G], bob[oi], bdst)
                    else:
                        store(dst, o[0:M, 0:G], bob[oi], bdst)

                def fm_chunk(col, cols, M, dst, bdst, rope, ci=0, si=1, scale=1.0):
                    pbx = psr.next()
                    proj_fm(pbx, col, M)
                    if rope:
                        pbs = psr.next()
                        proj_fm(pbs, cols, M)
                        evac_rope(pbx, pbs, M, ci, si, dst, bdst)
                    else:
                        evac_plain(pbx, M, dst, bdst, scale=scale)

                def rms_fm(src_t, bsrc_t, nchunks, nfeat, rstd_t, brstd):
                    for i in range(nchunks):
                        act.op(lambda i=i: S.activation(out=sq[:, i, :], in_=src_t[:, i, :], func=AF.Square),
                               reads=[bsrc_t], writes=[bsq])
                    _ckpt("sq")
                    pb = psr.next()
                    fns = [(lambda i=i: T.matmul(PSB[pb][:], lhsT=ones_b[:], rhs=sq[:, i, :], start=(i == 0),
                                                 stop=(i == nchunks - 1))) for i in range(nchunks)]
                    pe.op(fns, reads=[bsq, b_const], writes=[PSBUF[pb]])
                    _ckpt("onesmm")
                    act_pow(rstd_t[:], PSB[pb][:], -0.5, 1.0 / nfeat, CB_EPS, [PSBUF[pb]], [brstd])

                def mla_from_ckvn(k0):
                    for h in range(8):
                        pb = psr.next()
                        fns = [(lambda kc=kc, h=h, pb=pb: T.matmul(PSB[pb][0:64, :], lhsT=WUKV[:, kc, h * 64:(h + 1) * 64],
                                                                   rhs=ckvn[:, kc, :], start=(kc == 0), stop=(kc == 1)))
                               for kc in range(2)]
                        pe.op(fns, reads=[bWUKV, bckvn], writes=[PSBUF[pb]])
                        evac_plain(pb, 64, s_km[h, 0:64, k0:k0 + 512], B["km"], use_act=(h % 2 == 0))
                    for t in range(4):
                        pb = psr.next()
                        fns = [(lambda kc=kc, t=t, pb=pb: T.matmul(PSB[pb][:], lhsT=ckvn[:, kc, t * 128:(t + 1) * 128],
                                                                   rhs=WUKV[:, kc, 512:1024], start=(kc == 0),
                                                                   stop=(kc == 1))) for kc in range(2)]
                        pe.op(fns, reads=[bWUKV, bckvn], writes=[PSBUF[pb]])
                        vi = obvring.next()
                        if t % 2 == 1:
                            act.op(lambda vi=vi, pb=pb: S.copy(out=obv[vi][:, :, 0:64],
                                                               in_=PSB[pb][:].rearrange("p (h e) -> p h e", e=64)),
                                   reads=[PSBUF[pb]], writes=[bobv[vi]])
                        else:
                            dve.op(lambda vi=vi, pb=pb: V.tensor_copy(out=obv[vi][:, :, 0:64],
                                                                      in_=PSB[pb][:].rearrange("p (h e) -> p h e", e=64)),
                                   reads=[PSBUF[pb]], writes=[bobv[vi]])
                        store(s_vm[k0 + t * 128:k0 + (t + 1) * 128, :, :], obv[vi][:], bobv[vi], B["vm"])

                def cache_group():
                    ck = [sbt(es, f"ck{i}", [128, 512], F32) for i in range(2)]
                    bck = [Buf("ck0"), Buf("ck1")]
                    ckring = Ring([0, 1])
                    for t in range(4):
                        i = ckring.next()
                        q_ld.dma(ck[i][:], c_dv[l, t * 128:(t + 1) * 128, :], writes=[bck[i]])
                        oi = oring.next()
                        act.op(lambda oi=oi, i=i: S.copy(out=ob[oi][:], in_=ck[i][:]), reads=[bck[i]], writes=[bob[oi]])
                        store(s_vd[t * 128:(t + 1) * 128, :], ob[oi][:], bob[oi], B["vd"])
                    for t in range(4):
                        i = ckring.next()
                        q_ld.dma(ck[i][:], c_dk[l, t * 128:(t + 1) * 128, :], writes=[bck[i]])
                        pb = psr.next()
                        fns = [(lambda h=h, i=i, pb=pb: T.transpose(PSB[pb][:, h * 128:(h + 1) * 128],
                                                                    ck[i][:, h * 128:(h + 1) * 128], ident_f[:]))
                               for h in range(4)]
                        pe.op(fns, reads=[bck[i], b_const], writes=[PSBUF[pb]])
                        oi = oring.next()
                        act.op(lambda oi=oi, pb=pb: S.copy(out=ob[oi][:], in_=PSB[pb][:]), reads=[PSBUF[pb]],
                               writes=[bob[oi]])
                        for h in range(4):
                            store(s_kd[h, :, t * 128:(t + 1) * 128], ob[oi][:, h * 128:(h + 1) * 128], bob[oi], B["kd"])
                    pbk = psr.next()
                    for t in range(4):
                        i = ckring.next()
                        q_ld.dma(ck[i][:, 0:256], c_ckv[l, t * 128:(t + 1) * 128, :], writes=[bck[i]])
                        q_ld.dma(ck[i][:, 256:288], c_kpe[l, t * 128:(t + 1) * 128, :], writes=[bck[i]])
                        pb = psr.next()
                        fns = [(lambda kc=kc, i=i, pb=pb: T.transpose(PSB[pb][:, kc * 128:(kc + 1) * 128],
                                                                      ck[i][:, kc * 128:(kc + 1) * 128], ident_f[:]))
                               for kc in range(2)]
                        pe.op(fns, reads=[bck[i], b_const], writes=[PSBUF[pb]])
                        for kc in range(2):
                            dve.op(lambda kc=kc, t=t, pb=pb: V.tensor_copy(out=ckvn[:, kc, t * 128:(t + 1) * 128],
                                                                           in_=PSB[pb][:, kc * 128:(kc + 1) * 128]),
                                   reads=[PSBUF[pb]], writes=[bckvn])
                        pe.op(lambda t=t, i=i: T.transpose(PSB[pbk][0:32, t * 128:(t + 1) * 128], ck[i][:, 256:288],
                                                           ident_f[:]), reads=[bck[i], b_const], writes=[PSBUF[pbk]])
                    evac_plain(pbk, 32, s_kpe[:, 0:LCTX], B["kpe"])
                    mla_from_ckvn(0)

                def token_group(g0):
                    is_sample = g0 < NS
                    cidx = 0 if is_sample else 1
                    k0 = LCTX + g0
                    rope = is_sample
                    nonlocal hT, bhT
                    slot = (g0 // 512) % 2
                    if normed[0] != g0:
                        norm_to_hT(ctx, xsrc, bsrc, g0, cidx, 0, pstr, slot=slot)
                    hT, bhT = ctx["hTs"][slot], ctx["bhTs"][slot]
                    _ckpt("norm")
                    if rope:
                        for i in range(8):
                            q_ld.dma(rt[:, i, :], rope_t[i, :, g0:g0 + 512], writes=[brt])
                    for h in range(4):
                        fm_chunk(A_DQ + h * 128, A_DQS + h * 128, 128, s_qd[h, :, g0:g0 + 512], B["qd"], rope, 0, 1)
                    for h in range(4):
                        fm_chunk(A_DK + h * 128, A_DKS + h * 128, 128, s_kd[h, :, k0:k0 + 512], B["kd"], rope, 0, 1)
                    _ckpt("dqk")
                    if g0 + 512 < NT:
                        norm_to_hT(ctx, xsrc, bsrc, g0 + 512, 0 if g0 + 512 < NS else 1, 0, pstr, slot=1 - slot)
                        normed[0] = g0 + 512
                    for h in range(4):
                        fm_chunk(A_RQ2 + h * 128, A_RQ2S + h * 128, 128, s_rq[h, :, g0:g0 + 512], B["rq"], rope, 0, 1)
                    for h in range(4):
                        fm_chunk(A_RK + h * 64, A_RKS + h * 64, 64, s_rk[h, :, g0:g0 + 512], B["rk"], rope, 2, 3,
                                 scale=0.125)
                    _ckpt("rqk")
                    if rope:
                        pbx, pbs = psr.next(), psr.next()
                        proj_fm(pbx, A_KPE, 32)
                        proj_fm(pbs, A_KPES, 32)
                        evac_rope(pbx, pbs, 32, 6, 7, s_kpe[:, k0:k0 + 512], B["kpe"])
                    else:
                        pbx = psr.next()
                        proj_fm(pbx, A_KPE, 32)
                        evac_plain(pbx, 32, s_kpe[:, k0:k0 + 512], B["kpe"])
                    _ckpt("kpe")
                    for i in range(3):
                        pb = psr.next()
                        proj_fm(pb, A_CQ + i * 128, 128)
                        act.op(lambda i=i, pb=pb: S.copy(out=cqf[:, i, :], in_=PSB[pb][:]), reads=[PSBUF[pb]], writes=[bcqf])
                        dve.op(lambda i=i: V.tensor_scalar(out=cqg[:, i, :], in0=cqf[:, i, :], scalar1=qg[:, i:i + 1],
                                                           scalar2=None, op0=ALU.mult),
                               reads=[bcqf, bsm], writes=[bcqg])
                    _ckpt("cq")
                    rms_fm(cqf, bcqf, 3, 384, rstd_q, b_rq_)
                    _ckpt("rms")
                    for h in range(8):
                        if h == 1:
                            _ckpt("uq1e")
                        pbx = psr.next()
                        fns = [(lambda kc=kc, h=h, pbx=pbx: T.matmul(PSB[pbx][0:96, :], lhsT=WUQ[:, kc, h * 96:(h + 1) * 96],
                                                                     rhs=cqg[:, kc, :], start=(kc == 0), stop=(kc == 2)))
                               for kc in range(3)]
                        pe.op(fns, reads=[bWUQ, bcqg], writes=[PSBUF[pbx]])
                        _ckpt("uq1")
                        if rope:
                            pbs = psr.next()
                            fns = [(lambda kc=kc, h=h, pbs=pbs: T.matmul(PSB[pbs][0:96, :],
                                                                         lhsT=WUQ[:, kc, 768 + h * 96:768 + (h + 1) * 96],
                                                                         rhs=cqg[:, kc, :], start=(kc == 0),
                                                                         stop=(kc == 2))) for kc in range(3)]
                            pe.op(fns, reads=[bWUQ, bcqg], writes=[PSBUF[pbs]])
                            evac_rope(pbx, pbs, 96, 4, 5, s_qm[h, 0:96, g0:g0 + 512], B["qm"], rstd=rstd_q)
                        else:
                            oi = oring.next()
                            dve.op(lambda oi=oi, pbx=pbx: V.tensor_tensor(out=ob[oi][0:96, :], in0=PSB[pbx][0:96, :],
                                                                          in1=rstd_q[0:96, :], op=ALU.mult),
                                   reads=[PSBUF[pbx], b_rq_], writes=[bob[oi]])
                            store(s_qm[h, 0:96, g0:g0 + 512], ob[oi][0:96, :], bob[oi], B["qm"])
                    _ckpt("qmla")
                    for i in range(2):
                        pb = psr.next()
                        proj_fm(pb, A_CKV + i * 128, 128)
                        act.op(lambda i=i, pb=pb: S.copy(out=cqf[:, i, :], in_=PSB[pb][:]), reads=[PSBUF[pb]], writes=[bcqf])
                    rms_fm(cqf, bcqf, 2, 256, rstd_k, b_rk_)
                    for i in range(2):
                        dve.op(lambda i=i: V.scalar_tensor_tensor(out=ckvn[:, i, :], in0=cqf[:, i, :], scalar=kvg[:, i:i + 1],
                                                                  in1=rstd_k[:], op0=ALU.mult, op1=ALU.mult),
                               reads=[bcqf, b_rk_, bsm], writes=[bckvn])
                    mla_from_ckvn(k0)
                    _ckpt("ckv")
                    for t in range(4):
                        tk = g0 + t * 128
                        for (col, dst, bdst, is_dv) in [(A_DV, s_vd[LCTX + tk:LCTX + tk + 128, :], B["vd"], True),
                                                        (A_RV, s_rv[tk:tk + 128, :], B["rv"], False)]:
                            pb = psr.next()
                            fns = [(lambda kc=kc, pb=pb, col=col, t=t: T.matmul(PSB[pb][:], lhsT=hT[:, kc, t * 128:(t + 1) * 128],
                                                                                rhs=WA[:, kc, col:col + 512],
                                                                                start=(kc == 0), stop=(kc == 7)))
                                   for kc in range(8)]
                            pe.op(fns, reads=[bWA, bhT], writes=[PSBUF[pb]])
                            evac_plain(pb, 128, dst, bdst, use_act=is_dv)
                            if is_dv and not is_sample:
                                p = (tk - NS) // PS_
                                s0 = (tk - NS) % PS_
                                fi = ofring.next()
                                dve.op(lambda fi=fi, pb=pb: V.tensor_copy(out=of[fi][:], in_=PSB[pb][:]),
                                       reads=[PSBUF[pb]], writes=[bof[fi]])
                                store(o_dv[p, l, s0:s0 + 128, :], of[fi][:], bof[fi], B["out"])
                        if not is_sample:
                            p = (tk - NS) // PS_
                            s0 = (tk - NS) % PS_
                            pb = psr.next()
                            fns = [(lambda kc=kc, pb=pb, t=t: T.matmul(PSB[pb][:], lhsT=hT[:, kc, t * 128:(t + 1) * 128],
                                                                       rhs=WA[:, kc, A_DK:A_DK + 512], start=(kc == 0),
                                                                       stop=(kc == 7))) for kc in range(8)]
                            pe.op(fns, reads=[bWA, bhT], writes=[PSBUF[pb]])
                            fi = ofring.next()
                            act.op(lambda fi=fi, pb=pb: S.copy(out=of[fi][:], in_=PSB[pb][:]), reads=[PSBUF[pb]],
                                   writes=[bof[fi]])
                            store(o_dk[p, l, s0:s0 + 128, :], of[fi][:], bof[fi], B["out"])
                            pb = psr.next()
                            fns = [(lambda kc=kc, pb=pb, t=t: T.matmul(PSB[pb][:, 0:288], lhsT=hT[:, kc, t * 128:(t + 1) * 128],
                                                                       rhs=WA[:, kc, A_CKV:A_CKV + 288], start=(kc == 0),
                                                                       stop=(kc == 7))) for kc in range(8)]
                            pe.op(fns, reads=[bWA, bhT], writes=[PSBUF[pb]])
                            fi = ofring.next()
                            act.op(lambda fi=fi, pb=pb: S.copy(out=of[fi][:, 0:288], in_=PSB[pb][:, 0:288]),
                                   reads=[PSBUF[pb]], writes=[bof[fi]])
                            act.op(lambda fi=fi: S.activation(out=ctx["junk"][:, 0:256], in_=of[fi][:, 0:256],
                                                              func=AF.Square, accum_out=sst[:, 2:3]),
                                   reads=[bof[fi]], writes=[ctx["bjunk"], bsst])
                            act_pow(sst[:, 3:4], sst[:, 2:3], -0.5, 1.0 / 256, CB_EPS, [bsst], [bsst])
                            dve.op(lambda fi=fi: V.scalar_tensor_tensor(out=of[fi][:, 0:256], in0=of[fi][:, 0:256],
                                                                        scalar=sst[:, 3:4], in1=kvrow[:],
                                                                        op0=ALU.mult, op1=ALU.mult),
                                   reads=[bsst, bsm], writes=[bof[fi]])
                            store(o_ckv[p, l, s0:s0 + 128, :], of[fi][:, 0:256], bof[fi], B["out"])
                            store(o_kpe[p, l, s0:s0 + 128, :], of[fi][:, 256:288], bof[fi], B["out"])

                try:
                    _ckpt("load")
                    cache_group()
                    _ckpt("cache")
                    for g in range(NT // 512):
                        token_group(g * 512)
                        _ckpt(f"g{g + 1}")
                except _StopBuild:
                    pass
                fw.barrier()

        def phase_r(l):
            with contextlib.ExitStack() as es:
                rc = sbt(es, "rc", [128, 5, 128], F32)
                rz = sbt(es, "rz", [128, 2], F32)
                dl = sbt(es, "dl", [128, 8], F32)
                lg = sbt(es, "lg", [128, 8], F32)
                lgx = sbt(es, "lgx", [128, 4], F32)
                Dm = sbt(es, "Dm", [128, 4, 128], F32)
                e2 = sbt(es, "e2", [128, 128], F32)
                XI = sbt(es, "XI", [128, 4, 128], F32)
                ZF = sbt(es, "ZF", [128, 4, 2], F32)
                GC = sbt(es, "GC", [128, 4], F32)
                bc = Buf("rconst")
                q_ld.dma(rc[:], ret_c.rearrange("a p f -> p a f"), writes=[bc])
                q_ld.dma(rz[:], ret_z, writes=[bc])
                q_ld.dma(dl[:, 0:4], dec_f[l].partition_broadcast(128), writes=[bc])
                q_ld.dma(dl[:, 4:8], dec_b[l].partition_broadcast(128), writes=[bc])
                act.op(lambda: S.activation(out=lg[:], in_=dl[:], func=AF.Exp, scale=-1.0), reads=[bc], writes=[bc])
                act.op(lambda: S.activation(out=lg[:], in_=lg[:], func=AF.Ln, bias=cb_t[:, 1:2]), reads=[bc], writes=[bc])
                dve.op(lambda: V.tensor_scalar(out=lg[:], in0=lg[:], scalar1=-1.0, scalar2=None, op0=ALU.mult),
                       reads=[bc], writes=[bc])
                dve.op(lambda: V.tensor_copy(out=lgx[0:64, :], in_=lg[0:64, 0:4]), reads=[bc], writes=[bc])
                dve.op(lambda: V.tensor_copy(out=lgx[64:128, :], in_=lg[64:128, 4:8]), reads=[bc], writes=[bc])
                for h in range(4):
                    act.op(lambda h=h: S.activation(out=Dm[:, h, :], in_=rc[:, 0, :], func=AF.Exp, scale=lg[:, h:h + 1]),
                           reads=[bc], writes=[bc])
                    dve.op(lambda h=h: V.tensor_tensor(out=Dm[:, h, :], in0=Dm[:, h, :], in1=rc[:, 2, :], op=ALU.mult),
                           reads=[bc], writes=[bc])
                    act.op(lambda h=h: S.activation(out=e2[:], in_=rc[:, 1, :], func=AF.Exp, scale=lg[:, 4 + h:5 + h]),
                           reads=[bc], writes=[bc])
                    dve.op(lambda h=h: V.tensor_tensor(out=e2[:], in0=e2[:], in1=rc[:, 3, :], op=ALU.mult),
                           reads=[bc], writes=[bc])
                    dve.op(lambda h=h: V.tensor_tensor(out=Dm[:, h, :], in0=Dm[:, h, :], in1=e2[:], op=ALU.add),
                           reads=[bc], writes=[bc])
                    act.op(lambda h=h: S.activation(out=XI[:, h, :], in_=rc[:, 4, :], func=AF.Exp, scale=lgx[:, h:h + 1]),
                           reads=[bc], writes=[bc])
                    act.op(lambda h=h: S.activation(out=ZF[:, h, 0:1], in_=rz[:, 0:1], func=AF.Exp, scale=lg[:, h:h + 1]),
                           reads=[bc], writes=[bc])
                    act.op(lambda h=h: S.activation(out=ZF[:, h, 1:2], in_=rz[:, 1:2], func=AF.Exp,
                                                    scale=lg[:, 4 + h:5 + h]), reads=[bc], writes=[bc])
                act.op(lambda: S.activation(out=GC[:], in_=lgx[:], func=AF.Exp, scale=128.0), reads=[bc], writes=[bc])

                for (tok0, S_, key0_, NK_, nseq) in BLOCKS:
                    is_sample = (nseq == 1)
                    n = S_ // 128
                    nall = n * nseq
                    with contextlib.ExitStack() as es2:
                        RQ = sbt(es2, "RQ", [128, 4, S_ * nseq], BF16)
                        RK = sbt(es2, "RK", [64, 4, S_ * nseq], BF16)
                        RV = sbt(es2, "RV", [128, nall, 512], BF16)
                        KZ = sbt(es2, "KZ", [128, nall, 4, 128], BF16)
                        SALL = sbt(es2, "SALL", [128, nall, 4, 128], BF16)
                        ST = sbt(es2, "ST", [128, 4, 128], F32)
                        bin_ = Buf("rin"); bKZ = Buf("KZ"); bSALLf = Buf("SALLf"); bSALLb = Buf("SALLb")
                        bSTf = Buf("STf"); bSTb = Buf("STb")
                        for h in range(4):
                            q_ld.dma(RQ[:, h, :], s_rq[h, :, tok0:tok0 + S_ * nseq], reads=[B["rq"]], writes=[bin_])
                            q_ld.dma(RK[:, h, :], s_rk[h, :, tok0:tok0 + S_ * nseq], reads=[B["rk"]], writes=[bin_])
                        for i in range(0, nall, 8):
                            m = min(8, nall - i)
                            q_ld.dma(RV[:, i:i + m, :],
                                     s_rv[tok0 + i * 128:tok0 + (i + m) * 128, :].rearrange("(n p) c -> p n c", p=128),
                                     reads=[B["rv"]], writes=[bin_])
                        psr = Ring([0, 1, 2, 3])
                        for i in range(nall):
                            pb = psr.next()
                            pst = PSB[pb][:].bitcast(BF16)
                            fns = [(lambda h=h, i=i, pst=pst: T.transpose(pst[:, h * 64:(h + 1) * 64],
                                                                          RK[:, h, i * 128:(i + 1) * 128],
                                                                          ident_b[0:64, 0:64])) for h in range(4)]
                            pe.op(fns, reads=[bin_, b_const], writes=[PSBUF[pb]])
                            for h in range(4):
                                for fb in range(2):
                                    dve.op(lambda h=h, fb=fb, i=i, pst=pst: V.tensor_scalar(
                                        out=KZ[:, i, h, fb * 64:(fb + 1) * 64], in0=pst[:, h * 64:(h + 1) * 64],
                                        scalar1=ZF[:, h, fb:fb + 1], scalar2=None, op0=ALU.mult),
                                           reads=[PSBUF[pb], bc], writes=[bKZ])

                        def u_chunk(i):
                            pb = psr.next()
                            fns = [(lambda h=h, i=i, pb=pb: T.matmul(PSB[pb][:, h * 128:(h + 1) * 128], lhsT=KZ[:, i, h, :],
                                                                     rhs=RV[:, i, h * 128:(h + 1) * 128], start=True,
                                                                     stop=True)) for h in range(4)]
                            pe.op(fns, reads=[bKZ, bin_], writes=[PSBUF[pb]])
                            return pb

                        def scan_step(i, lo, hi, bST, bSALLx):
                            pb = u_chunk(i)
                            act.op(lambda: S.copy(out=SALL[lo:hi, i, :, :], in_=ST[lo:hi, :, :]), reads=[bST], writes=[bSALLx])
                            for h in range(4):
                                dve.op(lambda h=h: V.scalar_tensor_tensor(
                                    out=ST[lo:hi, h, :], in0=ST[lo:hi, h, :], scalar=GC[lo:hi, h:h + 1],
                                    in1=PSB[pb][lo:hi, h * 128:(h + 1) * 128], op0=ALU.mult, op1=ALU.add),
                                       reads=[PSBUF[pb], bc, bST], writes=[bST])

                        for si in range(nseq):
                            co = si * n
                            if is_sample:
                                q_ld.dma(ST[0:64, :, :], st_f[l].rearrange("h d e -> d h e"), writes=[bSTf])
                                q_ld.dma(ST[64:128, :, :], st_b[l].rearrange("h d e -> d h e"), writes=[bSTb])
                            else:
                                dve.op(lambda: V.memset(ST[0:64, :, :], 0.0), writes=[bSTf])
                                dve.op(lambda: V.memset(ST[64:128, :, :], 0.0), writes=[bSTb])
                            for i in range(n):
                                scan_step(co + i, 0, 64, bSTf, bSALLf)
                                scan_step(co + n - 1 - i, 64, 128, bSTb, bSALLb)
                            if not is_sample:
                                p = (tok0 - NS) // PS_ + si
                                q_st.dma(o_rf[p, l].rearrange("h d e -> d h e"), ST[0:64, :, :], reads=[bSTf], writes=[B["out"]])
                                q_st.dma(o_rb[p, l].rearrange("h d e -> d h e"), ST[64:128, :, :], reads=[bSTb], writes=[B["out"]])
                        Mt = [sbt(es2, f"Mt{i}", [128, 4, 128], BF16) for i in range(2)]
                        bMt = [Buf("Mt0"), Buf("Mt1")]
                        QX = [sbt(es2, f"QX{i}", [128, 4, 128], BF16) for i in range(2)]
                        bQX = [Buf("QX0"), Buf("QX1")]
                        stt = [sbt(es2, f"stt{i}", [128, 4, 6], F32) for i in range(2)]
                        mv = [sbt(es2, f"mv{i}", [128, 4, 2], F32) for i in range(2)]
                        bstt = [Buf("stt0"), Buf("stt1")]
                        orn = [sbt(es2, f"orn{i}", [128, 512], BF16) for i in range(2)]
                        born = [Buf("orn0"), Buf("orn1")]
                        orT = [sbt(es2, f"orT{i}", [128, 512], BF16) for i in range(2)]
                        borT = [Buf("orT0"), Buf("orT1")]
                        ps2 = Ring([4, 5])
                        ps3 = Ring([6, 7])
                        pbo_of = {}

                        def stage_a(i):
                            k = i % 2
                            pbs = psr.next()
                            fns = [(lambda h=h: T.matmul(PSB[pbs][:, h * 128:(h + 1) * 128],
                                                         lhsT=RK[0:64, h, i * 128:(i + 1) * 128],
                                                         rhs=RQ[0:64, h, i * 128:(i + 1) * 128], start=True,
                                                         stop=True)) for h in range(4)]
                            pe.op(fns, reads=[bin_], writes=[PSBUF[pbs]])
                            dve.op(lambda: V.tensor_tensor(out=Mt[k][:].rearrange("p h f -> p (h f)"), in0=PSB[pbs][:],
                                                           in1=Dm[:].rearrange("p h f -> p (h f)"), op=ALU.mult),
                                   reads=[PSBUF[pbs], bc], writes=[bMt[k]])
                            pool.op(lambda: G_.tensor_tensor(out=QX[k][:], in0=RQ[:, :, i * 128:(i + 1) * 128],
                                                             in1=XI[:], op=ALU.mult),
                                    reads=[bin_, bc], writes=[bQX[k]])

                        def stage_b(i):
                            k = i % 2
                            pbo = ps2.next()
                            pbo_of[i] = pbo
                            fns = []
                            for h in range(4):
                                fns.append(lambda h=h: T.matmul(PSB[pbo][:, h * 128:(h + 1) * 128], lhsT=Mt[k][:, h, :],
                                                                rhs=RV[:, i, h * 128:(h + 1) * 128], start=True, stop=False))
                                fns.append(lambda h=h: T.matmul(PSB[pbo][:, h * 128:(h + 1) * 128], lhsT=QX[k][:, h, :],
                                                                rhs=SALL[:, i, h, :], start=False, stop=True))
                            pe.op(fns, reads=[bMt[k], bQX[k], bin_, bSALLf, bSALLb], writes=[PSBUF[pbo]])
                            for h in range(4):
                                dve.op(lambda h=h: V.bn_stats(out=stt[k][:, h, :], in_=PSB[pbo][:, h * 128:(h + 1) * 128]),
                                       reads=[PSBUF[pbo]], writes=[bstt[k]])
                            for h in range(4):
                                dve.op(lambda h=h: V.bn_aggr(out=mv[k][:, h, :], in_=stt[k][:, h, :]),
                                       reads=[bstt[k]], writes=[bstt[k]])
                            act_pow(mv[k][:, :, 1], mv[k][:, :, 1], -0.5, 1.0, CB_EPS, [bstt[k]], [bstt[k]])
                            for h in range(4):
                                dve.op(lambda h=h: V.tensor_scalar(out=orn[k][:, h * 128:(h + 1) * 128],
                                                                   in0=PSB[pbo][:, h * 128:(h + 1) * 128],
                                                                   scalar1=mv[k][:, h, 0:1], scalar2=mv[k][:, h, 1:2],
                                                                   op0=ALU.subtract, op1=ALU.mult),
                                       reads=[PSBUF[pbo], bstt[k]], writes=[born[k]])

                        def stage_c(i):
                            k = i % 2
                            pbt = ps3.next()
                            pst = PSB[pbt][:].bitcast(BF16)
                            fns = [(lambda h=h: T.transpose(pst[:, h * 128:(h + 1) * 128], orn[k][:, h * 128:(h + 1) * 128],
                                                            ident_b[:])) for h in range(4)]
                            pe.op(fns, reads=[born[k], b_const], writes=[PSBUF[pbt]])
                            act.op(lambda: S.copy(out=orT[k][:], in_=pst[:, 0:512]), reads=[PSBUF[pbt]], writes=[borT[k]])
                            for h in range(4):
                                q_st.dma(s_oret[h, :, tok0 + i * 128:tok0 + (i + 1) * 128],
                                         orT[k][:, h * 128:(h + 1) * 128], reads=[borT[k]], writes=[B["oret"]])

                        stage_a(0)
                        for i in range(nall):
                            if i + 1 < nall:
                                stage_a(i + 1)
                            stage_b(i)
                            if i >= 1:
                                stage_c(i - 1)
                        stage_c(nall - 1)
                        fw.barrier()

        def phase_d(l):
            lam_init = 0.8 - 0.6 * math.exp(-0.3 * l)
            with contextlib.ExitStack() as es:
                dlt = sbt(es, "dlt", [128, 256], F32)
                lam = sbt(es, "lam", [128, 8], F32)
                bl = Buf("lam")
                q_ld.dma(dlt[:], dlam[l].partition_broadcast(128), writes=[bl])
                dve.op(lambda: V.tensor_tensor(out=dlt[:, 0:64], in0=dlt[:, 0:64], in1=dlt[:, 64:128], op=ALU.mult),
                       reads=[bl], writes=[bl])
                dve.op(lambda: V.tensor_tensor(out=dlt[:, 128:192], in0=dlt[:, 128:192], in1=dlt[:, 192:256], op=ALU.mult),
                       reads=[bl], writes=[bl])
                dve.op(lambda: V.tensor_reduce(out=lam[:, 0:1], in_=dlt[:, 0:64], axis=mybir.AxisListType.X, op=ALU.add), reads=[bl],
                       writes=[bl])
                dve.op(lambda: V.tensor_reduce(out=lam[:, 1:2], in_=dlt[:, 128:192], axis=mybir.AxisListType.X, op=ALU.add), reads=[bl],
                       writes=[bl])
                act.op(lambda: S.activation(out=lam[:, 2:4], in_=lam[:, 0:2], func=AF.Exp), reads=[bl], writes=[bl])
                dve.op(lambda: V.tensor_tensor(out=lam[:, 4:5], in0=lam[:, 3:4], in1=lam[:, 2:3], op=ALU.subtract),
                       reads=[bl], writes=[bl])
                dve.op(lambda: V.tensor_scalar(out=lam[:, 5:6], in0=lam[:, 4:5], scalar1=-lam_init, scalar2=None,
                                               op0=ALU.add), reads=[bl], writes=[bl])
                for (tok0, S_, key0, NK, nseq) in BLOCKS:
                    G = min(S_, 512)
                    nkt = NK // 128
                    with contextlib.ExitStack() as es2:
                        KD = sbt(es2, "KD", [128, 4, NK * nseq], BF16)
                        VD = sbt(es2, "VD", [128, nkt * nseq, 512], BF16)
                        bkv = Buf("kv")
                        for h in range(4):
                            q_ld.dma(KD[:, h, :], s_kd[h, :, key0:key0 + NK * nseq], reads=[B["kd"]], writes=[bkv])
                        for i in range(0, nkt * nseq, 6):
                            m = min(6, nkt * nseq - i)
                            q_ld.dma(VD[:, i:i + m, :],
                                     s_vd[key0 + i * 128:key0 + (i + m) * 128, :].rearrange("(n p) c -> p n c", p=128),
                                     reads=[B["vd"]], writes=[bkv])
                        QD = [sbt(es2, f"QD{i}", [128, 4, 2, G], BF16) for i in range(2)]
                        bQD = [Buf("QD0"), Buf("QD1")]
                        for i_ in range(2):
                            pool.op(lambda i_=i_: G_.memset(QD[i_][:], 0.0), writes=[bQD[i_]])
                        PT = [sbt(es2, f"PT{i}", [128, G], BF16) for i in range(6)]
                        bPT = [Buf(f"PT{i}") for i in range(6)]
                        ptr = Ring([0, 1, 2, 3, 4, 5])
                        rr = [sbt(es2, f"rr{i}", [128, G], F32) for i in range(2)]
                        za = [sbt(es2, f"za{i}", [128, G], F32) for i in range(2)]
                        bza = [Buf(f"za{i}") for i in range(2)]
                        oc = [[sbt(es2, f"oc{p_}{c_}", [128, G], F32) for c_ in range(2)] for p_ in range(2)]
                        zc = [sbt(es2, f"zc{p_}", [128, G], F32) for p_ in range(2)]
                        boc = [Buf("oc0"), Buf("oc1")]
                        tt = [sbt(es2, f"tt{i}", [128, G], F32) for i in range(2)]
                        osq = sbt(es2, "osq", [128, G], F32)
                        oo = sbt(es2, "oo", [128, G], F32)
                        yb = [sbt(es2, f"yb{i}", [128, G], BF16) for i in range(2)]
                        byb = [Buf("yb0"), Buf("yb1")]
                        bfin = Buf("fin")
                        psc = Ring([0, 1, 2, 3])
                        for qb_ in range(nseq * (S_ // G)):
                            si, qb = divmod(qb_, S_ // G)
                            q0 = tok0 + si * S_ + qb * G
                            kto = si * nkt
                            qi = qb_ % 2
                            for h in range(4):
                                q_ld.dma(QD[qi][0:64, h, 0, :], s_qd[h, 0:64, q0:q0 + G], reads=[B["qd"]], writes=[bQD[qi]])
                                q_ld.dma(QD[qi][64:128, h, 1, :], s_qd[h, 64:128, q0:q0 + G], reads=[B["qd"]], writes=[bQD[qi]])
                            tiles = [(h, kt, c) for h in range(4) for kt in range(nkt) for c in range(2)]
                            LA = 3
                            pend = {}
                            deferred = []

                            def issue_score(idx, qi=qi):
                                h, kt, c = tiles[idx]
                                pb = psc.next()
                                pe.op(lambda: T.matmul(PSB[pb][:, 0:G], lhsT=KD[:, h, (kto + kt) * 128:(kto + kt + 1) * 128],
                                                       rhs=QD[qi][:, h, c, :], start=True, stop=True),
                                      reads=[bkv, bQD[qi]], writes=[PSBUF[pb]])
                                pi = ptr.next()
                                act.op(lambda: S.activation(out=PT[pi][:], in_=PSB[pb][:, 0:G], func=AF.Exp, scale=0.125),
                                       reads=[PSBUF[pb]], writes=[bPT[pi]])
                                pend[idx] = pi

                            def issue_pv(idx):
                                h, kt, c = tiles[idx]
                                pi = pend.pop(idx)
                                if c == 0:
                                    pe.op([lambda: T.matmul(PSB[4][:, 0:G], lhsT=VD[:, kto + kt, h * 128:(h + 1) * 128], rhs=PT[pi][:],
                                                            start=(kt == 0), stop=(kt == nkt - 1)),
                                           lambda: T.matmul(PSB[6][:, 0:G], lhsT=ones_b[:], rhs=PT[pi][:], start=(kt == 0),
                                                            stop=(kt == nkt - 1))],
                                          reads=[bkv, bPT[pi], b_const], writes=[PSBUF[4], PSBUF[6]])
                                else:
                                    pe.op(lambda: T.matmul(PSB[5][:, 0:G], lhsT=VD[:, kto + kt, h * 128:(h + 1) * 128], rhs=PT[pi][:],
                                                           start=(kt == 0), stop=(kt == nkt - 1)),
                                          reads=[bkv, bPT[pi]], writes=[PSBUF[5]])
                                    zi = h % 2
                                    if kt == 0:
                                        dve.op(lambda: V.tensor_copy(out=za[zi][:], in_=PT[pi][:]), reads=[bPT[pi]], writes=[bza[zi]])
                                    else:
                                        dve.op(lambda: V.tensor_tensor(out=za[zi][:], in0=za[zi][:], in1=PT[pi][:], op=ALU.add),
                                               reads=[bPT[pi], bza[zi]], writes=[bza[zi]])

                            def fin_stage0(h):
                                p_ = h % 2
                                dve.op(lambda: V.tensor_copy(out=oc[p_][0][:], in_=PSB[4][:, 0:G]), reads=[PSBUF[4]], writes=[boc[p_]])
                                act.op(lambda: S.copy(out=oc[p_][1][:], in_=PSB[5][:, 0:G]), reads=[PSBUF[5]], writes=[boc[p_]])
                                dve.op(lambda: V.tensor_copy(out=zc[p_][:], in_=PSB[6][:, 0:G]), reads=[PSBUF[6]], writes=[boc[p_]])

                            def fin_stage1(h, q0=q0):
                                p_ = h % 2
                                pe.op(lambda: T.matmul(PSB[7][:, 0:G], lhsT=ones_f[:], rhs=za[p_][:], start=True, stop=True),
                                      reads=[bza[p_], b_const], writes=[PSBUF[7]])
                                act_pow(rr[1][:], PSB[7][:, 0:G], -1.0, 1.0, CB_ZERO, [PSBUF[7]], [bfin])
                                act_pow(rr[0][:], zc[p_][:], -1.0, 1.0, CB_ZERO, [boc[p_]], [bfin])
                                for c in range(2):
                                    dve.op(lambda c=c: V.tensor_tensor(out=tt[c][:], in0=oc[p_][c][:], in1=rr[c][:], op=ALU.mult),
                                           reads=[boc[p_], bfin], writes=[bfin])
                                dve.op(lambda: V.scalar_tensor_tensor(out=oo[:], in0=tt[1][:], scalar=lam[:, 5:6],
                                                                      in1=tt[0][:], op0=ALU.mult, op1=ALU.add),
                                       reads=[bfin, bl], writes=[bfin])
                                act.op(lambda: S.activation(out=osq[:], in_=oo[:], func=AF.Square), reads=[bfin],
                                       writes=[bfin])

                            def fin_stage2(h, q0=q0):
                                pe.op(lambda: T.matmul(PSB[7][:, 0:G], lhsT=ones_f[:], rhs=osq[:], start=True, stop=True),
                                      reads=[bfin, b_const], writes=[PSBUF[7]])
                                act_pow(rr[0][:], PSB[7][:, 0:G], -0.5, 1.0, CB_128EPS, [PSBUF[7]], [bfin])
                                yi = h % 2
                                dve.op(lambda yi=yi: V.scalar_tensor_tensor(out=yb[yi][:], in0=oo[:],
                                                                            scalar=math.sqrt(128.0) * (1.0 - lam_init),
                                                                            in1=rr[0][:], op0=ALU.mult, op1=ALU.mult),
                                       reads=[bfin], writes=[byb[yi]])
                                q_st.dma(s_yd[h, :, q0:q0 + G], yb[yi][:], reads=[byb[yi]], writes=[B["yd"]])

                            nt_ = len(tiles)
                            for idx in range(nt_ + LA):
                                if idx < nt_:
                                    issue_score(idx)
                                if idx >= LA:
                                    j = idx - LA
                                    issue_pv(j)
                                    h, kt, c = tiles[j]
                                    if kt == nkt - 1 and c == 1:
                                        fin_stage0(h)
                                        tph = 2 * nkt
                                        d1 = min(6, tph)
                                        d2 = min(20, tph + d1 - 1)
                                        deferred.append((idx + d1, fin_stage1, h))
                                        deferred.append((idx + d2, fin_stage2, h))
                                while deferred and deferred[0][0] <= idx:
                                    _, fn_, h_ = deferred.pop(0)
                                    fn_(h_)
                                deferred.sort(key=lambda x: x[0])
                            for (_, fn_, h_) in sorted(deferred, key=lambda x: x[0]):
                                fn_(h_)
                        fw.barrier()

        def phase_m(l):
            with contextlib.ExitStack() as es:
                sel = sbt(es, "sel", [65, 64], F32)
                bsel = Buf("sel")
                q_ld.dma(sel[:], sel65, writes=[bsel])
                for (tok0, S_, key0, NK1, nseq) in BLOCKS:
                    G = min(S_, 512)
                    nkt1 = NK1 // 128
                    NK = NK1 * nseq
                    nkt = nkt1 * nseq
                    with contextlib.ExitStack() as es2:
                        KM = sbt(es2, "KM", [128, 8, NK], BF16)
                        VM = sbt(es2, "VM", [128, nkt, 8, 128], BF16)
                        bkv = Buf("kv")
                        for h in range(8):
                            q_ld.dma(KM[0:64, h, :], s_km[h, 0:64, key0:key0 + NK], reads=[B["km"]], writes=[bkv])
                            q_ld.dma(KM[64:96, h, :], s_kpe[:, key0:key0 + NK], reads=[B["kpe"]], writes=[bkv])
                            q_ld.dma(KM[96:128, h, :], s_km[h, 96:128, key0:key0 + NK], reads=[B["km"]], writes=[bkv])
                        for i in range(0, nkt, 6):
                            m_ = min(6, nkt - i)
                            q_ld.dma(VM[:, i:i + m_, :, :].rearrange("p n h e -> p n (h e)"),
                                     s_vm[key0 + i * 128:key0 + (i + m_) * 128, :, :].rearrange("(n p) h e -> p n (h e)", p=128),
                                     reads=[B["vm"]], writes=[bkv])
                        QM = [sbt(es2, f"QM{i}", [128, 8, G], BF16) for i in range(2)]
                        bQM = [Buf("QM0"), Buf("QM1")]
                        PT = [sbt(es2, f"PT{i}", [128, G], BF16) for i in range(6)]
                        bPT = [Buf(f"PT{i}") for i in range(6)]
                        ptr = Ring([0, 1, 2, 3, 4, 5])
                        osb = [sbt(es2, f"osb{i}", [65, G], F32) for i in range(2)]
                        bosb = [Buf("osb0"), Buf("osb1")]
                        yb = [sbt(es2, f"yb{i}", [64, G], BF16) for i in range(2)]
                        byb = [Buf("yb0"), Buf("yb1")]
                        psc = Ring([0, 1, 2, 7])
                        pso = Ring([3, 4])
                        psb_ = Ring([5, 6])
                        nkt_all = nkt
                        nkt = nkt1
                        for qb_ in range(nseq * (S_ // G)):
                            si, qb = divmod(qb_, S_ // G)
                            q0 = tok0 + si * S_ + qb * G
                            kto = si * nkt
                            qi = qb_ % 2
                            for h in range(8):
                                q_ld.dma(QM[qi][:, h, :], s_qm[h, :, q0:q0 + G], reads=[B["qm"]], writes=[bQM[qi]])
                            tiles = [(h, kt) for h in range(8) for kt in range(nkt)]
                            LA = 3
                            pend = {}
                            pos = {}

                            def issue_score(idx, qi=qi):
                                h, kt = tiles[idx]
                                pb = psc.next()
                                pe.op(lambda: T.matmul(PSB[pb][:, 0:G], lhsT=KM[:, h, (kto + kt) * 128:(kto + kt + 1) * 128], rhs=QM[qi][:, h, :],
                                                       start=True, stop=True), reads=[bkv, bQM[qi]], writes=[PSBUF[pb]])
                                pi = ptr.next()
                                act.op(lambda: S.activation(out=PT[pi][:], in_=PSB[pb][:, 0:G], func=AF.Exp, scale=MLA_SCALE),
                                       reads=[PSBUF[pb]], writes=[bPT[pi]])
                                pend[idx] = pi

                            def issue_pv(idx):
                                h, kt = tiles[idx]
                                if kt == 0:
                                    pos[h] = pso.next()
                                po = pos[h]
                                pi = pend.pop(idx)
                                pe.op(lambda: T.matmul(PSB[po][:, 0:G], lhsT=VM[:, kto + kt, h, :], rhs=PT[pi][:], start=(kt == 0),
                                                       stop=(kt == nkt - 1)), reads=[bkv, bPT[pi]], writes=[PSBUF[po]])

                            deferred = []

                            def fin_stage1(h, q0=q0):
                                po = pos[h]
                                oi = h % 2
                                dve.op(lambda oi=oi, po=po: V.tensor_copy(out=osb[oi][0:64, :], in_=PSB[po][0:64, 0:G]),
                                       reads=[PSBUF[po]], writes=[bosb[oi]])
                                act_pow(osb[oi][64:65, :], PSB[po][64:65, 0:G], -1.0, 1.0, CB_ZERO, [PSBUF[po]], [bosb[oi]],
                                        p0=64, p1=65)

                            def fin_stage2(h, q0=q0):
                                oi = h % 2
                                pbb = psb_.next()
                                pe.op(lambda oi=oi, pbb=pbb: T.matmul(PSB[pbb][0:64, 0:G], lhsT=sel[:], rhs=osb[oi][:],
                                                                      start=True, stop=True),
                                      reads=[bosb[oi], bsel], writes=[PSBUF[pbb]])
                                dve.op(lambda oi=oi, pbb=pbb: V.tensor_tensor(out=yb[oi][:], in0=osb[oi][0:64, :],
                                                                              in1=PSB[pbb][0:64, 0:G], op=ALU.mult),
                                       reads=[bosb[oi], PSBUF[pbb]], writes=[byb[oi]])
                                q_st.dma(s_ym[h * 64:(h + 1) * 64, q0:q0 + G], yb[oi][:], reads=[byb[oi]],
                                         writes=[B["ym"]])

                            nt_ = len(tiles)
                            for idx in range(nt_ + LA):
                                if idx < nt_:
                                    issue_score(idx)
                                if idx >= LA:
                                    j = idx - LA
                                    issue_pv(j)
                                    h, kt = tiles[j]
                                    if kt == nkt - 1:
                                        fin_stage1(h)
                                        deferred.append((idx + min(6, 2 * nkt - 1), fin_stage2, h))
                                while deferred and deferred[0][0] <= idx:
                                    _, fn_, h_ = deferred.pop(0)
                                    fn_(h_)
                            for (_, fn_, h_) in deferred:
                                fn_(h_)
                        fw.barrier()

        def phase_c1(l, xsrc, bsrc):
            with contextlib.ExitStack() as es:
                WC, bWC = load_w(es, "WC", w_c1[l], 8, NC1, "wc1", l)
                WBR, bWBR = load_w(es, "WBR", w_br[l], 12, D, "wbr", l)
                WO, bWO = load_w(es, "WO", w_o[l], 8, D, "wo", l)
                ctx = make_norm_ctx(es, nx=8, nh=2)
                hT, bhT = ctx["hT"], ctx["bhT"]
                xt, bx = ctx["xt"], ctx["bx"]
                g1b = sbt(es, "g1b", [128, 2, D], F32)
                bg1b = Buf("g1b")
                for c in range(2):
                    q_ld.dma(g1b[:, c, :], s_gb[0, c], reads=[B["gb"]], writes=[bg1b])
                YB = sbt(es, "YB", [128, 12, 512], BF16)
                bYB = Buf("YB")
                bYR = Buf("YR")
                srg = sbt(es, "srg", [128, 512], BF16)
                bsrg = Buf("srg")
                gate = [sbt(es, f"gate{i}", [128, 512], F32) for i in range(3)]
                bgate = [Buf(f"gate{i}") for i in range(3)]
                tm = [sbt(es, f"tm{i}", [128, 512], F32) for i in range(3)]
                btm = [Buf(f"tm{i}") for i in range(3)]
                mrg = sbt(es, "mrg", [128, 8, 512], BF16)
                bmrg = Buf("mrg")
                psr = Ring([0, 1, 2, 3, 4, 5])
                pstr = Ring([6, 7])

                def proj_c(pb, col0):
                    fns = [(lambda kc=kc: T.matmul(PSB[pb][:], lhsT=WC[:, kc, col0:col0 + 128], rhs=hT[:, kc, :],
                                                   start=(kc == 0), stop=(kc == 7))) for kc in range(8)]
                    pe.op(fns, reads=[bWC, bhT], writes=[PSBUF[pb]])

                for g in range(NT // 512):
                    g0 = g * 512
                    cidx = 0 if g0 < NS else 1
                    slot = g % 2
                    xo = 4 * slot
                    if g == 0:
                        norm_to_hT(ctx, xsrc, bsrc, g0, cidx, 0, pstr, slot=slot, xo=xo)
                    hT, bhT = ctx["hTs"][slot], ctx["bhTs"][slot]
                    for h in range(4):
                        q_ld.dma(YB[:, h, :], s_oret[h, :, g0:g0 + 512], reads=[B["oret"]], writes=[bYR])
                        q_ld.dma(YB[:, 4 + h, :], s_yd[h, :, g0:g0 + 512], reads=[B["yd"]], writes=[bYB])
                        q_ld.dma(YB[:, 8 + h, :], s_ym[h * 128:(h + 1) * 128, g0:g0 + 512], reads=[B["ym"]], writes=[bYB])
                    for j in range(4):
                        pb = psr.next()
                        proj_c(pb, j * 128)
                        act.op(lambda pb=pb: S.activation(out=srg[:], in_=PSB[pb][:], func=AF.Silu), reads=[PSBUF[pb]],
                               writes=[bsrg])
                        pool.op(lambda j=j: G_.tensor_tensor(out=YB[:, j, :], in0=YB[:, j, :], in1=srg[:], op=ALU.mult),
                                reads=[bsrg, bYR], writes=[bYR])
                    if g0 + 512 < NT:
                        norm_to_hT(ctx, xsrc, bsrc, g0 + 512, 0 if g0 + 512 < NS else 1, 0, pstr, slot=1 - slot, xo=4 - xo)
                    for oc in range(8):
                        for gi in range(3):
                            pb = psr.next()
                            proj_c(pb, 512 + gi * D + oc * 128)
                            act.op(lambda pb=pb, gi=gi: S.activation(out=gate[gi][:], in_=PSB[pb][:], func=AF.Sigmoid),
                                   reads=[PSBUF[pb]], writes=[bgate[gi]])
                            pb2 = psr.next()
                            fns = [(lambda kc=kc, gi=gi, oc=oc, pb2=pb2: T.matmul(
                                PSB[pb2][:], lhsT=WBR[:, gi * 4 + kc, oc * 128:(oc + 1) * 128], rhs=YB[:, gi * 4 + kc, :],
                                start=(kc == 0), stop=(kc == 3))) for kc in range(4)]
                            pe.op(fns, reads=[bWBR, bYB, bYR], writes=[PSBUF[pb2]])
                            dve.op(lambda gi=gi, pb2=pb2: V.tensor_tensor(out=tm[gi][:], in0=PSB[pb2][:], in1=gate[gi][:],
                                                                          op=ALU.mult),
                                   reads=[PSBUF[pb2], bgate[gi]], writes=[btm[gi]])
                        pool.op(lambda: G_.tensor_tensor(out=tm[0][:], in0=tm[0][:], in1=tm[1][:], op=ALU.add),
                                reads=[btm[0], btm[1]], writes=[btm[0]])
                        pool.op(lambda oc=oc: G_.tensor_tensor(out=mrg[:, oc, :], in0=tm[0][:], in1=tm[2][:], op=ALU.add),
                                reads=[btm[0], btm[2]], writes=[bmrg])
                    for t in range(4):
                        for half in range(2):
                            pb = psr.next()
                            fns = [(lambda kc=kc, t=t, half=half, pb=pb: T.matmul(
                                PSB[pb][:], lhsT=mrg[:, kc, t * 128:(t + 1) * 128], rhs=WO[:, kc, half * 512:(half + 1) * 512],
                                start=(kc == 0), stop=(kc == 7))) for kc in range(8)]
                            pe.op(fns, reads=[bWO, bmrg], writes=[PSBUF[pb]])
                            ti = half
                            dve.op(lambda ti=ti, pb=pb, half=half: V.tensor_tensor(
                                out=tm[ti][:], in0=PSB[pb][:], in1=g1b[:, cidx, half * 512:(half + 1) * 512], op=ALU.mult),
                                   reads=[PSBUF[pb], bg1b], writes=[btm[ti]])
                            (dve if half == 0 else pool).op(lambda ti=ti, t=t, half=half, xo=xo: (V if half == 0 else G_).tensor_tensor(
                                out=xt[xo + t][:, half * 512:(half + 1) * 512], in0=xt[xo + t][:, half * 512:(half + 1) * 512],
                                in1=tm[ti][:], op=ALU.add), reads=[btm[ti], bx[xo + t]], writes=[bx[xo + t]])
                        q_st.dma(s_xmid[g0 + t * 128:g0 + (t + 1) * 128, :], xt[xo + t][:], reads=[bx[xo + t]], writes=[B["xmid"]])
                fw.barrier()

        def phase_c2(l, last):
            with contextlib.ExitStack() as es:
                W1, bW1 = load_w(es, "W1", w_f1[l], 8, 2 * DFF, "wf1", l)
                W2, bW2 = load_w(es, "W2", w_f2[l], 22, D, "wf2", l)
                ctx = make_norm_ctx(es)
                hT, bhT = ctx["hT"], ctx["bhT"]
                xt, bx = ctx["xt"], ctx["bx"]
                g2b = sbt(es, "g2b", [128, D], F32)
                bg2b = Buf("g2b")
                cur_c = [-1]
                bfg = Buf("fg")
                if last:
                    fg = sbt(es, "fg", [128, D], F32)
                    q_ld.dma(fg[:], fin_g.partition_broadcast(128), writes=[bfg])
                uT = sbt(es, "uT", [128, 22, 512], BF16)
                buT = Buf("uT")
                sa = [sbt(es, f"sa{i}", [128, 512], BF16) for i in range(2)]
                bsa = [Buf("sa0"), Buf("sa1")]
                tm = [sbt(es, f"tm{i}", [128, 512], F32) for i in range(2)]
                btm = [Buf("tm0"), Buf("tm1")]
                fs = sbt(es, "fs", [128, 8], F32)
                bfs = Buf("fs")
                psr = Ring([0, 1, 2, 3, 4, 5])
                pstr = Ring([6, 7])
                for g in range(NT // 512):
                    g0 = g * 512
                    cidx = 0 if g0 < NS else 1
                    if cur_c[0] != cidx:
                        cur_c[0] = cidx
                        q_ld.dma(g2b[:], s_gb[1, cidx], reads=[B["gb"]], writes=[bg2b])
                    norm_to_hT(ctx, s_xmid, B["xmid"], g0, cidx, 1, pstr)
                    for j in range(22):
                        pa, pb = psr.next(), psr.next()
                        for (pp, col0) in [(pa, j * 128), (pb, DFF + j * 128)]:
                            fns = [(lambda kc=kc, pp=pp, col0=col0: T.matmul(PSB[pp][:], lhsT=W1[:, kc, col0:col0 + 128],
                                                                            rhs=hT[:, kc, :], start=(kc == 0),
                                                                            stop=(kc == 7))) for kc in range(8)]
                            pe.op(fns, reads=[bW1, bhT], writes=[PSBUF[pp]])
                        si = j % 2
                        act.op(lambda si=si, pa=pa: S.activation(out=sa[si][:], in_=PSB[pa][:], func=AF.Silu),
                               reads=[PSBUF[pa]], writes=[bsa[si]])
                        dve.op(lambda si=si, pb=pb, j=j: V.tensor_tensor(out=uT[:, j, :], in0=PSB[pb][:], in1=sa[si][:],
                                                                         op=ALU.mult),
                               reads=[PSBUF[pb], bsa[si]], writes=[buT])
                    for t in range(4):
                        for half in range(2):
                            pb = psr.next()
                            fns = [(lambda j=j, t=t, half=half, pb=pb: T.matmul(
                                PSB[pb][:], lhsT=uT[:, j, t * 128:(t + 1) * 128], rhs=W2[:, j, half * 512:(half + 1) * 512],
                                start=(j == 0), stop=(j == 21))) for j in range(22)]
                            pe.op(fns, reads=[bW2, buT], writes=[PSBUF[pb]])
                            ti = half
                            dve.op(lambda ti=ti, pb=pb, half=half: V.tensor_tensor(
                                out=tm[ti][:], in0=PSB[pb][:], in1=g2b[:, half * 512:(half + 1) * 512], op=ALU.mult),
                                   reads=[PSBUF[pb], bg2b], writes=[btm[ti]])
                            (dve if half == 0 else pool).op(lambda ti=ti, t=t, half=half: (V if half == 0 else G_).tensor_tensor(
                                out=xt[t][:, half * 512:(half + 1) * 512], in0=xt[t][:, half * 512:(half + 1) * 512],
                                in1=tm[ti][:], op=ALU.add), reads=[btm[ti], bx[t]], writes=[bx[t]])
                        if not last:
                            q_st.dma(s_x1[g0 + t * 128:g0 + (t + 1) * 128, :], xt[t][:], reads=[bx[t]], writes=[B["x1"]])
                        else:
                            junk, bjunk = ctx["junk"], ctx["bjunk"]
                            act.op(lambda t=t: S.activation(out=junk[:], in_=xt[t][:], func=AF.Square,
                                                            accum_out=fs[:, t:t + 1]), reads=[bx[t]],
                                   writes=[bjunk, bfs])
                            act_pow(fs[:, 4 + t:5 + t], fs[:, t:t + 1], -0.5, 1.0 / D, CB_EPS, [bfs], [bfs])
                            dve.op(lambda t=t: V.scalar_tensor_tensor(out=xt[t][:], in0=xt[t][:], scalar=fs[:, 4 + t:5 + t],
                                                                      in1=fg[:], op0=ALU.mult, op1=ALU.mult),
                                   reads=[bx[t], bfs, bfg], writes=[bx[t]])
                            q_st.dma(y_out[g0 + t * 128:g0 + (t + 1) * 128, :], xt[t][:], reads=[bx[t]], writes=[B["out"]])
                fw.barrier()

        plist = []
        for l in range(DEPTH):
            plist += [(l, p) for p in ["ada", "A", "R", "D", "M", "C1", "C2"]]
        for (l, p) in plist:
            xsrc, bsrc = (x_in, Buf("x_in")) if l == 0 else (s_x1, B["x1"])
            if p == "ada":
                ada_phase(l)
            elif p == "A":
                phase_a(l, xsrc, bsrc)
            elif p == "R":
                phase_r(l)
            elif p == "D":
                phase_d(l)
            elif p == "M":
                phase_m(l)
            elif p == "C1":
                phase_c1(l, xsrc, bsrc)
            elif p == "C2":
                phase_c2(l, last=(l == DEPTH - 1))
            if stop_after is not None and (l, p) == stop_after:
                break
        fw.barrier()
        nc._fw_stats = (fw.ninst, fw.ndma)
    return nc


def _swap_idx(n):
    h = n // 2
    return list(range(h, n)) + list(range(0, h))


def _perm_a():
    idx = []
    sw64 = _swap_idx(64)
    def sec(off, n):
        return list(range(off, off + n))
    def sec_sw(off, n, blk):
        out = []
        sw = _swap_idx(blk)
        for b in range(n // blk):
            out += [off + b * blk + s for s in sw]
        return out
    idx += sec(O_DQ, 512)
    idx += sec_sw(O_DQ, 512, 64)
    idx += sec(O_DK, 512)
    idx += sec_sw(O_DK, 512, 64)
    for h in range(4):
        idx += sec(O_RQ + h * 64, 64) * 2
    for h in range(4):
        idx += sec_sw(O_RQ + h * 64, 64, 64) * 2
    idx += sec(O_RK, 256)
    idx += sec_sw(O_RK, 256, 64)
    idx += sec(O_CQ, 384)
    idx += sec(O_CKV, 256)
    idx += sec(O_KPE, 32)
    idx += sec_sw(O_KPE, 32, 32)
    idx += sec(O_DV, 512)
    idx += sec(O_RV, 512)
    assert len(idx) == NA
    return np.array(idx)


def _rope_tables():
    t = np.arange(NS)
    row = (t // GRID_W).astype(np.float32)
    col = (t % GRID_W).astype(np.float32)

    def ang_tab(rot_dim):
        nf = rot_dim // 4
        inv = (np.float32(10000.0) ** (-np.arange(nf, dtype=np.float32) / np.float32(nf))).astype(np.float32)
        ang = np.concatenate([row[:, None] * inv, col[:, None] * inv], axis=-1).astype(np.float32)
        return np.cos(ang).astype(np.float32), np.sin(ang).astype(np.float32)

    c64, s64 = ang_tab(64)
    c32, s32 = ang_tab(32)
    C64 = np.zeros((128, NS), np.float32)
    S64 = np.zeros((128, NS), np.float32)
    for p in range(128):
        d = p % 64
        i = d % 32
        C64[p] = c64[:, i]
        S64[p] = -s64[:, i] if d < 32 else s64[:, i]
    C96 = np.ones((128, NS), np.float32)
    S96 = np.zeros((128, NS), np.float32)
    for j in range(32):
        i = j % 16
        C96[64 + j] = c32[:, i]
        S96[64 + j] = -s32[:, i] if j < 16 else s32[:, i]
    return C64, S64, C96, S96


def _host_inputs(inp):
    f = np.float32
    pa = _perm_a()
    w_in = inp["w_in"]
    w_a = np.ascontiguousarray(w_in[:, :, pa])
    w_c1 = np.ascontiguousarray(np.concatenate([w_in[:, :, O_RG:O_RG + 512], w_in[:, :, O_GATE:O_GATE + 3072]], axis=2))
    uq = inp["w_uq"]
    idx = []
    for h in range(8):
        idx += list(range(h * 96, h * 96 + 64)) + [h * 96 + 64 + s for s in _swap_idx(32)]
    w_uq2 = np.ascontiguousarray(np.concatenate([uq, uq[:, :, np.array(idx)]], axis=2))
    ukv = inp["w_ukv"]
    idxk = []
    idxv = []
    for h in range(8):
        idxk += list(range(h * 128, h * 128 + 64))
        idxv += list(range(h * 128 + 64, h * 128 + 128))
    w_ukv2 = np.ascontiguousarray(ukv[:, :, np.array(idxk + idxv)])
    C64, S64, C96, S96 = _rope_tables()
    C32 = np.ones((128, NS), f); S32 = np.zeros((128, NS), f)
    C32[0:32] = C96[64:96]; S32[0:32] = S96[64:96]
    rope_t = np.stack([C64, S64, (C64 * 0.125).astype(f), (S64 * 0.125).astype(f), C96, S96, C32, S32]).astype(f)
    i = np.arange(128, dtype=f)
    jj = i[:, None]
    ii = i[None, :]
    DF = np.maximum(ii - jj, 0.0)
    DB = np.maximum(jj - ii, 0.0)
    MF = (ii >= jj).astype(f)
    MB = (jj > ii).astype(f)
    XIc = np.zeros((128, 128), f)
    XIc[0:64, :] = (i + 1.0)[None, :]
    XIc[64:128, :] = (128.0 - i)[None, :]
    ret_c = np.stack([DF, DB, MF, MB, XIc]).astype(f)
    ret_z = np.stack([127.0 - i, i], axis=1).astype(f)
    sel65 = np.zeros((65, 64), f)
    sel65[64, :] = 1.0
    common = {
        "n1g": np.ascontiguousarray(inp["norm1_g"].reshape(DEPTH, 8, 128).transpose(0, 2, 1)),
        "n2g": np.ascontiguousarray(inp["norm2_g"].reshape(DEPTH, 8, 128).transpose(0, 2, 1)),
        "w_ada": inp["w_ada"],
        "b_ada_fm": np.ascontiguousarray(inp["b_ada"].reshape(DEPTH, 48, 128).transpose(0, 2, 1)),
        "b_ada": inp["b_ada"],
        "w_a": w_a, "w_c1": w_c1,
        "dec_f": inp["ret_decay_fwd"], "dec_b": inp["ret_decay_bwd"],
        "dlam": np.ascontiguousarray(inp["diff_lambda"].reshape(DEPTH, 256)),
        "qng": np.ascontiguousarray(inp["mla_q_norm"].reshape(DEPTH, 3, 128).transpose(0, 2, 1)),
        "kvng": np.ascontiguousarray(inp["mla_kv_norm"].reshape(DEPTH, 2, 128).transpose(0, 2, 1)),
        "kvn_row": inp["mla_kv_norm"],
        "w_uq": w_uq2, "w_ukv": w_ukv2,
        "w_br": np.ascontiguousarray(inp["w_branch"].reshape(DEPTH, 1536, D)),
        "w_o": inp["w_out"], "w_f1": inp["w_ffn_in"], "w_f2": inp["w_ffn_out"],
        "fin_g": inp["final_g"],
        "ident_in": np.eye(128, dtype=f),
        "rope_t": rope_t, "ret_c": ret_c, "ret_z": ret_z, "sel65": sel65,
    }
    maps = []
    for c in range(8):
        m = dict(common)
        m["x_in"] = np.ascontiguousarray(np.concatenate(
            [inp["x_sample"][c], inp["x_prompt"][4 * c:4 * c + 4].reshape(NPSEQ * PS_, D)], axis=0))
        m["st_f"] = np.ascontiguousarray(inp["state_ret_fwd"][c])
        m["st_b"] = np.ascontiguousarray(inp["state_ret_bwd"][c])
        m["c_dk"] = np.ascontiguousarray(inp["cache_diff_k"][c].reshape(DEPTH, LCTX, 512))
        m["c_dv"] = np.ascontiguousarray(inp["cache_diff_v"][c].reshape(DEPTH, LCTX, 512))
        m["c_ckv"] = np.ascontiguousarray(inp["cache_mla_ckv"][c])
        m["c_kpe"] = np.ascontiguousarray(inp["cache_mla_kpe"][c])
        cond = np.stack([inp["c"][c], inp["c_ctx"]], axis=1)
        m["cond_fm"] = np.ascontiguousarray(cond.reshape(8, 128, 2).transpose(1, 0, 2))
        maps.append(m)
    return maps


_NC_CACHE = {}


def kernel(**inputs):
    inp = {k: np.asarray(v) for k, v in inputs.items()}
    if "nc" not in _NC_CACHE:
        _NC_CACHE["nc"] = build_program()
    nc = _NC_CACHE["nc"]
    maps = _host_inputs(inp)
    res = run_bass_kernel_spmd(nc, maps, core_ids=list(range(8)))
    R = res.results
    y_sample = np.stack([R[c]["y_out"][:NS] for c in range(8)]).astype(np.float32)
    y_prompt = np.concatenate([R[c]["y_out"][NS:].reshape(NPSEQ, PS_, D) for c in range(8)]).astype(np.float32)
    rf = np.concatenate([R[c]["o_rf"] for c in range(8)]).astype(np.float32)
    rb = np.concatenate([R[c]["o_rb"] for c in range(8)]).astype(np.float32)
    dk = np.concatenate([R[c]["o_dk"] for c in range(8)]).reshape(32, DEPTH, PS_, 4, 128).astype(np.float32)
    dv = np.concatenate([R[c]["o_dv"] for c in range(8)]).reshape(32, DEPTH, PS_, 4, 128).astype(np.float32)
    ckv = np.concatenate([R[c]["o_ckv"] for c in range(8)]).astype(np.float32)
    kpe = np.concatenate([R[c]["o_kpe"] for c in range(8)]).astype(np.float32)
    return (y_prompt, y_sample, rf, rb, dk, dv, ckv, kpe)
```
